# Optimizing a Trainium2 kernel written in Bass

```python
import math
import jax
import jax.numpy as jnp
from jax import lax
import numpy as np

D_MODEL = 1024
BATCH = 4
SEQ = 4096
DEPTH = 4

GRID_W = 64
CTX_LEN = 256
N_MOD = 9
D_FF = 256 * (-(-8 * D_MODEL // (3 * 256)))
CONV_K = 5
EPS = 1e-6
N_BRANCH = 4
BRANCH_W = D_MODEL // 2
SSM_P = 64
SSM_H = BRANCH_W // SSM_P
SSM_G = 2
SSM_N = 128
SSM_CHUNK = 128
SSM_W = SSM_H * SSM_P
SSM_CONV_CH = SSM_W + 2 * SSM_G * SSM_N
ML_DV = 128
ML_H = BRANCH_W // ML_DV
ML_DK = ML_DV // 2
ML_CHUNK = 64
S5_GC = 16
S5_GROUPS = BRANCH_W // S5_GC
S5_P = 64
GDN_DK = 128
GDN_DV = 128
GDN_H = BRANCH_W // GDN_DV
GDN_CHUNK = 64
IN_SPLITS = (SSM_W, SSM_CONV_CH, 2 * SSM_H,
             2 * ML_H * ML_DK, ML_H * ML_DV, ML_H * ML_DV, 4 * ML_H,
             S5_GROUPS * S5_GC,
             GDN_H * (2 * GDN_DK + GDN_DV), GDN_H * GDN_DV, 2 * GDN_H, 2 * GDN_H,
             N_BRANCH * D_MODEL)
D_IN = sum(IN_SPLITS)

kernel_name = 'hybrid_flow_backbone'


def rmsnorm(x, w):
    xf = x.astype(jnp.float32)
    y = xf * lax.rsqrt(jnp.mean(xf * xf, axis=-1, keepdims=True) + EPS)
    return (y * w.astype(jnp.float32)).astype(x.dtype)


def l2norm(a):
    return a * lax.rsqrt(jnp.sum(a * a, axis=-1, keepdims=True) + EPS)


def modulate(x, shift, scale):
    return x * (1 + scale) + shift


def swiglu(x, w_up, w_down):
    g, u = jnp.split(x @ w_up, 2, axis=-1)
    return (jax.nn.silu(g) * u) @ w_down


def split_cols(a, sizes):
    return jnp.split(a, np.cumsum(sizes)[:-1].tolist(), axis=-1)


def dwconv(x, w, b=None):
    y = lax.conv_general_dilated(x, w[:, None, :].astype(x.dtype), window_strides=(1,),
                                 padding=[(CONV_K // 2, CONV_K // 2)],
                                 dimension_numbers=('NWC', 'WIO', 'NWC'),
                                 feature_group_count=x.shape[-1])
    return y if b is None else y + b.astype(x.dtype)


def raster_to_col(a, rows):
    bsz, n, d = a.shape
    return a.reshape(bsz, rows, GRID_W, d).transpose(0, 2, 1, 3).reshape(bsz, n, d)


def col_to_raster(a, rows):
    bsz, n, d = a.shape
    return a.reshape(bsz, GRID_W, rows, d).transpose(0, 2, 1, 3).reshape(bsz, n, d)


def to_chunks(a, q):
    return a.reshape(a.shape[0], a.shape[1] // q, q, *a.shape[2:])


def from_chunks(a):
    return a.reshape(a.shape[0], a.shape[1] * a.shape[2], *a.shape[3:])


def masked_diff(gt, strict):
    q = gt.shape[-1]
    idx = jnp.arange(q)
    mask = (idx[:, None] > idx[None, :]) if strict else (idx[:, None] >= idx[None, :])
    return jnp.where(mask, gt[..., :, None] - gt[..., None, :], -jnp.inf)


def two_way(scan_fn, ctx_f, lat_f, par_f, ctx_b, lat_b, par_b, init):
    flip = lambda args: tuple(jnp.flip(a, axis=1) for a in args)
    yc_f, s_f = scan_fn(*ctx_f, *par_f, init)
    yl_f, _ = scan_fn(*lat_f, *par_f, s_f)
    yc_b, s_b = scan_fn(*flip(ctx_b), *par_b, init)
    yl_b, _ = scan_fn(*flip(lat_b), *par_b, s_b)
    return yc_f + jnp.flip(yc_b, axis=1), yl_f + jnp.flip(yl_b, axis=1)


def ssd_scan(xs, dt, bm, cm, a, h0):
    xq, dq, bq, cq = (to_chunks(t, SSM_CHUNK) for t in (xs, dt, bm, cm))
    g = jnp.cumsum(dq * a, axis=2)
    decay = jnp.exp(masked_diff(jnp.swapaxes(g, 2, 3), False))
    xdt = xq * dq[..., None]
    scores = jnp.einsum('bcihn,bcjhn->bchij', cq, bq) * decay
    y_intra = jnp.einsum('bchij,bcjhp->bcihp', scores, xdt)
    g_last = g[:, :, -1]
    states = jnp.einsum('bcjhn,bcjh,bcjhp->bchnp', bq, jnp.exp(g_last[:, :, None] - g), xdt)
    def step(h, inp):
        s, dec = inp
        return dec[:, :, None, None] * h + s, h
    h_fin, h_prev = lax.scan(step, h0, (jnp.moveaxis(states, 1, 0), jnp.moveaxis(jnp.exp(g_last), 1, 0)))
    h_prev = jnp.moveaxis(h_prev, 0, 1)
    y_inter = jnp.einsum('bcihn,bcih,bchnp->bcihp', cq, jnp.exp(g), h_prev)
    return from_chunks(y_intra + y_inter), h_fin


def mlstm_scan(q, k, v, ig, lf, state):
    qq, kq, vq = (to_chunks(t, ML_CHUNK) for t in (q, k, v))
    bt = jnp.swapaxes(jnp.cumsum(to_chunks(lf, ML_CHUNK), axis=2), 2, 3)
    it = jnp.swapaxes(to_chunks(ig, ML_CHUNK), 2, 3)
    dmat = masked_diff(bt, False) + it[..., None, :]
    m_intra = jnp.max(dmat, axis=-1)
    a_end = bt[..., -1:] - bt + it
    m_loc = jnp.max(a_end, axis=-1)
    w_end = jnp.exp(a_end - m_loc[..., None])
    c_loc = jnp.einsum('bchj,bcjhk,bcjhv->bchkv', w_end, kq, vq)
    n_loc = jnp.einsum('bchj,bcjhk->bchk', w_end, kq)
    b_last = bt[..., -1]
    def step(carry, inp):
        c_st, n_st, m_st = carry
        cl, nl, ml, bl = inp
        m_new = jnp.maximum(bl + m_st, ml)
        s_old = jnp.exp(bl + m_st - m_new)
        s_loc = jnp.exp(ml - m_new)
        c_new = s_old[..., None, None] * c_st + s_loc[..., None, None] * cl
        n_new = s_old[..., None] * n_st + s_loc[..., None] * nl
        return (c_new, n_new, m_new), (c_st, n_st, m_st)
    final, prev = lax.scan(step, state, tuple(jnp.moveaxis(t, 1, 0) for t in (c_loc, n_loc, m_loc, b_last)))
    c_prev, n_prev, m_prev = (jnp.moveaxis(t, 0, 1) for t in prev)
    e = bt + m_prev[..., None]
    m_out = jnp.maximum(e, m_intra)
    w_intra = jnp.exp(dmat - m_out[..., None]) * jnp.einsum('bcihk,bcjhk->bchij', qq, kq)
    w_x = jnp.exp(e - m_out)
    num = (jnp.einsum('bchij,bcjhv->bcihv', w_intra, vq)
           + jnp.einsum('bchi,bcihk,bchkv->bcihv', w_x, qq, c_prev))
    den = jnp.sum(w_intra, axis=-1) + w_x * jnp.einsum('bcihk,bchk->bchi', qq, n_prev)
    den = jnp.maximum(jnp.abs(den), jnp.exp(-m_out))
    h = num / jnp.swapaxes(den, 2, 3)[..., None]
    return from_chunks(h), final


def complex_affine(e1, e2):
    a1r, a1i, b1r, b1i = e1
    a2r, a2i, b2r, b2i = e2
    return (a2r * a1r - a2i * a1i, a2r * a1i + a2i * a1r,
            a2r * b1r - a2i * b1i + b2r, a2r * b1i + a2i * b1r + b2i)


def s5_scan(u, lam_re, lam_im, log_dt, b_re, b_im, c_re, c_im, h0):
    f32 = jnp.float32
    lre = jnp.minimum(lam_re.astype(f32), -1e-4)
    lim = lam_im.astype(f32)
    dt = jnp.exp(log_dt.astype(f32))[:, None]
    mag = jnp.exp(lre * dt)
    ab_re, ab_im = mag * jnp.cos(lim * dt), mag * jnp.sin(lim * dt)
    den = lre * lre + lim * lim
    f_re = ((ab_re - 1.0) * lre + ab_im * lim) / den
    f_im = (ab_im * lre - (ab_re - 1.0) * lim) / den
    br, bi = b_re.astype(f32), b_im.astype(f32)
    bb_re = f_re[..., None] * br - f_im[..., None] * bi
    bb_im = f_re[..., None] * bi + f_im[..., None] * br
    bu_re = jnp.einsum('gpc,bsgc->bsgp', bb_re, u)
    bu_im = jnp.einsum('gpc,bsgc->bsgp', bb_im, u)
    h0_re, h0_im = h0
    bu_re = bu_re.at[:, 0].add(ab_re * h0_re - ab_im * h0_im)
    bu_im = bu_im.at[:, 0].add(ab_re * h0_im + ab_im * h0_re)
    n = u.shape[1]
    a_re = jnp.broadcast_to(ab_re, (1, n) + ab_re.shape)
    a_im = jnp.broadcast_to(ab_im, (1, n) + ab_im.shape)
    _, _, h_re, h_im = lax.associative_scan(complex_affine, (a_re, a_im, bu_re, bu_im), axis=1)
    y = (jnp.einsum('gcp,bsgp->bsgc', c_re.astype(f32), h_re)
         - jnp.einsum('gcp,bsgp->bsgc', c_im.astype(f32), h_im))
    return y, (h_re[:, -1], h_im[:, -1])


def gdn_scan(q, k, v, lg, beta, s0):
    qq, kq, vq, bq = (to_chunks(t, GDN_CHUNK) for t in (q, k, v, beta))
    g = jnp.cumsum(to_chunks(lg, GDN_CHUNK), axis=2)
    gt = jnp.swapaxes(g, 2, 3)
    kb = kq * bq[..., None]
    lmat = jnp.einsum('bcihk,bcjhk->bchij', kb, kq) * jnp.exp(masked_diff(gt, True))
    eye = jnp.eye(GDN_CHUNK, dtype=jnp.float32)
    tmat = lax.linalg.triangular_solve(eye + lmat, jnp.broadcast_to(eye, lmat.shape),
                                       left_side=True, lower=True, unit_diagonal=True)
    u_base = jnp.einsum('bchij,bcjhv->bcihv', tmat, vq * bq[..., None])
    w_k = jnp.einsum('bchij,bcjhk->bcihk', tmat, kb * jnp.exp(g)[..., None])
    a_qk = jnp.einsum('bcihk,bcjhk->bchij', qq, kq) * jnp.exp(masked_diff(gt, False))
    q_dec = qq * jnp.exp(g)[..., None]
    g_last = g[:, :, -1]
    k_dec = kq * jnp.exp(g_last[:, :, None] - g)[..., None]
    def step(s, inp):
        ub, wc, qd, aqk, kd, gl = inp
        u = ub - jnp.einsum('bihk,bhkv->bihv', wc, s)
        o = jnp.einsum('bihk,bhkv->bihv', qd, s) + jnp.einsum('bhij,bjhv->bihv', aqk, u)
        s_new = jnp.exp(gl)[..., None, None] * s + jnp.einsum('bjhk,bjhv->bhkv', kd, u)
        return s_new, o
    s_fin, o = lax.scan(step, s0, tuple(jnp.moveaxis(t, 1, 0) for t in (u_base, w_k, q_dec, a_qk, k_dec, g_last)))
    return from_chunks(jnp.moveaxis(o, 0, 1)), s_fin


def mamba_branch(p_ctx, p_lat, need_ctx, conv_w, conv_b, dt_bias, a_log, d_skip, norm_w):
    f32 = jnp.float32
    def prep(z, xbc, dt):
        bsz, n = xbc.shape[:2]
        xbc = jax.nn.silu(dwconv(xbc, conv_w, conv_b)).astype(f32)
        xs, bm, cm = split_cols(xbc, (SSM_W, SSM_G * SSM_N, SSM_G * SSM_N))
        xs = xs.reshape(bsz, n, SSM_H, SSM_P)
        bm = jnp.repeat(bm.reshape(bsz, n, SSM_G, SSM_N), SSM_H // SSM_G, axis=2)
        cm = jnp.repeat(cm.reshape(bsz, n, SSM_G, SSM_N), SSM_H // SSM_G, axis=2)
        dt = jax.nn.softplus(dt.astype(f32).reshape(bsz, n, 2, SSM_H) + dt_bias.astype(f32))
        return z, xs, bm, cm, dt
    zc, xc, bc, cc, dtc = prep(*p_ctx)
    zl, xl, bl, cl, dtl = prep(*p_lat)
    a = -jnp.exp(a_log.astype(f32))
    h0 = jnp.zeros((xl.shape[0], SSM_H, SSM_N, SSM_P), f32)
    yc, yl = two_way(ssd_scan, (xc, dtc[:, :, 0], bc, cc), (xl, dtl[:, :, 0], bl, cl), (a[0],),
                     (xc, dtc[:, :, 1], bc, cc), (xl, dtl[:, :, 1], bl, cl), (a[1],), h0)
    def post(y, xs, z):
        bsz, n = y.shape[:2]
        y = (y + d_skip.astype(f32)[:, None] * xs).reshape(bsz, n, SSM_W)
        return rmsnorm(y * jax.nn.silu(z.astype(f32)), norm_w)
    return (post(yc, xc, zc) if need_ctx else None), post(yl, xl, zl)


def mlstm_branch(p_ctx, p_lat, need_ctx, conv_w, conv_b, gate_b, norm_w):
    f32 = jnp.float32
    def prep(qk, v, o, gates):
        bsz, n = v.shape[:2]
        qk = jax.nn.silu(dwconv(qk, conv_w, conv_b)).astype(f32)
        q, k = jnp.split(qk, 2, axis=-1)
        q = q.reshape(bsz, n, ML_H, ML_DK) * ML_DK ** -0.5
        k = k.reshape(bsz, n, ML_H, ML_DK)
        v = v.astype(f32).reshape(bsz, n, ML_H, ML_DV)
        gg = gates.astype(f32).reshape(bsz, n, 2, 2, ML_H) + gate_b.astype(f32)
        return o, q, k, v, gg[:, :, :, 0], jax.nn.log_sigmoid(gg[:, :, :, 1])
    oc, qc, kc, vc, igc, lfc = prep(*p_ctx)
    ol, ql, kl, vl, igl, lfl = prep(*p_lat)
    bsz = ql.shape[0]
    state0 = (jnp.zeros((bsz, ML_H, ML_DK, ML_DV), f32), jnp.zeros((bsz, ML_H, ML_DK), f32),
              jnp.zeros((bsz, ML_H), f32))
    yc, yl = two_way(mlstm_scan, (qc, kc, vc, igc[:, :, 0], lfc[:, :, 0]), (ql, kl, vl, igl[:, :, 0], lfl[:, :, 0]), (),
                     (qc, kc, vc, igc[:, :, 1], lfc[:, :, 1]), (ql, kl, vl, igl[:, :, 1], lfl[:, :, 1]), (), state0)
    def post(h, o):
        bsz_, n = h.shape[:2]
        h = rmsnorm(h, norm_w.reshape(ML_H, ML_DV)).reshape(bsz_, n, ML_H * ML_DV)
        return jax.nn.sigmoid(o.astype(f32)) * h
    return (post(yc, oc) if need_ctx else None), post(yl, ol)


def s5_branch(u_ctx, u_lat, need_ctx, a_re, a_im, log_dt, b_re, b_im, c_re, c_im, d_skip, glu_w, glu_b):
    f32 = jnp.float32
    def prep(u):
        return u.astype(f32).reshape(u.shape[0], u.shape[1], S5_GROUPS, S5_GC)
    uc, ul = prep(u_ctx), prep(u_lat)
    zero = jnp.zeros((ul.shape[0], S5_GROUPS, S5_P), f32)
    par_f = (a_re[0], a_im[0], log_dt[0], b_re, b_im, c_re, c_im)
    par_b = (a_re[1], a_im[1], log_dt[1], b_re, b_im, c_re, c_im)
    yc, yl = two_way(s5_scan, (uc,), (ul,), par_f, (uc,), (ul,), par_b, (zero, zero))
    def post(y, u):
        bsz, n = y.shape[:2]
        y = (y + d_skip.astype(f32).reshape(S5_GROUPS, S5_GC) * u).reshape(bsz, n, BRANCH_W)
        g = jax.nn.gelu(y)
        return g * jax.nn.sigmoid(g @ glu_w.astype(f32) + glu_b.astype(f32))
    return (post(yc, uc) if need_ctx else None), post(yl, ul)


def gdn_branch(p_ctx, p_lat, need_ctx, conv_w, dt_bias, a_log, norm_w):
    f32 = jnp.float32
    def prep(qkv, z, a, beta):
        bsz, n = qkv.shape[:2]
        qkv = jax.nn.silu(dwconv(qkv, conv_w)).astype(f32)
        q, k, v = split_cols(qkv, (GDN_H * GDN_DK, GDN_H * GDN_DK, GDN_H * GDN_DV))
        q = l2norm(q.reshape(bsz, n, GDN_H, GDN_DK)) * GDN_DK ** -0.5
        k = l2norm(k.reshape(bsz, n, GDN_H, GDN_DK))
        v = v.reshape(bsz, n, GDN_H, GDN_DV)
        lg = -jnp.exp(a_log.astype(f32)) * jax.nn.softplus(a.astype(f32).reshape(bsz, n, 2, GDN_H) + dt_bias.astype(f32))
        bt = jax.nn.sigmoid(beta.astype(f32).reshape(bsz, n, 2, GDN_H))
        return z, q, k, v, lg, bt
    zc, qc, kc, vc, lgc, btc = prep(*p_ctx)
    zl, ql, kl, vl, lgl, btl = prep(*p_lat)
    s0 = jnp.zeros((ql.shape[0], GDN_H, GDN_DK, GDN_DV), f32)
    yc, yl = two_way(gdn_scan, (qc, kc, vc, lgc[:, :, 0], btc[:, :, 0]), (ql, kl, vl, lgl[:, :, 0], btl[:, :, 0]), (),
                     (qc, kc, vc, lgc[:, :, 1], btc[:, :, 1]), (ql, kl, vl, lgl[:, :, 1], btl[:, :, 1]), (), s0)
    def post(o, z):
        bsz, n = o.shape[:2]
        o = rmsnorm(o, norm_w) * jax.nn.silu(z.astype(f32).reshape(bsz, n, GDN_H, GDN_DV))
        return o.reshape(bsz, n, GDN_H * GDN_DV)
    return (post(yc, zc) if need_ctx else None), post(yl, zl)


def merge(ys, gate_logits, gate_b, w_branch, w_out, dtype):
    bsz, n = gate_logits.shape[:2]
    gates = jax.nn.sigmoid(gate_logits.reshape(bsz, n, N_BRANCH, D_MODEL) + gate_b)
    ys = jnp.stack(ys, axis=2).astype(dtype)
    merged = jnp.einsum('bskd,bskw,kwd->bsd', gates, ys, w_branch)
    return merged @ w_out


def token_mixers(xl, xc, need_ctx, w_in, ssm_conv_w, ssm_conv_b, ssm_dt_bias, ssm_a_log, ssm_d, ssm_norm,
                 ml_conv_w, ml_conv_b, ml_gate_b, ml_norm, s5_a_re, s5_a_im, s5_log_dt, s5_b_re, s5_b_im,
                 s5_c_re, s5_c_im, s5_d, s5_glu_w, s5_glu_b, gdn_conv_w, gdn_dt_bias, gdn_a_log, gdn_norm,
                 gate_b, w_branch, w_out):
    pl = split_cols(xl @ w_in, IN_SPLITS)
    pc = split_cols(xc @ w_in, IN_SPLITS)
    ya_c, ya_l = mamba_branch(pc[0:3], pl[0:3], need_ctx, ssm_conv_w, ssm_conv_b, ssm_dt_bias, ssm_a_log, ssm_d, ssm_norm)
    yb_c, yb_l = mlstm_branch(pc[3:7], pl[3:7], need_ctx, ml_conv_w, ml_conv_b, ml_gate_b, ml_norm)
    yc_c, yc_l = s5_branch(pc[7], pl[7], need_ctx, s5_a_re, s5_a_im, s5_log_dt, s5_b_re, s5_b_im,
                           s5_c_re, s5_c_im, s5_d, s5_glu_w, s5_glu_b)
    yd_c, yd_l = gdn_branch(pc[8:12], pl[8:12], need_ctx, gdn_conv_w, gdn_dt_bias, gdn_a_log, gdn_norm)
    y_lat = merge((ya_l, yb_l, yc_l, yd_l), pl[12], gate_b, w_branch, w_out, xl.dtype)
    y_ctx = merge((ya_c, yb_c, yc_c, yd_c), pc[12], gate_b, w_branch, w_out, xc.dtype) if need_ctx else None
    return y_lat, y_ctx


def setup_inputs(seed: int = 0) -> dict:
    key = jax.random.key(seed)
    ks = iter(jax.random.split(key, 64))
    f32 = jnp.float32
    def nrm(shape, scale):
        return scale * jax.random.normal(next(ks), shape, f32)
    def unif(shape, lo, hi):
        return jax.random.uniform(next(ks), shape, f32, lo, hi)
    def dt_bias(shape):
        dt = jnp.exp(unif(shape, math.log(1e-3), math.log(1e-1)))
        return dt + jnp.log(-jnp.expm1(-dt))
    d = D_MODEL
    ig_b = nrm((DEPTH, 2, 1, ML_H), 0.1)
    fg_b = jnp.linspace(3.0, 6.0, ML_H, dtype=f32) + nrm((DEPTH, 2, 1, ML_H), 0.1)
    return {
        'x': nrm((BATCH, SEQ, d), 1.0),
        'c': nrm((BATCH, d), 1.0),
        'ctx': nrm((BATCH, CTX_LEN, d), 1.0),
        'c_ctx': nrm((d,), 1.0),
        'ada_w': nrm((DEPTH, d, N_MOD * d), 0.5 * d ** -0.5),
        'ada_b': nrm((DEPTH, N_MOD * d), 0.02),
        'norm_w': 1.0 + nrm((DEPTH, 3, d), 0.02),
        'ffn_up': nrm((DEPTH, 2, d, 2 * D_FF), d ** -0.5),
        'ffn_down': nrm((DEPTH, 2, D_FF, d), D_FF ** -0.5),
        'w_in': nrm((DEPTH, d, D_IN), d ** -0.5),
        'ssm_conv_w': nrm((DEPTH, CONV_K, SSM_CONV_CH), CONV_K ** -0.5),
        'ssm_conv_b': nrm((DEPTH, SSM_CONV_CH), 0.02),
        'ssm_dt_bias': dt_bias((DEPTH, 2, SSM_H)),
        'ssm_a_log': jnp.log(unif((DEPTH, 2, SSM_H), 1.0, 16.0)),
        'ssm_d': 1.0 + nrm((DEPTH, SSM_H), 0.02),
        'ssm_norm': 1.0 + nrm((DEPTH, SSM_W), 0.02),
        'ml_conv_w': nrm((DEPTH, CONV_K, 2 * ML_H * ML_DK), CONV_K ** -0.5),
        'ml_conv_b': nrm((DEPTH, 2 * ML_H * ML_DK), 0.02),
        'ml_gate_b': jnp.concatenate([ig_b, fg_b], axis=2),
        'ml_norm': 1.0 + nrm((DEPTH, ML_H * ML_DV), 0.02),
        's5_a_re': -0.5 + nrm((DEPTH, 2, S5_GROUPS, S5_P), 0.01),
        's5_a_im': jnp.pi * jnp.arange(S5_P, dtype=f32) + nrm((DEPTH, 2, S5_GROUPS, S5_P), 0.01),
        's5_log_dt': unif((DEPTH, 2, S5_GROUPS), math.log(1e-3), math.log(1e-1)),
        's5_b_re': nrm((DEPTH, S5_GROUPS, S5_P, S5_GC), 0.7 * S5_GC ** -0.5),
        's5_b_im': nrm((DEPTH, S5_GROUPS, S5_P, S5_GC), 0.7 * S5_GC ** -0.5),
        's5_c_re': nrm((DEPTH, S5_GROUPS, S5_GC, S5_P), 0.7 * S5_P ** -0.5),
        's5_c_im': nrm((DEPTH, S5_GROUPS, S5_GC, S5_P), 0.7 * S5_P ** -0.5),
        's5_d': nrm((DEPTH, BRANCH_W), 1.0),
        's5_glu_w': nrm((DEPTH, BRANCH_W, BRANCH_W), BRANCH_W ** -0.5),
        's5_glu_b': nrm((DEPTH, BRANCH_W), 0.02),
        'gdn_conv_w': nrm((DEPTH, CONV_K, GDN_H * (2 * GDN_DK + GDN_DV)), CONV_K ** -0.5),
        'gdn_dt_bias': dt_bias((DEPTH, 2, GDN_H)),
        'gdn_a_log': jnp.log(unif((DEPTH, 2, GDN_H), 1.0, 16.0)),
        'gdn_norm': 1.0 + nrm((DEPTH, GDN_DV), 0.02),
        'gate_b': nrm((DEPTH, N_BRANCH, d), 0.02),
        'w_branch': nrm((DEPTH, N_BRANCH, BRANCH_W, d), BRANCH_W ** -0.5),
        'w_out': nrm((DEPTH, d, d), d ** -0.5),
        'final_norm': 1.0 + nrm((d,), 0.02),
    }


def reference(x, c, ctx, c_ctx, ada_w, ada_b, norm_w, ffn_up, ffn_down, w_in, ssm_conv_w, ssm_conv_b,
              ssm_dt_bias, ssm_a_log, ssm_d, ssm_norm, ml_conv_w, ml_conv_b, ml_gate_b, ml_norm,
              s5_a_re, s5_a_im, s5_log_dt, s5_b_re, s5_b_im, s5_c_re, s5_c_im, s5_d, s5_glu_w, s5_glu_b,
              gdn_conv_w, gdn_dt_bias, gdn_a_log, gdn_norm, gate_b, w_branch, w_out, final_norm):
    rows = x.shape[1] // GRID_W
    h_lat, h_ctx = x, ctx
    s_c, s_cc = jax.nn.silu(c), jax.nn.silu(c_ctx)
    for l in range(DEPTH):
        need_ctx = l < DEPTH - 1
        mod = jnp.split((s_c @ ada_w[l] + ada_b[l])[:, None, :], N_MOD, axis=-1)
        mod_c = jnp.split(s_cc @ ada_w[l] + ada_b[l], N_MOD, axis=-1)
        h_lat = h_lat + 0.5 * mod[2] * swiglu(modulate(rmsnorm(h_lat, norm_w[l, 0]), mod[0], mod[1]), ffn_up[l, 0], ffn_down[l, 0])
        h_ctx = h_ctx + 0.5 * mod_c[2] * swiglu(modulate(rmsnorm(h_ctx, norm_w[l, 0]), mod_c[0], mod_c[1]), ffn_up[l, 0], ffn_down[l, 0])
        xl = modulate(rmsnorm(h_lat, norm_w[l, 1]), mod[3], mod[4])
        xc = modulate(rmsnorm(h_ctx, norm_w[l, 1]), mod_c[3], mod_c[4])
        col_order = l % 2 == 1
        if col_order:
            xl = raster_to_col(xl, rows)
        y_lat, y_ctx = token_mixers(xl, xc, need_ctx, w_in[l], ssm_conv_w[l], ssm_conv_b[l], ssm_dt_bias[l],
                                    ssm_a_log[l], ssm_d[l], ssm_norm[l], ml_conv_w[l], ml_conv_b[l], ml_gate_b[l],
                                    ml_norm[l], s5_a_re[l], s5_a_im[l], s5_log_dt[l], s5_b_re[l], s5_b_im[l],
                                    s5_c_re[l], s5_c_im[l], s5_d[l], s5_glu_w[l], s5_glu_b[l], gdn_conv_w[l],
                                    gdn_dt_bias[l], gdn_a_log[l], gdn_norm[l], gate_b[l], w_branch[l], w_out[l])
        if col_order:
            y_lat = col_to_raster(y_lat, rows)
        h_lat = h_lat + mod[5] * y_lat
        h_lat = h_lat + 0.5 * mod[8] * swiglu(modulate(rmsnorm(h_lat, norm_w[l, 2]), mod[6], mod[7]), ffn_up[l, 1], ffn_down[l, 1])
        if need_ctx:
            h_ctx = h_ctx + mod_c[5] * y_ctx
            h_ctx = h_ctx + 0.5 * mod_c[8] * swiglu(modulate(rmsnorm(h_ctx, norm_w[l, 2]), mod_c[6], mod_c[7]), ffn_up[l, 1], ffn_down[l, 1])
    return rmsnorm(h_lat, final_norm)
```

```python
import numpy as np
from contextlib import ExitStack
import concourse.bass as bass
import concourse.mybir as mybir
from concourse.bass_utils import run_bass_kernel_spmd
from concourse.ap import AP

F32 = mybir.dt.float32
BF16 = mybir.dt.bfloat16
ALU = mybir.AluOpType
AF = mybir.ActivationFunctionType
AX = mybir.AxisListType

D = 1024
SEQ = 4096
NCTX = 256
NTOK = SEQ + NCTX
DEPTH = 4
DFF = 2816
DIN = 9776
EPS = 1e-6
NB = 4


class Buf:
    __slots__ = ("name", "w", "r", "excl")

    def __init__(self, name="", excl=False):
        self.name = name
        self.w = None
        self.r = {}
        self.excl = excl


class Eng:
    def __init__(self, ctx, name, e, sem, is_pe=False):
        self.ctx, self.name, self.e, self.sem = ctx, name, e, sem
        self.count = 0
        self.known = {}
        self.is_pe = is_pe
        self.dsems = []
        self.dn = 0
        self.pending = None
        self.pkey = None

    def close(self):
        if self.pending is not None:
            self.pending.then_inc(self.sem, 1)
            self.count += 1
            self.pending = None
            self.pkey = None

    def _wait(self, ev):
        if ev is None:
            return
        sem, val = ev
        if self.is_pe and sem is self.sem:
            return
        if self.known.get(id(sem), 0) >= val:
            return
        pe = self.ctx.pe
        if sem is pe.sem and pe.pending is not None and val == pe.count + 1:
            pe.close()
        self.e.wait_ge(sem, val)
        self.known[id(sem)] = val
        self.ctx.ninstr += 1

    def _deps(self, reads, writes):
        for b in reads:
            self._wait(b.w)
            if b.excl:
                for ev in list(b.r.values()):
                    if ev[0] is not self.sem:
                        self._wait(ev)
        for b in writes:
            self._wait(b.w)
            for ev in list(b.r.values()):
                self._wait(ev)

    def _mark(self, ev, reads, writes):
        for b in reads:
            b.r[id(ev[0])] = ev
        for b in writes:
            b.w = ev
            b.r = {}

    def op(self, fn, reads=(), writes=()):
        if self.is_pe:
            key = tuple(id(b) for b in writes)
            if self.pending is not None and key != self.pkey:
                self.close()
            self._deps(reads, writes)
            if self.pending is not None and key != self.pkey:
                self.close()
            ins = fn(self.e)
            self.pending, self.pkey = ins, key
            ev = (self.sem, self.count + 1)
            self._mark(ev, reads, writes)
            self.ctx.ninstr += 1
            return ev
        self._deps(reads, writes)
        ins = fn(self.e)
        self.count += 1
        ins.then_inc(self.sem, 1)
        ev = (self.sem, self.count)
        self._mark(ev, reads, writes)
        self.ctx.ninstr += 1
        return ev

    def dma(self, out, in_, reads=(), writes=(), **kw):
        K = len(self.dsems)
        n = self.dn
        sem = self.dsems[n % K]
        if n >= K:
            self._wait((sem, 16 * (n // K)))
        self._deps(reads, writes)
        ins = self.e.dma_start(out=out, in_=in_, **kw)
        ins.then_inc(sem, 16)
        ev = (sem, 16 * (n // K + 1))
        self.dn += 1
        self._mark(ev, reads, writes)
        self.ctx.ninstr += 1
        return ev

    def dma_final_events(self):
        K = len(self.dsems)
        out = []
        for i in range(min(K, self.dn)):
            cnt = (self.dn - 1 - i) // K + 1
            out.append((self.dsems[i], 16 * cnt))
        return out


class Ctx:
    def __init__(self, nc, stack, ndsem=8):
        self.nc = nc
        self.ninstr = 0
        S = lambda n: stack.enter_context(nc.semaphore(n))
        self.pe = Eng(self, "pe", nc.tensor, S("s_pe"), is_pe=True)
        self.act = Eng(self, "act", nc.scalar, S("s_act"))
        self.dve = Eng(self, "dve", nc.vector, S("s_dve"))
        self.pool = Eng(self, "pool", nc.gpsimd, S("s_pool"))
        self.sp = Eng(self, "sp", nc.sync, S("s_sp"))
        self.engs = [self.pe, self.act, self.dve, self.pool, self.sp]
        self.dmaq = [self.sp, self.pool, self.act]
        for e in self.dmaq:
            e.dsems = [S(f"d_{e.name}{i}") for i in range(ndsem)]

    def barrier(self):
        self.pe.close()
        evs = []
        for q in self.dmaq:
            evs += q.dma_final_events()
        for o in self.engs:
            if o.count:
                evs.append((o.sem, o.count))
        for e in self.engs:
            for ev in evs:
                if ev[0] is e.sem:
                    continue
                e._wait(ev)


class Pool_:
    _uid = [0]

    def __init__(self, nc, stack):
        self.nc, self.stack = nc, stack
        Pool_._uid[0] += 1
        self.uid = Pool_._uid[0]
        self.n = 0

    def sb(self, shape, dt=F32, name=None):
        self.n += 1
        t = self.stack.enter_context(self.nc.sbuf_tensor(f"{name or 't'}_{self.n}_{self.uid}", list(shape), dt))
        return t, Buf(name or "t")

    def ps(self, shape, dt=F32, name=None):
        self.n += 1
        t = self.stack.enter_context(self.nc.psum_tensor(f"{name or 'p'}_{self.n}_{self.uid}", list(shape), dt))
        nbytes = int(np.prod(shape[1:])) * (4 if dt == F32 else 2)
        assert nbytes == 2048, (name, shape)
        return t, Buf(name or "p", excl=True)


def bcast_row(ap_row, n=128):
    return ap_row.partition_broadcast(n)


def build(cfg=None):
    cfg = cfg or {}
    nlayers = cfg.get("nlayers", DEPTH)
    do_mix = cfg.get("mix", True)
    nc = bass.Bass("TRN2", target_bir_lowering=False)
    dt_in = lambda name, shape: nc.dram_tensor(name, list(shape), F32, kind="ExternalInput").ap()
    x_d = dt_in("x", [SEQ, D])
    ctx_d = dt_in("ctx", [NCTX, D])
    cc_d = dt_in("cc", [2, D])
    ada_w = dt_in("ada_w", [DEPTH, D, 9 * D])
    ada_b = dt_in("ada_b", [DEPTH, 9 * D])
    norm_w = dt_in("norm_w", [DEPTH, 3, D])
    ffn_up = dt_in("ffn_up", [DEPTH, 2, D, 2 * DFF])
    ffn_down = dt_in("ffn_down", [DEPTH, 2, DFF, D])
    final_norm = dt_in("final_norm", [1, D])
    w_in = dt_in("w_in", [DEPTH, D, DIN])
    ssm_conv_w = dt_in("ssm_conv_w", [DEPTH, 5, 1024])
    ssm_conv_b = dt_in("ssm_conv_b", [DEPTH, 1024])
    ssm_dt_bias = dt_in("ssm_dt_bias", [DEPTH, 16])
    ssm_a_log = dt_in("ssm_a_log", [DEPTH, 16])
    ssm_d = dt_in("ssm_d", [DEPTH, 8])
    ssm_norm = dt_in("ssm_norm", [DEPTH, 512])
    ml_conv_w = dt_in("ml_conv_w", [DEPTH, 5, 512])
    ml_conv_b = dt_in("ml_conv_b", [DEPTH, 512])
    ml_gate_b = dt_in("ml_gate_b", [DEPTH, 16])
    ml_norm = dt_in("ml_norm", [DEPTH, 512])
    s5_a_re = dt_in("s5_a_re", [DEPTH, 2, 32, 64])
    s5_a_im = dt_in("s5_a_im", [DEPTH, 2, 32, 64])
    s5_log_dt = dt_in("s5_log_dt", [DEPTH, 2, 32])
    s5_b_re = dt_in("s5_b_re", [DEPTH, 32, 64, 16])
    s5_b_im = dt_in("s5_b_im", [DEPTH, 32, 64, 16])
    s5_c_re = dt_in("s5_c_re", [DEPTH, 32, 16, 64])
    s5_c_im = dt_in("s5_c_im", [DEPTH, 32, 16, 64])
    s5_d = dt_in("s5_d", [DEPTH, 512])
    s5_glu_w = dt_in("s5_glu_w", [DEPTH, 512, 512])
    s5_glu_b = dt_in("s5_glu_b", [DEPTH, 512])
    gdn_conv_w = dt_in("gdn_conv_w", [DEPTH, 5, 1536])
    gdn_dt_bias = dt_in("gdn_dt_bias", [DEPTH, 8])
    gdn_a_log = dt_in("gdn_a_log", [DEPTH, 8])
    gdn_norm = dt_in("gdn_norm", [1, DEPTH * 128])
    gate_b = dt_in("gate_b", [DEPTH, 4096])
    w_branch = dt_in("w_branch", [DEPTH, 4, 512, D])
    w_out = dt_in("w_out", [DEPTH, D, D])
    projT_d = nc.dram_tensor("projT_scr", [3072, NTOK], BF16, kind="Internal").ap()
    projU_d = nc.dram_tensor("projU_scr", [512, NTOK], F32, kind="Internal").ap()
    ptok_d = nc.dram_tensor("ptok_scr", [NTOK, 2096], F32, kind="Internal").ap()
    gsig_d = nc.dram_tensor("gsig_scr", [NTOK, 4096], BF16, kind="Internal").ap()
    convT_d = nc.dram_tensor("convT_scr", [3072, NTOK], BF16, kind="Internal").ap()
    ctok_d = nc.dram_tensor("ctok_scr", [NTOK, 2048], F32, kind="Internal").ap()
    yf_d = nc.dram_tensor("yf_scr", [NTOK, 512], F32, kind="Internal").ap()
    ys_d = nc.dram_tensor("ys_scr", [4, NTOK, 512], F32, kind="Internal").ap()
    s5T_d = nc.dram_tensor("s5T_scr", [512, NTOK], BF16, kind="Internal").ap()
    s5acc_d = nc.dram_tensor("s5acc_scr", [512, NTOK], F32, kind="Internal").ap()
    out_d = nc.dram_tensor("out", [SEQ, D], F32, kind="ExternalOutput").ap()
    h_d = nc.dram_tensor("h_scr", [NTOK, D], F32, kind="Internal").ap()
    mod_d = nc.dram_tensor("mod_scr", [DEPTH, 2, 9 * D], F32, kind="Internal").ap()
    dbg = {}
    for name, shape in cfg.get("dbg", {}).items():
        dbg[name] = nc.dram_tensor(name, list(shape), F32, kind="ExternalOutput").ap()

    hB = [Buf(f"h{i}") for i in range(NTOK // 128)]
    modB = Buf("mod")

    with ExitStack() as top:
        c = Ctx(nc, top)
        pe, act, dve, pool, sp = c.pe, c.act, c.dve, c.pool, c.sp

        cst = Pool_(nc, top)
        ident, identB = cst.sb([128, 128], BF16, "ident")
        identf, identfB = cst.sb([128, 128], F32, "identf")
        pool.op(lambda e: e.memset(identf[:], 0.0), writes=[identfB])
        pool.op(lambda e: e.affine_select(out=identf[:], in_=identf[:], pattern=[[-1, 128]], compare_op=ALU.not_equal,
                                          fill=1.0, base=0, channel_multiplier=1), reads=[identfB], writes=[identfB])
        dve.op(lambda e: e.tensor_copy(out=ident[:], in_=identf[:]), reads=[identfB], writes=[identB])

        with ExitStack() as ph:
            P = Pool_(nc, ph)
            sp.dma(h_d[0:NCTX, :], ctx_d[:, :], writes=hB[0:2])
            for i in range(8):
                sp.dma(h_d[NCTX + i * 512:NCTX + (i + 1) * 512, :], x_d[i * 512:(i + 1) * 512, :],
                       writes=hB[2 + 4 * i:2 + 4 * (i + 1)])
            ccT, ccTB = P.sb([128, 8, 2], F32, "ccT")
            for k in range(8):
                for r in range(2):
                    src = AP(cc_d.tensor, r * D + k * 128, [[1, 128], [1, 1]])
                    sp.dma(ccT[:, k, r:r + 1], src, writes=[ccTB])
            sT, sTB = P.sb([128, 8, 2], F32, "sT")
            act.op(lambda e: e.activation(out=sT[:], in_=ccT[:], func=AF.Silu), reads=[ccTB], writes=[sTB])
            wbuf = [P.sb([128, 8, 512], F32, f"adaw{i}") for i in range(2)]
            mps = [P.ps([2, 512], F32, f"mps{i}") for i in range(2)]
            n_it = 0
            row, rowB = P.sb([2, 9 * D], F32, "modrow")
            bia, biaB = P.sb([2, 9 * D], F32, "modb")
            for l in range(nlayers):
                sp.dma(bia[:], ada_b[l:l + 1, :].partition_broadcast(2), writes=[biaB])
                for nb in range(18):
                    wt, wB = wbuf[n_it % 2]
                    pt, pB = mps[n_it % 2]
                    n_it += 1
                    q = sp if nb % 2 == 0 else act
                    q.dma(wt[:], ada_w[l, :, nb * 512:(nb + 1) * 512].rearrange("(k p) n -> p k n", p=128), writes=[wB])
                    for k in range(8):
                        pe.op(lambda e, k=k: e.matmul(pt[:], lhsT=sT[:, k, :], rhs=wt[:, k, :], start=(k == 0), stop=(k == 7)),
                              reads=[sTB, wB], writes=[pB])
                    dve.op(lambda e: e.tensor_tensor(out=row[:, nb * 512:(nb + 1) * 512], in0=pt[:], in1=bia[:, nb * 512:(nb + 1) * 512],
                                                     op=ALU.add), reads=[pB, biaB], writes=[rowB])
                sp.dma(mod_d[l, :, :], row[:], reads=[rowB], writes=[modB])
        c.barrier()

        def rstd_inplace(ss, ssB, scale, sl=None):
            a = ss[:] if sl is None else sl
            dve.op(lambda e: e.tensor_scalar(out=a, in0=a, scalar1=scale, scalar2=EPS, op0=ALU.mult, op1=ALU.add), reads=[ssB], writes=[ssB])
            act.op(lambda e: e.activation(out=a, in_=a, func=AF.Sqrt), reads=[ssB], writes=[ssB])
            dve.op(lambda e: e.reciprocal(out=a, in_=a), reads=[ssB], writes=[ssB])

        def alloc_bc(P):
            return [P.sb([128, D], F32, n) for n in ("g1", "sh", "gt")]

        def fill_bc(bc, l, which, v, shift_i, scale_i, gate_i, gate_mul):
            (g1, g1B), (sh, shB), (gt, gtB) = bc
            tmp, tmpB = gt, gtB
            sp.dma(tmp[:], mod_d[l, v:v + 1, scale_i * D:(scale_i + 1) * D].partition_broadcast(128), reads=[modB], writes=[tmpB])
            sp.dma(g1[:], norm_w[l, which:which + 1, :].partition_broadcast(128), writes=[g1B])
            sp.dma(sh[:], mod_d[l, v:v + 1, shift_i * D:(shift_i + 1) * D].partition_broadcast(128), reads=[modB], writes=[shB])
            dve.op(lambda e: e.scalar_tensor_tensor(out=g1[:], in0=tmp[:], scalar=1.0, in1=g1[:], op0=ALU.add, op1=ALU.mult),
                   reads=[tmpB, g1B], writes=[g1B])
            sp.dma(gt[:], mod_d[l, v:v + 1, gate_i * D:(gate_i + 1) * D].partition_broadcast(128), reads=[modB], writes=[gtB])
            if gate_mul != 1.0:
                act.op(lambda e: e.mul(out=gt[:], in_=gt[:], mul=gate_mul), reads=[gtB], writes=[gtB])
            return (g1, g1B), (sh, shB), (gt, gtB)

        def norm_mod(P, ht, hBuf, g1, sh, out_bf, outB, ss, ssB, junk, junkB):
            act.op(lambda e: e.activation(out=junk[:], in_=ht[:], func=AF.Square, accum_out=ss[:]), reads=[hBuf], writes=[junkB, ssB])
            rstd_inplace(ss, ssB, 1.0 / D)
            dve.op(lambda e: e.scalar_tensor_tensor(out=junk[:], in0=ht[:], scalar=ss[:, 0:1], in1=g1[0][:], op0=ALU.mult, op1=ALU.mult),
                   reads=[hBuf, ssB, g1[1]], writes=[junkB])
            dve.op(lambda e: e.tensor_tensor(out=out_bf[:], in0=junk[:], in1=sh[0][:], op=ALU.add), reads=[junkB, sh[1]], writes=[outB])

        def groups_for(include_ctx=True):
            gs = []
            if include_ctx:
                gs.append((1, [0, 1]))
            for i in range(8):
                gs.append((0, [2 + 4 * i + j for j in range(4)]))
            return gs

        def ffn_phase(l, which, include_ctx=True):
            shift_i, scale_i, gate_i = (0, 1, 2) if which == 0 else (6, 7, 8)
            nw_i = 0 if which == 0 else 2
            with ExitStack() as ph:
                P = Pool_(nc, ph)
                wup, wupB = P.sb([128, 8, 2 * DFF], BF16, "wup")
                wdn, wdnB = P.sb([128, 22, D], BF16, "wdn")
                wupBs = [Buf() for _ in range(8)]
                for k in range(8):
                    pool.dma(wup[:, k, :], ffn_up[l, which, k * 128:(k + 1) * 128, :], writes=[wupBs[k]])
                wdnBs = [Buf() for _ in range(2)]
                for j in range(2):
                    pool.dma(wdn[:, 11 * j:11 * (j + 1), :],
                             ffn_down[l, which, 11 * j * 128:11 * (j + 1) * 128, :].rearrange("(k p) n -> p k n", p=128), writes=[wdnBs[j]])
                xT, xTB = P.sb([128, 8, 512], BF16, "xT")
                actT, actTB = P.sb([128, 22, 512], BF16, "actT")
                hts = [P.sb([128, D], F32, f"ht{i}") for i in range(1)]
                xms = [P.sb([128, D], BF16, f"xm{i}") for i in range(1)]
                sgs = [P.sb([128, 512], BF16, f"sg{i}") for i in range(2)]
                junk, junkB = P.sb([128, D], F32, "junk")
                ss, ssB = P.sb([128, 1], F32, "ss")
                tps = [P.ps([128, 8, 128], BF16, f"tp{i}") for i in range(2)]
                gps = [P.ps([128, 512], F32, f"gp{i}") for i in range(4)]
                dps = [P.ps([128, 512], F32, f"dp{i}") for i in range(2)]
                cur_v = None
                it = 0
                bct = alloc_bc(P)
                for v, tiles in groups_for(include_ctx):
                    if v != cur_v:
                        bc = fill_bc(bct, l, nw_i, v, shift_i, scale_i, gate_i, 0.5)
                        cur_v = v
                    g1, sh, gt = bc
                    nt = len(tiles)
                    W = nt * 128
                    for ti, t in enumerate(tiles):
                        ht, htB = hts[0]
                        xm, xmB = xms[0]
                        tp, tpB = tps[it % 2]
                        it += 1
                        sp.dma(ht[:], h_d[t * 128:(t + 1) * 128, :], reads=[hB[t]], writes=[htB])
                        norm_mod(P, ht, htB, g1, sh, xm, xmB, ss, ssB, junk, junkB)
                        for k in range(8):
                            pe.op(lambda e, k=k: e.transpose(out=tp[:, k, :], in_=xm[:, k * 128:(k + 1) * 128], identity=ident[:]),
                                  reads=[xmB, identB], writes=[tpB])
                        act.op(lambda e: e.copy(out=xT[:, :, ti * 128:(ti + 1) * 128], in_=tp[:]), reads=[tpB], writes=[xTB])
                    for f in range(22):
                        gp, gpB = gps[2 * (f % 2)]
                        up, upB = gps[2 * (f % 2) + 1]
                        sg, sgB = sgs[f % 2]
                        for k in range(8):
                            pe.op(lambda e, k=k: e.matmul(gp[:, 0:W], lhsT=wup[:, k, f * 128:(f + 1) * 128], rhs=xT[:, k, 0:W],
                                                          start=(k == 0), stop=(k == 7)), reads=[wupBs[k], xTB], writes=[gpB])
                        for k in range(8):
                            pe.op(lambda e, k=k: e.matmul(up[:, 0:W], lhsT=wup[:, k, DFF + f * 128:DFF + (f + 1) * 128], rhs=xT[:, k, 0:W],
                                                          start=(k == 0), stop=(k == 7)), reads=[wupBs[k], xTB], writes=[upB])
                        act.op(lambda e: e.activation(out=sg[:, 0:W], in_=gp[:, 0:W], func=AF.Silu), reads=[gpB], writes=[sgB])
                        dve.op(lambda e: e.tensor_tensor(out=actT[:, f, 0:W], in0=up[:, 0:W], in1=sg[:, 0:W], op=ALU.mult),
                               reads=[upB, sgB], writes=[actTB])
                    for ti, t in enumerate(tiles):
                        ht, htB = hts[0]
                        it += 1
                        sp.dma(ht[:], h_d[t * 128:(t + 1) * 128, :], reads=[hB[t]], writes=[htB])
                        for n in range(2):
                            dp, dpB = dps[n]
                            for f in range(22):
                                pe.op(lambda e, f=f: e.matmul(dp[:], lhsT=actT[:, f, ti * 128:(ti + 1) * 128], rhs=wdn[:, f, n * 512:(n + 1) * 512],
                                                              start=(f == 0), stop=(f == 21)), reads=[actTB, wdnBs[f // 11]], writes=[dpB])
                            dve.op(lambda e: e.tensor_tensor(out=junk[:, n * 512:(n + 1) * 512], in0=dp[:], in1=gt[0][:, n * 512:(n + 1) * 512],
                                                             op=ALU.mult), reads=[dpB, gt[1]], writes=[junkB])
                        pool.op(lambda e: e.tensor_tensor(out=ht[:], in0=ht[:], in1=junk[:], op=ALU.add), reads=[htB, junkB], writes=[htB])
                        sp.dma(h_d[t * 128:(t + 1) * 128, :], ht[:], reads=[htB], writes=[hB[t]])
            c.barrier()

        NEG = -30000.0
        ones_f, ones_fB = cst.sb([128, 128], F32, "ones_f")
        pool.op(lambda e: e.memset(ones_f[:], 1.0), writes=[ones_fB])
        def tri_const(name, pattern, cm, cmp, base_val, fill):
            t, tB = cst.sb([128, 128], F32, name)
            pool.op(lambda e: e.memset(t[:], base_val), writes=[tB])
            pool.op(lambda e: e.affine_select(out=t[:], in_=t[:], pattern=pattern, compare_op=cmp, fill=fill, base=0,
                                              channel_multiplier=cm), reads=[tB], writes=[tB])
            return t, tB
        TRI = [tri_const("trif", [[1, 128]], -1, ALU.is_ge, 1.0, 0.0), tri_const("trib", [[-1, 128]], 1, ALU.is_ge, 1.0, 0.0)]
        TRIC = [tri_const("tricf", [[-1, 128]], 1, ALU.is_gt, 1.0, 0.0), tri_const("tricb", [[1, 128]], -1, ALU.is_gt, 1.0, 0.0)]
        def rep_mask(name, pattern, cm, cmp):
            t, tB = cst.sb([128, 4, 128], F32, name)
            pool.op(lambda e: e.memset(t[:], 0.0), writes=[tB])
            pool.op(lambda e: e.affine_select(out=t[:], in_=t[:], pattern=[[0, 4]] + pattern, compare_op=cmp, fill=NEG, base=0,
                                              channel_multiplier=cm), reads=[tB], writes=[tB])
            return t, tB
        MASKT = [rep_mask("maskTf", [[1, 128]], -1, ALU.is_ge), rep_mask("maskTb", [[-1, 128]], 1, ALU.is_ge)]
        MASKL = [rep_mask("maskLf", [[-1, 128]], 1, ALU.is_gt), rep_mask("maskLb", [[1, 128]], -1, ALU.is_gt)]
        ind, indB = cst.sb([8, 8, 128], F32, "ind")
        pool.op(lambda e: e.memset(ind[:], 1.0), writes=[indB])
        pool.op(lambda e: e.affine_select(out=ind[:], in_=ind[:], pattern=[[-1, 8], [0, 128]], compare_op=ALU.is_equal, fill=0.0, base=0,
                                          channel_multiplier=1), reads=[indB], writes=[indB])
        ones8, ones8B = cst.sb([8, 128], F32, "ones8")
        nones8, nones8B = cst.sb([8, 128], F32, "nones8")
        pool.op(lambda e: e.memset(ones8[:], 1.0), writes=[ones8B])
        pool.op(lambda e: e.memset(nones8[:], -1.0), writes=[nones8B])
        ident4, ident4B = cst.sb([128, 4, 128], BF16, "ident4")
        for hh in range(4):
            dve.op(lambda e, hh=hh: e.tensor_copy(out=ident4[:, hh, :], in_=identf[:]), reads=[identfB], writes=[ident4B])

        onesb, onesbB = cst.sb([128, 64], BF16, "onesb")
        pool.op(lambda e: e.memset(onesb[:], 1.0), writes=[onesbB])
        msk, mskB = cst.sb([128, 2, 7, 128], BF16, "msk")
        pool.op(lambda e: e.memset(msk[:], 0.0), writes=[mskB])
        for lev in range(7):
            b_ = 1 << lev
            for m in range(128 // (2 * b_)):
                r0 = m * 2 * b_
                q_ = sp if (m % 2 == 0) else act
                q_.dma(msk[r0 + b_:r0 + 2 * b_, 0, lev, r0:r0 + b_], onesb[0:b_, 0:b_], reads=[onesbB], writes=[mskB])
                q_.dma(msk[r0:r0 + b_, 1, lev, r0 + b_:r0 + 2 * b_], onesb[0:b_, 0:b_], reads=[onesbB], writes=[mskB])
        projB, ptokB, gsigB, convTB, ctokB, yfB = Buf("projT"), Buf("ptok"), Buf("gsig"), Buf("convT"), Buf("ctok"), Buf("yf")
        ysB = [Buf(f"ys{i}") for i in range(4)]
        s5accB = Buf("s5acc")
        LAT = hB[2:]

        def h_pieces(l, t):
            if l % 2 == 0 or t < 2:
                return [(0, 128, h_d[t * 128:(t + 1) * 128, :])], [hB[t]]
            i = t - 2
            out = []
            for wi in range(2):
                out.append((wi * 64, 64, AP(h_d.tensor, (NCTX + 2 * i + wi) * D, [[64 * D, 64], [1, D]])))
            return out, LAT

        def prelude(P, l, tiles, W, res, bc):
            g1, sh, _ = bc
            for ti, t in enumerate(tiles):
                ht, htB = res["hts"][res["it"] % 2]
                xm, xmB = res["xms"][res["it"] % 2]
                tp, tpB = res["tps"][res["it"] % 2]
                res["it"] += 1
                pcs, deps = h_pieces(l, t)
                for (p0, n, ap) in pcs:
                    sp.dma(ht[p0:p0 + n, :], ap, reads=deps, writes=[htB])
                norm_mod(P, ht, htB, g1, sh, xm, xmB, res["ss"], res["ssB"], res["junk"], res["junkB"])
                for k in range(8):
                    pe.op(lambda e, k=k: e.transpose(out=tp[:, k, :], in_=xm[:, k * 128:(k + 1) * 128], identity=ident[:]),
                          reads=[xmB, identB], writes=[tpB])
                act.op(lambda e: e.copy(out=res["xT"][:, :, ti * 128:(ti + 1) * 128], in_=tp[:]), reads=[tpB], writes=[res["xTB"]])

        def prelude_res(P):
            res = {"it": 0}
            res["xT"], res["xTB"] = P.sb([128, 8, 512], BF16, "xT")
            res["hts"] = [P.sb([128, D], F32, f"ht{i}") for i in range(2)]
            res["xms"] = [P.sb([128, D], BF16, f"xm{i}") for i in range(2)]
            res["junk"], res["junkB"] = P.sb([128, D], F32, "junk")
            res["ss"], res["ssB"] = P.sb([128, 1], F32, "ss")
            res["tps"] = [P.ps([128, 8, 128], BF16, f"tp{i}") for i in range(2)]
            return res

        SEGF = [(512, 1536), (1552, 2064), (3616, 5152), (3104, 3616)]
        SEGT = [(0, 512), (1536, 1552), (2064, 2576), (2576, 3088), (3088, 3104), (5152, 5664), (5664, 5672), (5672, 5680)]
        NF, NT = 3584, 2096

        def proj_phase_a(l):
            with ExitStack() as ph:
                P = Pool_(nc, ph)
                wF, wFB = P.sb([128, 8, NF], BF16, "wF")
                wT, wTB = P.sb([128, 8, NT], BF16, "wT")
                off = 0
                for (a, b) in SEGF:
                    pool.dma(wF[:, :, off:off + b - a], w_in[l, :, a:b].rearrange("(k p) n -> p k n", p=128), writes=[wFB])
                    off += b - a
                off = 0
                for (a, b) in SEGT:
                    pool.dma(wT[:, :, off:off + b - a], w_in[l, :, a:b].rearrange("(k p) n -> p k n", p=128), writes=[wTB])
                    off += b - a
                res = prelude_res(P)
                bct = alloc_bc(P)
                fst = [P.sb([128, 512], F32, f"fst{i}") for i in range(2)]
                fstb = [P.sb([128, 512], BF16, f"fstb{i}") for i in range(4)]
                tst = [P.sb([128, NT], F32, f"tst{i}") for i in range(2)]
                fps = [P.ps([128, 512], F32, f"fps{i}") for i in range(3)]
                tpp = [P.ps([128, 512], F32, f"tpp{i}") for i in range(3)]
                cur_v = None
                n1 = n2 = n3 = 0
                for v, tiles in groups_for(True):
                    if v != cur_v:
                        bc = fill_bc(bct, l, 1, v, 3, 4, 5, 1.0)
                        cur_v = v
                    W = len(tiles) * 128
                    tok0 = tiles[0] * 128
                    prelude(P, l, tiles, W, res, bc)
                    xT, xTB = res["xT"], res["xTB"]
                    for cf in range(NF // 128):
                        fp, fpB = fps[n1 % 3]
                        st, stB = fstb[n1 % 4] if cf < 24 else fst[n1 % 2]
                        for k in range(8):
                            pe.op(lambda e, k=k: e.matmul(fp[:, 0:W], lhsT=wF[:, k, cf * 128:(cf + 1) * 128], rhs=xT[:, k, 0:W],
                                                          start=(k == 0), stop=(k == 7)), reads=[wFB, xTB], writes=[fpB])
                        eng = act if n1 % 2 == 0 else dve
                        if eng is act:
                            act.op(lambda e: e.copy(out=st[:, 0:W], in_=fp[:, 0:W]), reads=[fpB], writes=[stB])
                        else:
                            dve.op(lambda e: e.tensor_copy(out=st[:, 0:W], in_=fp[:, 0:W]), reads=[fpB], writes=[stB])
                        n1 += 1
                        if cf < 24:
                            sp.dma(projT_d[cf * 128:(cf + 1) * 128, tok0:tok0 + W], st[:, 0:W], reads=[stB], writes=[projB])
                        else:
                            sp.dma(projU_d[(cf - 24) * 128:(cf - 23) * 128, tok0:tok0 + W], st[:, 0:W], reads=[stB], writes=[projB])
                    for ti, t in enumerate(tiles):
                        ts_, tsB = tst[n2 % 2]
                        n2 += 1
                        for c0 in range(0, NT, 512):
                            c1 = min(NT, c0 + 512)
                            tp_, tpB_ = tpp[n3 % 3]
                            for k in range(8):
                                pe.op(lambda e, k=k: e.matmul(tp_[:, 0:c1 - c0], lhsT=xT[:, k, ti * 128:(ti + 1) * 128], rhs=wT[:, k, c0:c1],
                                                              start=(k == 0), stop=(k == 7)), reads=[wTB, xTB], writes=[tpB_])
                            if n3 % 2 == 0:
                                act.op(lambda e: e.copy(out=ts_[:, c0:c1], in_=tp_[:, 0:c1 - c0]), reads=[tpB_], writes=[tsB])
                            else:
                                dve.op(lambda e: e.tensor_copy(out=ts_[:, c0:c1], in_=tp_[:, 0:c1 - c0]), reads=[tpB_], writes=[tsB])
                            n3 += 1
                        sp.dma(ptok_d[t * 128:(t + 1) * 128, :], ts_[:], reads=[tsB], writes=[ptokB])
            c.barrier()

        def proj_phase_b(l):
            with ExitStack() as ph:
                P = Pool_(nc, ph)
                wG, wGB = P.sb([128, 8, 4096], BF16, "wG")
                for k in range(8):
                    pool.dma(wG[:, k, :], w_in[l, k * 128:(k + 1) * 128, 5680:9776], writes=[wGB])
                gb, gbB = P.sb([128, 4096], F32, "gb")
                sp.dma(gb[:], gate_b[l:l + 1, :].partition_broadcast(128), writes=[gbB])
                res = prelude_res(P)
                bct = alloc_bc(P)
                gst = [P.sb([128, 4096], BF16, f"gst{i}") for i in range(2)]
                gtm = [P.sb([128, 512], F32, f"gtm{i}") for i in range(2)]
                gps_ = [P.ps([128, 512], F32, f"gps{i}") for i in range(4)]
                cur_v = None
                n2 = n3 = 0
                for v, tiles in groups_for(True):
                    if v != cur_v:
                        bc = fill_bc(bct, l, 1, v, 3, 4, 5, 1.0)
                        cur_v = v
                    W = len(tiles) * 128
                    prelude(P, l, tiles, W, res, bc)
                    xT, xTB = res["xT"], res["xTB"]
                    for ti, t in enumerate(tiles):
                        gs_, gsB = gst[n2 % 2]
                        n2 += 1
                        for cb in range(8):
                            gp, gpB = gps_[n3 % 4]
                            tm, tmB = gtm[n3 % 2]
                            n3 += 1
                            for k in range(8):
                                pe.op(lambda e, k=k: e.matmul(gp[:], lhsT=xT[:, k, ti * 128:(ti + 1) * 128], rhs=wG[:, k, cb * 512:(cb + 1) * 512],
                                                              start=(k == 0), stop=(k == 7)), reads=[wGB, xTB], writes=[gpB])
                            dve.op(lambda e: e.tensor_tensor(out=tm[:], in0=gp[:], in1=gb[:, cb * 512:(cb + 1) * 512], op=ALU.add),
                                   reads=[gpB, gbB], writes=[tmB])
                            act.op(lambda e: e.activation(out=gs_[:, cb * 512:(cb + 1) * 512], in_=tm[:], func=AF.Sigmoid), reads=[tmB], writes=[gsB])
                        sp.dma(gsig_d[t * 128:(t + 1) * 128, :], gs_[:], reads=[gsB], writes=[gsigB])
            c.barrier()

        CTOK_COL = {0: 0, 1: 128, 2: 256, 3: 384, 4: 512, 5: 640, 10: 768, 11: 896, 16: 1024, 17: 1152, 18: 1280, 19: 1408,
                    20: 1536, 21: 1664, 22: 1792, 23: 1920}
        SEGS = [(0, NCTX), (NCTX, NTOK)]

        def conv_phase(l):
            with ExitStack() as ph:
                P = Pool_(nc, ph)
                xins = [P.sb([128, NTOK], BF16, f"xin{i}") for i in range(2)]
                accs = [P.sb([128, NTOK], F32, f"acc{i}") for i in range(2)]
                sbfs = [P.sb([128, NTOK], BF16, f"sbf{i}") for i in range(2)]
                sq, sqB = P.sb([128, NTOK], F32, "sq")
                rn, rnB = P.sb([128, 512], F32, "rn")
                cws = [P.sb([128, 6], F32, f"cw{i}") for i in range(2)]
                dgs = [P.sb([128, 5, 128], BF16, f"dg{i}") for i in range(2)]
                tst_ = [P.sb([128, 4, 128], F32, f"cts{i}") for i in range(2)]
                nps = [P.ps([128, 512], F32, f"nps{i}") for i in range(2)]
                tps_ = [P.ps([128, 4, 128], F32, f"ctp{i}") for i in range(2)]
                cps = [P.ps([128, 512], F32, f"cps{i}") for i in range(2)]
                nn = 0
                ncv = 0
                for ct in range(24):
                    xin, xinB = xins[ct % 2]
                    acc, accB = accs[ct % 2]
                    sbf, sbfB = sbfs[ct % 2]
                    cw, cwB = cws[ct % 2]
                    dg, dgB = dgs[ct % 2]
                    ve = dve
                    sp.dma(xin[:], projT_d[ct * 128:(ct + 1) * 128, :], reads=[projB], writes=[xinB])
                    if ct < 8:
                        wsrc, C, c0, bsrc = ssm_conv_w, 1024, ct * 128, ssm_conv_b
                    elif ct < 12:
                        wsrc, C, c0, bsrc = ml_conv_w, 512, (ct - 8) * 128, ml_conv_b
                    else:
                        wsrc, C, c0, bsrc = gdn_conv_w, 1536, (ct - 12) * 128, None
                    act.dma(cw[:, 0:5], AP(wsrc.tensor, l * 5 * C + c0, [[1, 128], [C, 5]]), writes=[cwB], allow_slow_non_contiguous=True)
                    if bsrc is not None:
                        act.dma(cw[:, 5:6], AP(bsrc.tensor, l * C + c0, [[1, 128], [1, 1]]), writes=[cwB])
                    else:
                        pool.op(lambda e: e.memset(cw[:, 5:6], 0.0), writes=[cwB])
                    for k in range(5):
                        pool.op(lambda e, k=k: e.tensor_scalar(out=dg[:, k, :], in0=ident[:], scalar1=cw[:, k:k + 1], scalar2=None, op0=ALU.mult),
                                reads=[identB, cwB], writes=[dgB])
                    for (s0, s1) in SEGS:
                        for b0 in range(s0, s1, 512):
                            b1 = min(s1, b0 + 512)
                            Wb = b1 - b0
                            cp, cpB = cps[ncv % 2]
                            ncv += 1
                            pe.op(lambda e: e.matmul(cp[:, 0:Wb], lhsT=dg[:, 2, :], rhs=xin[:, b0:b1], start=True, stop=False), reads=[dgB, xinB], writes=[cpB])
                            for ki, k in enumerate((0, 1, 3, 4)):
                                d = k - 2
                                a_ = max(b0 + d, s0)
                                b_ = min(b1 + d, s1)
                                pe.op(lambda e, k=k, a_=a_, b_=b_, d=d: e.matmul(cp[:, a_ - d - b0:b_ - d - b0], lhsT=dg[:, k, :], rhs=xin[:, a_:b_],
                                                                              start=False, stop=(ki == 3)), reads=[dgB, xinB], writes=[cpB])
                            act.op(lambda e: e.activation(out=acc[:, b0:b1], in_=cp[:, 0:Wb], func=AF.Silu, bias=cw[:, 5:6]), reads=[cpB, cwB], writes=[accB])
                    if ct in (8, 9):
                        act.op(lambda e: e.mul(out=acc[:], in_=acc[:], mul=0.125), reads=[accB], writes=[accB])
                    if 12 <= ct < 20:
                        qs = 128.0 ** -0.5 if ct < 16 else 1.0
                        act.op(lambda e: e.activation(out=sq[:], in_=acc[:], func=AF.Square), reads=[accB], writes=[sqB])
                        for g0 in range(0, NTOK, 512):
                            g1_ = min(NTOK, g0 + 512)
                            Wd = g1_ - g0
                            npt, npB = nps[nn % 2]
                            nn += 1
                            pe.op(lambda e: e.matmul(npt[:, 0:Wd], lhsT=ones_f[:], rhs=sq[:, g0:g1_], start=True, stop=True),
                                  reads=[ones_fB, sqB], writes=[npB])
                            dve.op(lambda e: e.tensor_scalar(out=rn[:, 0:Wd], in0=npt[:, 0:Wd], scalar1=EPS, scalar2=None, op0=ALU.add),
                                   reads=[npB], writes=[rnB])
                            act.op(lambda e: e.activation(out=rn[:, 0:Wd], in_=rn[:, 0:Wd], func=AF.Sqrt), reads=[rnB], writes=[rnB])
                            dve.op(lambda e: e.reciprocal(out=rn[:, 0:Wd], in_=rn[:, 0:Wd]), reads=[rnB], writes=[rnB])
                            dve.op(lambda e: e.scalar_tensor_tensor(out=acc[:, g0:g1_], in0=acc[:, g0:g1_], scalar=qs, in1=rn[:, 0:Wd],
                                                                    op0=ALU.mult, op1=ALU.mult), reads=[accB, rnB], writes=[accB])
                    ve.op(lambda e: e.tensor_copy(out=sbf[:], in_=acc[:]), reads=[accB], writes=[sbfB])
                    sp.dma(convT_d[ct * 128:(ct + 1) * 128, :], sbf[:], reads=[sbfB], writes=[convTB])
                    if ct in CTOK_COL:
                        col = CTOK_COL[ct]
                        for t0 in range(0, NTOK // 128, 4):
                            nt_ = min(4, NTOK // 128 - t0)
                            tp_, tpB_ = tps_[nn % 2]
                            ts_, tsB_ = tst_[nn % 2]
                            nn += 1
                            for j in range(nt_):
                                pe.op(lambda e, j=j: e.transpose(out=tp_[:, j, :], in_=acc[:, (t0 + j) * 128:(t0 + j + 1) * 128], identity=identf[:]),
                                      reads=[accB, identfB], writes=[tpB_])
                            act.op(lambda e: e.copy(out=ts_[:, 0:nt_, :], in_=tp_[:, 0:nt_, :]), reads=[tpB_], writes=[tsB_])
                            dst = AP(ctok_d.tensor, t0 * 128 * 2048 + col, [[2048, 128], [128 * 2048, nt_], [1, 128]])
                            sp.dma(dst, ts_[:, 0:nt_, :], reads=[tsB_], writes=[ctokB])
            c.barrier()
        def gla_branch(l, kind):
            if kind == "ssm":
                H, NK, PV, br = 8, 128, 64, 0
            elif kind == "ml":
                H, NK, PV, br = 4, 64, 129, 1
            else:
                H, NK, PV, br = 4, 128, 128, 3
            HP = H * PV
            with ExitStack() as ph:
                P = Pool_(nc, ph)
                par, parB = P.sb([128, 64], F32, "par")
                if kind == "ssm":
                    sp.dma(par[:, 0:16], ssm_dt_bias[l:l + 1, :].partition_broadcast(128), writes=[parB])
                    sp.dma(par[:, 16:32], ssm_a_log[l:l + 1, :].partition_broadcast(128), writes=[parB])
                    sp.dma(par[:, 32:40], ssm_d[l:l + 1, :].partition_broadcast(128), writes=[parB])
                    act.op(lambda e: e.activation(out=par[:, 16:32], in_=par[:, 16:32], func=AF.Exp), reads=[parB], writes=[parB])
                    act.op(lambda e: e.mul(out=par[:, 16:32], in_=par[:, 16:32], mul=-1.0), reads=[parB], writes=[parB])
                elif kind == "ml":
                    sp.dma(par[:, 0:16], ml_gate_b[l:l + 1, :].partition_broadcast(128), writes=[parB])
                else:
                    sp.dma(par[:, 0:8], gdn_dt_bias[l:l + 1, :].partition_broadcast(128), writes=[parB])
                    sp.dma(par[:, 16:24], gdn_a_log[l:l + 1, :].partition_broadcast(128), writes=[parB])
                    act.op(lambda e: e.activation(out=par[:, 16:24], in_=par[:, 16:24], func=AF.Exp), reads=[parB], writes=[parB])
                    act.op(lambda e: e.mul(out=par[:, 16:24], in_=par[:, 16:24], mul=-1.0), reads=[parB], writes=[parB])
                nrm, nrmB = P.sb([128, 512], F32, "nrm")
                if kind == "ssm":
                    sp.dma(nrm[:], ssm_norm[l:l + 1, :].partition_broadcast(128), writes=[nrmB])
                elif kind == "ml":
                    sp.dma(nrm[:], ml_norm[l:l + 1, :].partition_broadcast(128), writes=[nrmB])
                else:
                    for hh in range(4):
                        sp.dma(nrm[:, hh * 128:(hh + 1) * 128], gdn_norm[0:1, l * 128:(l + 1) * 128].partition_broadcast(128), writes=[nrmB])
                T = lambda shape, dt=F32, n=None: P.sb(shape, dt, n)
                qT, qTB = T([NK, H if kind != "ssm" else 2, 128], BF16, "qT")
                kT, kTB = T([NK, H if kind != "ssm" else 2, 128], BF16, "kT")
                ktok, ktokB = T([128, (2 if kind == "ssm" else H) * NK], BF16, "ktok")
                vtok, vtokB = T([128, H, PV if kind != "ml" else 128], F32, "vtok")
                graw, grawB = T([128, 16], F32, "graw")
                gwk, gwkB = T([128, 4, H], F32, "gwk")
                e3, e3B = T([128, 3, H], F32, "e3")
                gcs, gcsB = T([128, H], F32, "gcs")
                gcT, gcTB = T([H, 128], F32, "gcT")
                ngcT, ngcTB = T([H, 128], F32, "ngcT")
                gbd, gbdB = T([H, H, 128], F32, "gbd")
                decT, decTB = T([128, H, 128], F32, "decT")
                scT, scTB = T([128, H, 128], BF16, "scT")
                vs, vsB = T([128, H, PV], BF16, "vs")
                vw, vwB = T([128, H, PV], BF16, "vw")
                S, SB = T([NK, H, PV], F32, "S")
                Sb, SbB = T([NK, H, PV], BF16, "Sb")
                ytmp, ytmpB = T([128, H, PV], F32, "ytmp")
                y, yB_ = T([128, H, PV], F32, "y")
                yfl, yflB = T([128, 512], F32, "yfl")
                yo, yoB = T([128, 512], F32, "yo")
                zt, ztB = T([128, 512], F32, "zt")
                st4, st4B = T([128, 8], F32, "st4")
                junk, junkB = T([128, 512], F32, "gjunk")
                if kind == "gdn":
                    decL, decLB = T([128, H, 128], F32, "decL")
                    L0, L0B = T([128, H, 128], BF16, "L0")
                    U0, U0B = T([128, H, 128], BF16, "U0")
                    Tb, TbB = T([128, H, 128], BF16, "Tb")
                    Ttb, TtbB = T([128, H, 128], BF16, "Ttb")
                    Mb, MbB = T([128, H, 128], BF16, "Mb")
                    Mtb, MtbB = T([128, H, 128], BF16, "Mtb")
                    Ab, AbB = T([128, H, 128], BF16, "Ab")
                    Atb, AtbB = T([128, H, 128], BF16, "Atb")
                    rhsb, rhsbB = T([128, H, 128], BF16, "rhsb")
                sm, smB = P.ps([128, 512], F32, "sm")
                gd = [P.ps([128, 4, 128], F32, f"gd{i}") for i in range(2)]
                cb, cbB = P.ps([128, 4, 128], F32, "cb")
                yi = [P.ps([128, 512], F32, f"yi{i}") for i in range(2)]
                yx = [P.ps([128, 512], F32, f"yx{i}") for i in range(2)]
                nG = (H * 128) // 512
                hpb = 512 // PV if kind != "ml" else 2
                ybank = lambda arr, hh: arr[hh // hpb]
                ycol = lambda hh: (hh % hpb) * PV

                def chunk_rows(t):
                    return slice(t * 128, (t + 1) * 128)

                def mk_lset(i):
                    return (T([NK, H if kind != "ssm" else 2, 128], BF16, f"qT{i}"), T([NK, H if kind != "ssm" else 2, 128], BF16, f"kT{i}"),
                            T([128, (2 if kind == "ssm" else H) * NK], BF16, f"ktok{i}"), T([128, H, PV if kind != "ml" else 128], F32, f"vtok{i}"),
                            T([128, 16], F32, f"graw{i}"))
                lsets = [((qT, qTB), (kT, kTB), (ktok, ktokB), (vtok, vtokB), (graw, grawB)), mk_lset(1), mk_lset(2)]
                itc = [0]
                e3s = [(e3, e3B), T([128, 3, H], F32, "e3b")]
                scTs = [(scT, scTB), T([128, H, 128], BF16, "scTb")]
                gwks = [(gwk, gwkB), T([128, 4, H], F32, "gwkb")]
                vss = [(vs, vsB), T([128, H, PV], BF16, "vsb")]
                vws = [(vw, vwB), T([128, H, PV], BF16, "vwb")]
                if kind == "gdn":
                    Ttbs = [(Ttb, TtbB), T([128, H, 128], BF16, "Ttbb")]
                def issue_loads(t, d, tl):
                    (qT, qTB), (kT, kTB), (ktok, ktokB), (vtok, vtokB), (graw, grawB) = tl
                    rows = chunk_rows(t)
                    if kind == "ssm":
                        sp.dma(kT[:], convT_d[512:768, rows].rearrange("(g n) t -> n g t", g=2), reads=[convTB], writes=[kTB])
                        sp.dma(qT[:], convT_d[768:1024, rows].rearrange("(g n) t -> n g t", g=2), reads=[convTB], writes=[qTB])
                        pool.dma(ktok[:], ctok_d[rows, 512:768], reads=[ctokB], writes=[ktokB])
                        sp.dma(vtok[:].rearrange("p h v -> p (h v)"), ctok_d[rows, 0:512], reads=[ctokB], writes=[vtokB])
                        sp.dma(graw[:, 0:8], ptok_d[rows, 512 + 8 * d:520 + 8 * d], reads=[ptokB], writes=[grawB])
                    elif kind == "ml":
                        sp.dma(qT[:], convT_d[1024:1280, rows].rearrange("(h n) t -> n h t", h=4), reads=[convTB], writes=[qTB])
                        sp.dma(kT[:], convT_d[1280:1536, rows].rearrange("(h n) t -> n h t", h=4), reads=[convTB], writes=[kTB])
                        pool.dma(ktok[:], ctok_d[rows, 768:1024], reads=[ctokB], writes=[ktokB])
                        sp.dma(vtok[:].rearrange("p h v -> p (h v)"), ptok_d[rows, 528:1040], reads=[ptokB], writes=[vtokB])
                        sp.dma(graw[:, 0:8], ptok_d[rows, 1552 + 8 * d:1560 + 8 * d], reads=[ptokB], writes=[grawB])
                    else:
                        sp.dma(qT[:], convT_d[1536:2048, rows].rearrange("(h n) t -> n h t", h=4), reads=[convTB], writes=[qTB])
                        sp.dma(kT[:], convT_d[2048:2560, rows].rearrange("(h n) t -> n h t", h=4), reads=[convTB], writes=[kTB])
                        pool.dma(ktok[:], ctok_d[rows, 1024:1536], reads=[ctokB], writes=[ktokB])
                        sp.dma(vtok[:].rearrange("p h v -> p (h v)"), ctok_d[rows, 1536:2048], reads=[ctokB], writes=[vtokB])
                        sp.dma(graw[:, 0:4], ptok_d[rows, 2080 + 4 * d:2084 + 4 * d], reads=[ptokB], writes=[grawB])
                        sp.dma(graw[:, 4:8], ptok_d[rows, 2088 + 4 * d:2092 + 4 * d], reads=[ptokB], writes=[grawB])

                def chunk_gen(d, t, order):
                    p_ = itc[0] % 2
                    e3, e3B = e3s[p_]
                    scT, scTB = scTs[p_]
                    gwk, gwkB = gwks[p_]
                    vs, vsB = vss[p_]
                    vw, vwB = vws[p_]
                    if kind == "gdn":
                        Ttb, TtbB = Ttbs[p_]
                    rows = chunk_rows(t)
                    idx_ = order.index(t)
                    if idx_ == 0:
                        issue_loads(t, d, lsets[itc[0] % 3])
                    (qT, qTB), (kT, kTB), (ktok, ktokB), (vtok, vtokB), (graw, grawB) = lsets[itc[0] % 3]
                    if idx_ + 1 < len(order):
                        issue_loads(order[idx_ + 1], d, lsets[(itc[0] + 1) % 3])
                    itc[0] += 1
                    if kind == "ssm":
                        dve.op(lambda e: e.tensor_tensor(out=gwk[:, 2, :], in0=graw[:, 0:8], in1=par[:, 8 * d:8 * d + 8], op=ALU.add),
                               reads=[grawB, parB], writes=[gwkB])
                        act.op(lambda e: e.activation(out=gwk[:, 2, :], in_=gwk[:, 2, :], func=AF.Exp), reads=[gwkB], writes=[gwkB])
                        act.op(lambda e: e.activation(out=gwk[:, 0, :], in_=gwk[:, 2, :], func=AF.Ln, bias=1.0), reads=[gwkB], writes=[gwkB])
                        dve.op(lambda e: e.tensor_tensor(out=gwk[:, 1, :], in0=gwk[:, 0, :], in1=par[:, 16 + 8 * d:24 + 8 * d], op=ALU.mult),
                               reads=[gwkB, parB], writes=[gwkB])
                    elif kind == "ml":
                        dve.op(lambda e: e.tensor_tensor(out=gwk[:, 2:4, :], in0=graw[:, 0:8].rearrange("p (a h) -> p a h", a=2),
                                                         in1=par[:, 8 * d:8 * d + 8].rearrange("p (a h) -> p a h", a=2), op=ALU.add),
                               reads=[grawB, parB], writes=[gwkB])
                        act.op(lambda e: e.activation(out=gwk[:, 0, :], in_=gwk[:, 2, :], func=AF.Exp), reads=[gwkB], writes=[gwkB])
                        act.op(lambda e: e.activation(out=gwk[:, 3, :], in_=gwk[:, 3, :], func=AF.Exp, scale=-1.0), reads=[gwkB], writes=[gwkB])
                        act.op(lambda e: e.activation(out=gwk[:, 3, :], in_=gwk[:, 3, :], func=AF.Ln, bias=1.0), reads=[gwkB], writes=[gwkB])
                        act.op(lambda e: e.mul(out=gwk[:, 1, :], in_=gwk[:, 3, :], mul=-1.0), reads=[gwkB], writes=[gwkB])
                    else:
                        dve.op(lambda e: e.tensor_tensor(out=gwk[:, 2, :], in0=graw[:, 0:4], in1=par[:, 4 * d:4 * d + 4], op=ALU.add),
                               reads=[grawB, parB], writes=[gwkB])
                        act.op(lambda e: e.activation(out=gwk[:, 2, :], in_=gwk[:, 2, :], func=AF.Exp), reads=[gwkB], writes=[gwkB])
                        act.op(lambda e: e.activation(out=gwk[:, 2, :], in_=gwk[:, 2, :], func=AF.Ln, bias=1.0), reads=[gwkB], writes=[gwkB])
                        dve.op(lambda e: e.tensor_tensor(out=gwk[:, 1, :], in0=gwk[:, 2, :], in1=par[:, 16 + 4 * d:20 + 4 * d], op=ALU.mult),
                               reads=[gwkB, parB], writes=[gwkB])
                        act.op(lambda e: e.activation(out=gwk[:, 3, :], in_=graw[:, 4:8], func=AF.Sigmoid), reads=[grawB], writes=[gwkB])
                    pe.op(lambda e: e.matmul(sm[:, 0:H], lhsT=TRI[d][0][:], rhs=gwk[:, 1, :], start=True, stop=True),
                          reads=[TRI[d][1], gwkB], writes=[smB])
                    pe.op(lambda e: e.matmul(sm[:, H:2 * H], lhsT=TRIC[d][0][:], rhs=gwk[:, 1, :], start=True, stop=True),
                          reads=[TRIC[d][1], gwkB], writes=[smB])
                    pe.op(lambda e: e.matmul(sm[:, 2 * H:3 * H], lhsT=ones_f[:], rhs=gwk[:, 1, :], start=True, stop=True),
                          reads=[ones_fB, gwkB], writes=[smB])
                    act.op(lambda e: e.activation(out=e3[:].rearrange("p a h -> p (a h)"), in_=sm[:, 0:3 * H], func=AF.Exp), reads=[smB], writes=[e3B])
                    dve.op(lambda e: e.tensor_copy(out=gcs[:], in_=sm[:, 0:H]), reads=[smB], writes=[gcsB])
                    pe.op(lambda e: e.matmul(sm[0:H, 128:256], lhsT=gwk[:, 1, :], rhs=TRI[d][0][:], start=True, stop=True), reads=[gwkB, TRI[d][1]], writes=[smB])
                    dve.op(lambda e: e.tensor_copy(out=gcT[:], in_=sm[0:H, 128:256]), reads=[smB], writes=[gcTB])
                    act.op(lambda e: e.mul(out=ngcT[:], in_=sm[0:H, 128:256], mul=-1.0), reads=[smB], writes=[ngcTB])
                    dve.op(lambda e: e.tensor_tensor(out=gbd[:], in0=gcT[:].unsqueeze(1).to_broadcast([H, H, 128]), in1=ind[0:H, 0:H, :], op=ALU.mult),
                           reads=[gcTB, indB], writes=[gbdB])
                    for g in range(nG):
                        hs = slice(4 * g, 4 * g + 4)
                        gp, gpB = gd[g]
                        pe.op(lambda e: e.matmul(gp[:], lhsT=ones8[0:H, :], rhs=gbd[:, hs, :], start=True, stop=False), reads=[ones8B, gbdB], writes=[gpB])
                        pe.op(lambda e: e.matmul(gp[:], lhsT=ngcT[:], rhs=ind[0:H, hs, :], start=False, stop=False), reads=[ngcTB, indB], writes=[gpB])
                        pe.op(lambda e: e.matmul(gp[:], lhsT=identf[:], rhs=MASKT[d][0][:, 0:4, :], start=False, stop=True),
                              reads=[identfB, MASKT[d][1]], writes=[gpB])
                        act.op(lambda e: e.activation(out=decT[:, hs, :], in_=gp[:], func=AF.Exp), reads=[gpB], writes=[decTB])
                    if kind == "ssm":
                        for g in range(2):
                            pe.op(lambda e, g=g: e.matmul(cb[:, g, :], lhsT=kT[:, g, :], rhs=qT[:, g, :], start=True, stop=True), reads=[kTB, qTB], writes=[cbB])
                        for g in range(2):
                            dve.op(lambda e, g=g: e.tensor_tensor(out=scT[:, 4 * g:4 * g + 4, :], in0=decT[:, 4 * g:4 * g + 4, :],
                                                                  in1=cb[:, g:g + 1, :].to_broadcast([128, 4, 128]), op=ALU.mult),
                                   reads=[decTB, cbB], writes=[scTB])
                    else:
                        for hh in range(H):
                            pe.op(lambda e, hh=hh: e.matmul(cb[:, hh, :], lhsT=kT[:, hh, :], rhs=qT[:, hh, :], start=True, stop=True), reads=[kTB, qTB], writes=[cbB])
                        dve.op(lambda e: e.tensor_tensor(out=scT[:], in0=decT[:], in1=cb[:], op=ALU.mult), reads=[decTB, cbB], writes=[scTB])
                    if kind == "ssm":
                        dve.op(lambda e: e.tensor_tensor(out=vs[:], in0=vtok[:], in1=gwk[:, 0, :].unsqueeze(2).to_broadcast([128, H, PV]), op=ALU.mult),
                               reads=[vtokB, gwkB], writes=[vsB])
                    elif kind == "ml":
                        dve.op(lambda e: e.tensor_tensor(out=vs[:, :, 0:128], in0=vtok[:], in1=gwk[:, 0, :].unsqueeze(2).to_broadcast([128, H, 128]), op=ALU.mult),
                               reads=[vtokB, gwkB], writes=[vsB])
                        dve.op(lambda e: e.tensor_copy(out=vs[:, :, 128:129], in_=gwk[:, 0, :].unsqueeze(2)), reads=[gwkB], writes=[vsB])
                    else:
                        for hh in range(H):
                            pe.op(lambda e, hh=hh: e.matmul(yi[1][0][:, hh * 128:(hh + 1) * 128], lhsT=kT[:, hh, :], rhs=kT[:, hh, :], start=True, stop=True),
                                  reads=[kTB], writes=[yi[1][1]])
                        gp, gpB = gd[1]
                        pe.op(lambda e: e.matmul(gp[:], lhsT=gcT[:], rhs=ind[0:H, 0:4, :], start=True, stop=False), reads=[gcTB, indB], writes=[gpB])
                        pe.op(lambda e: e.matmul(gp[:], lhsT=nones8[0:H, :], rhs=gbd[:], start=False, stop=False), reads=[nones8B, gbdB], writes=[gpB])
                        pe.op(lambda e: e.matmul(gp[:], lhsT=identf[:], rhs=MASKL[d][0][:, 0:4, :], start=False, stop=True),
                              reads=[identfB, MASKL[d][1]], writes=[gpB])
                        act.op(lambda e: e.activation(out=decL[:], in_=gp[:], func=AF.Exp), reads=[gpB], writes=[decLB])
                        dve.op(lambda e: e.tensor_tensor(out=decL[:], in0=decL[:], in1=yi[1][0][:].rearrange("p (h j) -> p h j", h=4), op=ALU.mult),
                               reads=[decLB, yi[1][1]], writes=[decLB])
                        dve.op(lambda e: e.tensor_tensor(out=L0[:], in0=decL[:], in1=gwk[:, 3, :].unsqueeze(2).to_broadcast([128, H, 128]), op=ALU.mult),
                               reads=[decLB, gwkB], writes=[L0B])
                        tpb, tpbB = yx[1]
                        tpv = tpb[:].bitcast(BF16).rearrange("p (h j) -> p h j", h=8)[:, 0:4, :]
                        for hh in range(H):
                            pe.op(lambda e, hh=hh: e.transpose(out=tpv[:, hh, :], in_=L0[:, hh, :], identity=ident[:]), reads=[L0B, identB], writes=[tpbB])
                        act.op(lambda e: e.copy(out=U0[:], in_=tpv), reads=[tpbB], writes=[U0B])
                        mlo, mup = (0, 1) if d == 0 else (1, 0)
                        mk = lambda a_, lev: msk[:, a_, lev, :].unsqueeze(1).to_broadcast([128, H, 128])
                        dve.op(lambda e: e.tensor_tensor(out=Mb[:], in0=L0[:], in1=mk(mlo, 0), op=ALU.mult), reads=[L0B, mskB], writes=[MbB])
                        dve.op(lambda e: e.tensor_tensor(out=Mtb[:], in0=U0[:], in1=mk(mup, 0), op=ALU.mult), reads=[U0B, mskB], writes=[MtbB])
                        dve.op(lambda e: e.tensor_tensor(out=Tb[:], in0=ident4[:], in1=Mb[:], op=ALU.subtract), reads=[ident4B, MbB], writes=[TbB])
                        dve.op(lambda e: e.tensor_tensor(out=Ttb[:], in0=ident4[:], in1=Mtb[:], op=ALU.subtract), reads=[ident4B, MtbB], writes=[TtbB])
                        pa, paB = yi[1]
                        pb_, pbB = yx[1]
                        pav = pa[:].rearrange("p (h j) -> p h j", h=4)
                        pbv = pb_[:].rearrange("p (h j) -> p h j", h=4)
                        for lev in range(1, 7):
                            last = lev == 6
                            dve.op(lambda e, lev=lev: e.tensor_tensor(out=Mb[:], in0=L0[:], in1=mk(mlo, lev), op=ALU.mult), reads=[L0B, mskB], writes=[MbB])
                            if not last:
                                dve.op(lambda e, lev=lev: e.tensor_tensor(out=Mtb[:], in0=U0[:], in1=mk(mup, lev), op=ALU.mult), reads=[U0B, mskB], writes=[MtbB])
                                for hh in range(H):
                                    pe.op(lambda e, hh=hh: e.matmul(pav[:, hh, :], lhsT=Mtb[:, hh, :], rhs=Tb[:, hh, :], start=True, stop=True), reads=[MtbB, TbB], writes=[paB])
                            for hh in range(H):
                                pe.op(lambda e, hh=hh: e.matmul(pbv[:, hh, :], lhsT=Mb[:, hh, :], rhs=Ttb[:, hh, :], start=True, stop=True), reads=[MbB, TtbB], writes=[pbB])
                            if not last:
                                act.op(lambda e: e.copy(out=Ab[:], in_=pav), reads=[paB], writes=[AbB])
                            dve.op(lambda e: e.tensor_copy(out=Atb[:], in_=pbv), reads=[pbB], writes=[AtbB])
                            if not last:
                                for hh in range(H):
                                    pe.op(lambda e, hh=hh: e.matmul(pav[:, hh, :], lhsT=Ttb[:, hh, :], rhs=Ab[:, hh, :], start=True, stop=True), reads=[TtbB, AbB], writes=[paB])
                            for hh in range(H):
                                pe.op(lambda e, hh=hh: e.matmul(pbv[:, hh, :], lhsT=Tb[:, hh, :], rhs=Atb[:, hh, :], start=True, stop=True), reads=[TbB, AtbB], writes=[pbB])
                            if not last:
                                dve.op(lambda e: e.tensor_tensor(out=Tb[:], in0=Tb[:], in1=pav, op=ALU.subtract), reads=[TbB, paB], writes=[TbB])
                            dve.op(lambda e: e.tensor_tensor(out=Ttb[:], in0=Ttb[:], in1=pbv, op=ALU.subtract), reads=[TtbB, pbB], writes=[TtbB])
                        Tt, TtB = Ttb, TtbB
                        yield
                        ks_, ksB = yi[1]
                        for hh in range(H):
                            pe.op(lambda e, hh=hh: e.matmul(ks_[:, hh * 128:(hh + 1) * 128], lhsT=kT[:, hh, :], rhs=Sb[:, hh, :], start=True, stop=True),
                                  reads=[kTB, SbB], writes=[ksB])
                        dve.op(lambda e: e.tensor_tensor(out=ytmp[:], in0=ks_[:].rearrange("p (h v) -> p h v", h=4),
                                                         in1=e3[:, 0, :].unsqueeze(2).to_broadcast([128, H, 128]), op=ALU.mult), reads=[ksB, e3B], writes=[ytmpB])
                        dve.op(lambda e: e.tensor_tensor(out=ytmp[:], in0=vtok[:], in1=ytmp[:], op=ALU.subtract), reads=[vtokB, ytmpB], writes=[ytmpB])
                        dve.op(lambda e: e.tensor_tensor(out=rhsb[:], in0=ytmp[:], in1=gwk[:, 3, :].unsqueeze(2).to_broadcast([128, H, 128]), op=ALU.mult),
                               reads=[ytmpB, gwkB], writes=[rhsbB])
                        for hh in range(H):
                            pe.op(lambda e, hh=hh: e.matmul(ks_[:, hh * 128:(hh + 1) * 128], lhsT=Tt[:, hh, :], rhs=rhsb[:, hh, :], start=True, stop=True),
                                  reads=[TtB, rhsbB], writes=[ksB])
                        act.op(lambda e: e.copy(out=vs[:], in_=ks_[:].rearrange("p (h v) -> p h v", h=4)), reads=[ksB], writes=[vsB])
                    dve.op(lambda e: e.tensor_tensor(out=vw[:], in0=vs[:], in1=e3[:, 1, :].unsqueeze(2).to_broadcast([128, H, PV]), op=ALU.mult),
                           reads=[vsB, e3B], writes=[vwB])
                    if kind != "gdn":
                        yield
                    for hh in range(H):
                        yb, ybB = ybank(yi, hh)
                        xb, xbB = ybank(yx, hh)
                        kq = (hh // 4) if kind == "ssm" else hh
                        pe.op(lambda e, hh=hh, yb=yb: e.matmul(yb[:, ycol(hh):ycol(hh) + PV], lhsT=scT[:, hh, :], rhs=vs[:, hh, :], start=True, stop=True),
                              reads=[scTB, vsB], writes=[ybB])
                        pe.op(lambda e, hh=hh, xb=xb, kq=kq: e.matmul(xb[:, ycol(hh):ycol(hh) + PV], lhsT=qT[:, kq, :], rhs=Sb[:, hh, :], start=True, stop=True),
                              reads=[qTB, SbB], writes=[xbB])
                    nb_ = (H + hpb - 1) // hpb
                    for b_ in range(nb_):
                        hs = slice(b_ * hpb, min(H, (b_ + 1) * hpb))
                        nh = hs.stop - hs.start
                        xv = yx[b_][0][:, 0:nh * PV].rearrange("p (h v) -> p h v", h=nh)
                        yv = yi[b_][0][:, 0:nh * PV].rearrange("p (h v) -> p h v", h=nh)
                        dve.op(lambda e: e.tensor_tensor(out=ytmp[:, hs, :], in0=xv, in1=e3[:, 0, hs].unsqueeze(2).to_broadcast([128, nh, PV]), op=ALU.mult),
                               reads=[yx[b_][1], e3B], writes=[ytmpB])
                        dve.op(lambda e: e.tensor_tensor(out=y[:, hs, :], in0=yv, in1=ytmp[:, hs, :], op=ALU.add), reads=[yi[b_][1], ytmpB], writes=[yB_])
                    nsb = (HP + 511) // 512
                    for hh in range(H):
                        sb_, sbB_ = gd[hh // hpb]
                        kq = (hh // 4) if kind == "ssm" else hh
                        pe.op(lambda e, hh=hh, sb_=sb_, kq=kq: e.matmul(sb_[0:NK].rearrange("p a b -> p (a b)")[:, ycol(hh):ycol(hh) + PV],
                                                                         lhsT=ktok[:, kq * NK:(kq + 1) * NK], rhs=vw[:, hh, :], start=True, stop=True),
                              reads=[ktokB, vwB], writes=[sbB_])
                    dve.op(lambda e: e.tensor_tensor(out=S[:], in0=S[:], in1=e3[0:NK, 2, :].unsqueeze(2).to_broadcast([NK, H, PV]), op=ALU.mult),
                           reads=[SB, e3B], writes=[SB])
                    for b_ in range(nb_):
                        hs = slice(b_ * hpb, min(H, (b_ + 1) * hpb))
                        nh = hs.stop - hs.start
                        sv = gd[b_][0][0:NK].rearrange("p a b -> p (a b)")[:, 0:nh * PV].rearrange("p (h v) -> p h v", h=nh)
                        dve.op(lambda e: e.tensor_tensor(out=S[:, hs, :], in0=S[:, hs, :], in1=sv, op=ALU.add), reads=[SB, gd[b_][1]], writes=[SB])
                    act.op(lambda e: e.copy(out=Sb[:], in_=S[:]), reads=[SB], writes=[SbB])
                    if kind == "ml":
                        act.op(lambda e: e.activation(out=st4[:, 0:4], in_=y[:, :, 128], func=AF.Abs), reads=[yB_], writes=[st4B])
                        dve.op(lambda e: e.tensor_scalar(out=st4[:, 0:4], in0=st4[:, 0:4], scalar1=1.0, scalar2=None, op0=ALU.max), reads=[st4B], writes=[st4B])
                        dve.op(lambda e: e.reciprocal(out=st4[:, 0:4], in_=st4[:, 0:4]), reads=[st4B], writes=[st4B])
                        dve.op(lambda e: e.tensor_tensor(out=yo[:].rearrange("p (h v) -> p h v", h=4), in0=y[:, :, 0:128],
                                                         in1=st4[:, 0:4].unsqueeze(2).to_broadcast([128, 4, 128]), op=ALU.mult), reads=[yB_, st4B], writes=[yoB])
                        ycur, ycurB = yo, yoB
                        yflat = yo[:]
                    else:
                        ycur, ycurB = y, yB_
                        yflat = y[:].rearrange("p h v -> p (h v)")
                    if d == 0:
                        sp.dma(yf_d[rows, :], yflat, reads=[ycurB], writes=[yfB])
                        return
                    sp.dma(yfl[:], yf_d[rows, :], reads=[yfB], writes=[yflB])
                    dve.op(lambda e: e.tensor_tensor(out=yfl[:], in0=yfl[:], in1=yflat, op=ALU.add), reads=[yflB, ycurB], writes=[yflB])
                    if kind == "ssm":
                        sp.dma(zt[:], ptok_d[rows, 0:512], reads=[ptokB], writes=[ztB])
                        dve.op(lambda e: e.tensor_tensor(out=ytmp[:], in0=vtok[:], in1=par[:, 32:40].unsqueeze(2).to_broadcast([128, 8, 64]), op=ALU.mult),
                               reads=[vtokB, parB], writes=[ytmpB])
                        dve.op(lambda e: e.tensor_tensor(out=yfl[:], in0=yfl[:], in1=ytmp[:].rearrange("p h v -> p (h v)"), op=ALU.add), reads=[yflB, ytmpB], writes=[yflB])
                        act.op(lambda e: e.activation(out=zt[:], in_=zt[:], func=AF.Silu), reads=[ztB], writes=[ztB])
                        dve.op(lambda e: e.tensor_tensor(out=yfl[:], in0=yfl[:], in1=zt[:], op=ALU.mult), reads=[yflB, ztB], writes=[yflB])
                        act.op(lambda e: e.activation(out=junk[:], in_=yfl[:], func=AF.Square, accum_out=st4[:, 4:5]), reads=[yflB], writes=[junkB, st4B])
                        rstd_inplace(st4, st4B, 1.0 / 512, st4[:, 4:5])
                        dve.op(lambda e: e.scalar_tensor_tensor(out=yo[:], in0=yfl[:], scalar=st4[:, 4:5], in1=nrm[:], op0=ALU.mult, op1=ALU.mult),
                               reads=[yflB, st4B, nrmB], writes=[yoB])
                    else:
                        zc = (1040, 1552) if kind == "ml" else (1568, 2080)
                        sp.dma(zt[:], ptok_d[rows, zc[0]:zc[1]], reads=[ptokB], writes=[ztB])
                        act.op(lambda e: e.activation(out=zt[:], in_=zt[:], func=AF.Sigmoid if kind == "ml" else AF.Silu), reads=[ztB], writes=[ztB])
                        dve.op(lambda e: e.tensor_tensor(out=junk[:], in0=yfl[:], in1=yfl[:], op=ALU.mult), reads=[yflB], writes=[junkB])
                        dve.op(lambda e: e.tensor_reduce(out=st4[:, 4:8], in_=junk[:].rearrange("p (h v) -> p h v", h=4), axis=AX.X, op=ALU.add),
                               reads=[junkB], writes=[st4B])
                        rstd_inplace(st4, st4B, 1.0 / 128, st4[:, 4:8])
                        dve.op(lambda e: e.tensor_tensor(out=junk[:].rearrange("p (h v) -> p h v", h=4), in0=yfl[:].rearrange("p (h v) -> p h v", h=4),
                                                         in1=st4[:, 4:8].unsqueeze(2).to_broadcast([128, 4, 128]), op=ALU.mult), reads=[yflB, st4B], writes=[junkB])
                        dve.op(lambda e: e.tensor_tensor(out=junk[:], in0=junk[:], in1=nrm[:], op=ALU.mult), reads=[junkB, nrmB], writes=[junkB])
                        dve.op(lambda e: e.tensor_tensor(out=yo[:], in0=junk[:], in1=zt[:], op=ALU.mult), reads=[junkB, ztB, yflB], writes=[yoB])
                    sp.dma(ys_d[br, rows, :], yo[:], reads=[yoB], writes=[ysB[br]])

                for d in range(2):
                    order = [0, 1] + list(range(2, 34)) if d == 0 else [1, 0] + list(range(33, 1, -1))
                    dve.op(lambda e: e.memset(S[:], 0.0), writes=[SB])
                    dve.op(lambda e: e.memset(Sb[:], 0.0), writes=[SbB])
                    prev = None
                    for t in order:
                        g_ = chunk_gen(d, t, order)
                        next(g_)
                        if prev is not None:
                            for _ in prev:
                                pass
                        prev = g_
                    for _ in prev:
                        pass
            c.barrier()
        def s5_phase(l):
            CH = 256
            chunks = [(i * CH, (i + 1) * CH) for i in range(NTOK // CH)]
            with ExitStack() as ph:
                P0 = Pool_(nc, ph)
                gT = [P0.sb([128, NTOK], BF16, f"gT{q}") for q in range(4)]
                with ExitStack() as ph2:
                    P = Pool_(nc, ph2)
                    T = lambda shape, dt=F32, n=None: P.sb(shape, dt, n)
                    uf, ufB = T([128, NTOK], F32, "uf")
                    uTq, uTqB = T([128, NTOK], BF16, "uTq")
                    ya, yaB = T([128, NTOK], F32, "yacc")
                    dsk, dskB = T([128, 4], F32, "dsk")
                    prs = [T([128, 24, 16], F32, "s5par")] * 2
                    bre, breB = T([128, 16, 16], F32, "bre")
                    bim, bimB = T([128, 16, 16], F32, "bim")
                    cre, creB = T([128, 16, 16], F32, "cre")
                    cim, cimB = T([128, 16, 16], F32, "cim")
                    bbr, bbrB = T([128, 16, 16], F32, "bbr")
                    bbi, bbiB = T([128, 16, 16], F32, "bbi")
                    tmpb, tmpbB = T([128, 16, 16], F32, "tmpb")
                    pad, padB = T([128, 16, 128], F32, "pad")
                    WB = [[T([128, 16, 128], BF16, f"WB{r}") for r in range(2)]] * 2
                    (WBr, WBrB), (WBi, WBiB) = WB[0]
                    WCr, WCrB = T([128, 16, 128], BF16, "WCr")
                    WCi, WCiB = T([128, 16, 128], BF16, "WCi")
                    E = [[T([128, 16, CH], F32, f"E{r}") for r in range(2)]] * 2
                    tab1, tab1B = T([128, 16, CH // 2], F32, "tab1")
                    tab2, tab2B = T([128, 16, CH // 2], F32, "tab2")
                    w1s = [[T([128, CH], F32, f"w1_{j}_{i}") for i in range(4)] for j in range(2)]
                    w1 = w1s[0]
                    grs = [T([128, CH], F32, f"gr{j}") for j in range(2)]
                    gis = [T([128, CH], F32, f"gi{j}") for j in range(2)]
                    hcIB = Buf("hcI")
                    hcCB = [Buf(f"hcC{j}") for j in range(4)]
                    hr = [T([128, CH], BF16, f"hr{i}") for i in range(2)]
                    hi = [T([128, CH], BF16, f"hi{i}") for i in range(2)]
                    hc, hcB = T([128, 5, 16], F32, "hc")
                    bps = [P.ps([128, 512], F32, f"bps{i}") for i in range(4)]
                    yps = [P.ps([128, 512], F32, f"yps{i}") for i in range(2)]
                    tpp_ = [P.ps([128, 4, 128], F32, f"s5tp{i}") for i in range(2)]
                    sp.dma(bre[:], AP(s5_b_re.tensor, l * 32768, [[16, 128], [2048, 16], [1, 16]]), writes=[breB])
                    sp.dma(bim[:], AP(s5_b_im.tensor, l * 32768, [[16, 128], [2048, 16], [1, 16]]), writes=[bimB])
                    for gg in range(2):
                        for s_ in range(16):
                            off = l * 32768 + (2 * s_ + gg) * 1024
                            sp.dma(cre[gg * 64:(gg + 1) * 64, s_, :], AP(s5_c_re.tensor, off, [[1, 64], [64, 16]]), writes=[creB], allow_slow_non_contiguous=True)
                            act.dma(cim[gg * 64:(gg + 1) * 64, s_, :], AP(s5_c_im.tensor, off, [[1, 64], [64, 16]]), writes=[cimB], allow_slow_non_contiguous=True)
                    for q in range(4):
                        sp.dma(dsk[:, q:q + 1], AP(s5_d.tensor, l * 512 + q * 128, [[1, 128], [1, 1]]), writes=[dskB])

                    def fill_pad(src, srcB, neg=False):
                        dve.op(lambda e: e.memset(pad[:], 0.0), writes=[padB])
                        for gg in range(2):
                            for r in range(4):
                                dst = pad[gg * 64:(gg + 1) * 64, r::4, 32 * r + 16 * gg:32 * r + 16 * gg + 16]
                                s_ = src[gg * 64:(gg + 1) * 64, r::4, :]
                                if neg:
                                    dve.op(lambda e, dst=dst, s_=s_: e.tensor_scalar(out=dst, in0=s_, scalar1=-1.0, scalar2=None, op0=ALU.mult), reads=[srcB], writes=[padB])
                                else:
                                    dve.op(lambda e, dst=dst, s_=s_: e.tensor_copy(out=dst, in_=s_), reads=[srcB], writes=[padB])

                    fill_pad(cre, creB)
                    dve.op(lambda e: e.tensor_copy(out=WCr[:], in_=pad[:]), reads=[padB], writes=[WCrB])
                    fill_pad(cim, cimB, neg=True)
                    dve.op(lambda e: e.tensor_copy(out=WCi[:], in_=pad[:]), reads=[padB], writes=[WCiB])

                    for d in range(2):
                        pr, prB = prs[d]
                        pv = lambda i, pr=pr: pr[:, i, :]
                        TS = lambda out, in0, s1, s2, op0, op1=None, prB=prB: dve.op(
                            (lambda e: e.tensor_scalar(out=out, in0=in0, scalar1=s1, scalar2=s2, op0=op0, op1=op1)) if op1 is not None else
                            (lambda e: e.tensor_scalar(out=out, in0=in0, scalar1=s1, scalar2=None, op0=op0)), reads=[prB], writes=[prB])
                        TT = lambda out, a_, b_, op, prB=prB: dve.op(lambda e: e.tensor_tensor(out=out, in0=a_, in1=b_, op=op), reads=[prB], writes=[prB])
                        sp.dma(pv(0), AP(s5_a_re.tensor, (l * 2 + d) * 2048, [[1, 128], [128, 16]]), writes=[prB], allow_slow_non_contiguous=True)
                        sp.dma(pv(1), AP(s5_a_im.tensor, (l * 2 + d) * 2048, [[1, 128], [128, 16]]), writes=[prB], allow_slow_non_contiguous=True)
                        for gg in range(2):
                            sp.dma(pr[gg * 64:(gg + 1) * 64, 2, :], AP(s5_log_dt.tensor, (l * 2 + d) * 32 + gg, [[0, 64], [2, 16]]), writes=[prB],
                                   allow_slow_non_contiguous=True)
                        TS(pv(0), pv(0), -1e-4, None, ALU.min)
                        act.op(lambda e: e.activation(out=pv(2), in_=pv(2), func=AF.Exp), reads=[prB], writes=[prB])
                        TT(pv(3), pv(0), pv(2), ALU.mult)
                        TT(pv(4), pv(1), pv(2), ALU.mult)
                        act.op(lambda e: e.activation(out=pv(5), in_=pv(3), func=AF.Exp), reads=[prB], writes=[prB])
                        TS(pv(8), pv(4), 1.0 / 64, None, ALU.mult)
                        TT(pv(9), pv(8), pv(8), ALU.mult)
                        TS(pv(10), pv(9), -1.0 / 42, 1.0, ALU.mult, ALU.add)
                        TT(pv(10), pv(10), pv(9), ALU.mult)
                        TS(pv(10), pv(10), -1.0 / 20, 1.0, ALU.mult, ALU.add)
                        TT(pv(10), pv(10), pv(9), ALU.mult)
                        TS(pv(10), pv(10), -1.0 / 6, 1.0, ALU.mult, ALU.add)
                        TT(pv(7), pv(10), pv(8), ALU.mult)
                        TS(pv(10), pv(9), -1.0 / 56, 1.0, ALU.mult, ALU.add)
                        TT(pv(10), pv(10), pv(9), ALU.mult)
                        TS(pv(10), pv(10), -1.0 / 30, 1.0, ALU.mult, ALU.add)
                        TT(pv(10), pv(10), pv(9), ALU.mult)
                        TS(pv(10), pv(10), -1.0 / 12, 1.0, ALU.mult, ALU.add)
                        TT(pv(10), pv(10), pv(9), ALU.mult)
                        TS(pv(6), pv(10), -0.5, 1.0, ALU.mult, ALU.add)

                        def csq(ci, si, co, so):
                            TT(pv(11), pv(ci), pv(ci), ALU.mult)
                            TT(pv(12), pv(si), pv(si), ALU.mult)
                            TT(pv(13), pv(ci), pv(si), ALU.mult)
                            TT(pv(co), pv(11), pv(12), ALU.subtract)
                            TS(pv(so), pv(13), 2.0, None, ALU.mult)
                        for _ in range(6):
                            csq(6, 7, 6, 7)
                        dve.op(lambda e: e.tensor_copy(out=pr[:, 18:20, :], in_=pr[:, 6:8, :]), reads=[prB], writes=[prB])
                        TT(pv(14), pv(5), pv(6), ALU.mult)
                        TT(pv(15), pv(5), pv(7), ALU.mult)
                        TS(pv(8), pv(14), -1.0, None, ALU.add)
                        TT(pv(9), pv(0), pv(0), ALU.mult)
                        TT(pv(10), pv(1), pv(1), ALU.mult)
                        TT(pv(9), pv(9), pv(10), ALU.add)
                        dve.op(lambda e: e.reciprocal(out=pv(9), in_=pv(9)), reads=[prB], writes=[prB])
                        TT(pv(10), pv(8), pv(0), ALU.mult)
                        TT(pv(11), pv(15), pv(1), ALU.mult)
                        TT(pv(10), pv(10), pv(11), ALU.add)
                        TT(pv(16), pv(10), pv(9), ALU.mult)
                        TT(pv(10), pv(15), pv(0), ALU.mult)
                        TT(pv(11), pv(8), pv(1), ALU.mult)
                        TT(pv(10), pv(10), pv(11), ALU.subtract)
                        TT(pv(17), pv(10), pv(9), ALU.mult)
                        fb = lambda i: pr[:, i, :].unsqueeze(2).to_broadcast([128, 16, 16])
                        dve.op(lambda e: e.tensor_tensor(out=bbr[:], in0=bre[:], in1=fb(16), op=ALU.mult), reads=[breB, prB], writes=[bbrB])
                        dve.op(lambda e: e.tensor_tensor(out=tmpb[:], in0=bim[:], in1=fb(17), op=ALU.mult), reads=[bimB, prB], writes=[tmpbB])
                        dve.op(lambda e: e.tensor_tensor(out=bbr[:], in0=bbr[:], in1=tmpb[:], op=ALU.subtract), reads=[bbrB, tmpbB], writes=[bbrB])
                        dve.op(lambda e: e.tensor_tensor(out=bbi[:], in0=bim[:], in1=fb(16), op=ALU.mult), reads=[bimB, prB], writes=[bbiB])
                        dve.op(lambda e: e.tensor_tensor(out=tmpb[:], in0=bre[:], in1=fb(17), op=ALU.mult), reads=[breB, prB], writes=[tmpbB])
                        dve.op(lambda e: e.tensor_tensor(out=bbi[:], in0=bbi[:], in1=tmpb[:], op=ALU.add), reads=[bbiB, tmpbB], writes=[bbiB])
                        for (src, srcB, (dstW, dstWB)) in ((bbr, bbrB, WB[d][0]), (bbi, bbiB, WB[d][1])):
                            fill_pad(src, srcB)
                            for s4 in range(4):
                                tp_, tpB_ = tpp_[s4 % 2]
                                for j in range(4):
                                    pe.op(lambda e, j=j: e.transpose(out=tp_[:, j, :], in_=pad[:, s4 * 4 + j, :], identity=identf[:]), reads=[padB, identfB], writes=[tpB_])
                                act.op(lambda e: e.copy(out=dstW[:, s4 * 4:s4 * 4 + 4, :], in_=tp_[:]), reads=[tpB_], writes=[dstWB])
                        (Er, ErB), (Ei, EiB) = E[d]
                        dve.op(lambda e: e.memset(Er[:, :, 0:1], 1.0), writes=[ErB])
                        dve.op(lambda e: e.memset(Ei[:, :, 0:1], 0.0), writes=[EiB])
                        k = 0
                        while (1 << k) < CH:
                            n = 1 << k
                            cb_ = pr[:, 6, :].unsqueeze(2).to_broadcast([128, 16, n])
                            sb_ = pr[:, 7, :].unsqueeze(2).to_broadcast([128, 16, n])
                            t1 = tab1[:, :, 0:n]
                            t2 = tab2[:, :, 0:n]
                            dve.op(lambda e: e.tensor_tensor(out=t1, in0=Er[:, :, 0:n], in1=cb_, op=ALU.mult), reads=[ErB, prB], writes=[tab1B])
                            dve.op(lambda e: e.tensor_tensor(out=t2, in0=Ei[:, :, 0:n], in1=sb_, op=ALU.mult), reads=[EiB, prB], writes=[tab2B])
                            dve.op(lambda e: e.tensor_tensor(out=Er[:, :, n:2 * n], in0=t1, in1=t2, op=ALU.subtract), reads=[tab1B, tab2B], writes=[ErB])
                            dve.op(lambda e: e.tensor_tensor(out=t1, in0=Er[:, :, 0:n], in1=sb_, op=ALU.mult), reads=[ErB, prB], writes=[tab1B])
                            dve.op(lambda e: e.tensor_tensor(out=t2, in0=Ei[:, :, 0:n], in1=cb_, op=ALU.mult), reads=[EiB, prB], writes=[tab2B])
                            dve.op(lambda e: e.tensor_tensor(out=Ei[:, :, n:2 * n], in0=t1, in1=t2, op=ALU.add), reads=[tab1B, tab2B], writes=[EiB])
                            k += 1
                            if (1 << k) < CH:
                                csq(6, 7, 6, 7)

                        for q in range(4):
                            sp.dma(uf[:], projU_d[q * 128:(q + 1) * 128, :], reads=[projB], writes=[ufB])
                            if d == 0:
                                dve.op(lambda e: e.tensor_copy(out=uTq[:], in_=uf[:]), reads=[ufB], writes=[uTqB])
                                dve.op(lambda e: e.tensor_scalar(out=ya[:], in0=uf[:], scalar1=dsk[:, q:q + 1], scalar2=None, op0=ALU.mult), reads=[ufB, dskB], writes=[yaB])
                            else:
                                for (s0, s1) in SEGS:
                                    rev = AP(uf[:].tensor, uf[:, s1 - 1:s1].offset, [list(uf[:].ap[0]), [-1, s1 - s0]])
                                    dve.op(lambda e, rev=rev: e.tensor_copy(out=uTq[:, s0:s1], in_=rev), reads=[ufB], writes=[uTqB])
                                sp.dma(ya[:], s5acc_d[q * 128:(q + 1) * 128, :], reads=[s5accB], writes=[yaB])
                            nacc = 0
                            if True:
                                usrc, usrcB = uTq, uTqB
                                dve.op(lambda e: e.memset(hc[:], 0.0), writes=[hcB, hcIB] + hcCB)
                                qs = slice(q * 4, q * 4 + 4)
                                for (c0, c1) in chunks:
                                    W = c1 - c0
                                    yp, ypB = yps[(c0 // CH) % 2]
                                    def s5_iter(s4, si):
                                        s = q * 4 + s4
                                        br_, brB_ = bps[2 * si]
                                        bi_, biB_ = bps[2 * si + 1]
                                        hr_, hrB_ = hr[si]
                                        hi_, hiB_ = hi[si]
                                        gr, grB = grs[si]
                                        gi, giB = gis[si]
                                        w1 = w1s[si]
                                        pe.op(lambda e: e.matmul(br_[:, 0:W], lhsT=WBr[:, s, :], rhs=usrc[:, c0:c1], start=True, stop=True), reads=[WBrB, usrcB], writes=[brB_])
                                        yield
                                        pe.op(lambda e: e.matmul(bi_[:, 0:W], lhsT=WBi[:, s, :], rhs=usrc[:, c0:c1], start=True, stop=True), reads=[WBiB, usrcB], writes=[biB_])
                                        yield
                                        er, ei = Er[:, s, 0:W], Ei[:, s, 0:W]
                                        a0, a1, a2, a3 = [w1[i][0][:, 0:W] for i in range(4)]
                                        aB = [w1[i][1] for i in range(4)]
                                        dve.op(lambda e: e.tensor_tensor(out=a0, in0=br_[:, 0:W], in1=er, op=ALU.mult), reads=[brB_, ErB], writes=[aB[0]])
                                        yield
                                        dve.op(lambda e: e.tensor_tensor(out=a1, in0=bi_[:, 0:W], in1=ei, op=ALU.mult), reads=[biB_, EiB], writes=[aB[1]])
                                        yield
                                        pool.op(lambda e: e.tensor_tensor(out=a0, in0=a0, in1=a1, op=ALU.add), reads=[aB[0], aB[1]], writes=[aB[0]])
                                        yield
                                        dve.op(lambda e: e.tensor_tensor(out=a2, in0=bi_[:, 0:W], in1=er, op=ALU.mult), reads=[biB_, ErB], writes=[aB[2]])
                                        yield
                                        dve.op(lambda e: e.tensor_tensor(out=a3, in0=br_[:, 0:W], in1=ei, op=ALU.mult), reads=[brB_, EiB], writes=[aB[3]])
                                        yield
                                        pool.op(lambda e: e.tensor_tensor(out=a2, in0=a2, in1=a3, op=ALU.subtract), reads=[aB[2], aB[3]], writes=[aB[2]])
                                        yield
                                        mg = pr[:, 5, s:s + 1].to_broadcast([128, W])
                                        dve.op(lambda e: e.tensor_tensor_scan(out=gr[:, 0:W], data0=mg, data1=a0, initial=hc[:, 2, s:s + 1], op0=ALU.mult, op1=ALU.add),
                                               reads=[prB, aB[0], hcIB], writes=[grB])
                                        dve.op(lambda e: e.tensor_tensor_scan(out=gi[:, 0:W], data0=mg, data1=a2, initial=hc[:, 3, s:s + 1], op0=ALU.mult, op1=ALU.add),
                                               reads=[prB, aB[2], hcIB], writes=[giB])
                                        dve.op(lambda e: e.tensor_tensor(out=a0, in0=gr[:, 0:W], in1=er, op=ALU.mult), reads=[grB, ErB], writes=[aB[0]])
                                        yield
                                        dve.op(lambda e: e.tensor_tensor(out=a1, in0=gi[:, 0:W], in1=ei, op=ALU.mult), reads=[giB, EiB], writes=[aB[1]])
                                        yield
                                        dve.op(lambda e: e.tensor_tensor(out=a0, in0=a0, in1=a1, op=ALU.subtract), reads=[aB[0], aB[1]], writes=[aB[0]])
                                        yield
                                        dve.op(lambda e: e.tensor_tensor(out=a2, in0=gr[:, 0:W], in1=ei, op=ALU.mult), reads=[grB, EiB], writes=[aB[2]])
                                        yield
                                        dve.op(lambda e: e.tensor_tensor(out=a3, in0=gi[:, 0:W], in1=er, op=ALU.mult), reads=[giB, ErB], writes=[aB[3]])
                                        yield
                                        pool.op(lambda e: e.tensor_tensor(out=a2, in0=a2, in1=a3, op=ALU.add), reads=[aB[2], aB[3]], writes=[aB[2]])
                                        yield
                                        act.op(lambda e: e.copy(out=hr_[:, 0:W], in_=a0), reads=[aB[0]], writes=[hrB_])
                                        yield
                                        act.op(lambda e: e.copy(out=hi_[:, 0:W], in_=a2), reads=[aB[2]], writes=[hiB_])
                                        yield
                                        act.op(lambda e: e.copy(out=hc[:, 0, s:s + 1], in_=w1[0][0][:, W - 1:W]), reads=[aB[0]], writes=[hcCB[s4]])
                                        yield
                                        act.op(lambda e: e.copy(out=hc[:, 1, s:s + 1], in_=w1[2][0][:, W - 1:W]), reads=[aB[2]], writes=[hcCB[s4]])
                                        yield
                                        pe.op(lambda e: e.matmul(yp[:, 0:W], lhsT=WCr[:, s, :], rhs=hr_[:, 0:W], start=(s4 == 0), stop=False), reads=[WCrB, hrB_], writes=[ypB])
                                        yield
                                        pe.op(lambda e: e.matmul(yp[:, 0:W], lhsT=WCi[:, s, :], rhs=hi_[:, 0:W], start=False, stop=(s4 == 3)), reads=[WCiB, hiB_], writes=[ypB])
                                        yield

                                    for pair in range(2):
                                        alive = [s5_iter(pair * 2, 0), s5_iter(pair * 2 + 1, 1)]
                                        while alive:
                                            for g_ in list(alive):
                                                try:
                                                    next(g_)
                                                except StopIteration:
                                                    alive.remove(g_)
                                    if d == 0:
                                        dve.op(lambda e: e.tensor_tensor(out=ya[:, c0:c1], in0=ya[:, c0:c1], in1=yp[:, 0:W], op=ALU.add), reads=[yaB, ypB], writes=[yaB])
                                    else:
                                        seg0, seg1 = (0, NCTX) if c0 < NCTX else (NCTX, NTOK)
                                        o1 = seg1 - 1 - (c0 - seg0)
                                        o0 = o1 - W + 1
                                        a1_ = w1[1][0]
                                        act.op(lambda e: e.copy(out=a1_[:, 0:W], in_=yp[:, 0:W]), reads=[ypB], writes=[w1[1][1]])
                                        rev = AP(a1_[:].tensor, a1_[:, W - 1:W].offset, [list(a1_[:].ap[0]), [-1, W]])
                                        dve.op(lambda e, rev=rev: e.tensor_tensor(out=ya[:, o0:o1 + 1], in0=ya[:, o0:o1 + 1], in1=rev, op=ALU.add),
                                               reads=[yaB, w1[1][1]], writes=[yaB])
                                    h0, h1, i0, i1, tt = hc[:, 0, qs], hc[:, 1, qs], hc[:, 2, qs], hc[:, 3, qs], hc[:, 4, qs]
                                    uc, us = pr[:, 18, qs], pr[:, 19, qs]
                                    dve.op(lambda e: e.tensor_tensor(out=i0, in0=h0, in1=uc, op=ALU.mult), reads=[hcB, prB] + hcCB, writes=[hcB, hcIB])
                                    dve.op(lambda e: e.tensor_tensor(out=tt, in0=h1, in1=us, op=ALU.mult), reads=[hcB, prB] + hcCB, writes=[hcB, hcIB])
                                    dve.op(lambda e: e.tensor_tensor(out=i0, in0=i0, in1=tt, op=ALU.subtract), reads=[hcB] + hcCB, writes=[hcB, hcIB])
                                    dve.op(lambda e: e.tensor_tensor(out=i1, in0=h0, in1=us, op=ALU.mult), reads=[hcB, prB] + hcCB, writes=[hcB, hcIB])
                                    dve.op(lambda e: e.tensor_tensor(out=tt, in0=h1, in1=uc, op=ALU.mult), reads=[hcB, prB] + hcCB, writes=[hcB, hcIB])
                                    dve.op(lambda e: e.tensor_tensor(out=i1, in0=i1, in1=tt, op=ALU.add), reads=[hcB] + hcCB, writes=[hcB, hcIB])
                            if d == 0:
                                sp.dma(s5acc_d[q * 128:(q + 1) * 128, :], ya[:], reads=[yaB], writes=[s5accB])
                                continue
                            dve.op(lambda e: e.tensor_tensor(out=uf[:], in0=ya[:], in1=ya[:], op=ALU.mult), reads=[yaB], writes=[ufB])
                            dve.op(lambda e: e.tensor_scalar(out=uf[:], in0=uf[:], scalar1=0.044715, scalar2=1.0, op0=ALU.mult, op1=ALU.add), reads=[ufB], writes=[ufB])
                            dve.op(lambda e: e.tensor_tensor(out=uf[:], in0=uf[:], in1=ya[:], op=ALU.mult), reads=[ufB, yaB], writes=[ufB])
                            act.op(lambda e: e.activation(out=uf[:], in_=uf[:], func=AF.Sigmoid, scale=1.5957691216057308), reads=[ufB], writes=[ufB])
                            dve.op(lambda e: e.tensor_tensor(out=gT[q][0][:], in0=ya[:], in1=uf[:], op=ALU.mult), reads=[ufB, yaB], writes=[gT[q][1]])

                c.barrier()
                with ExitStack() as ph3:
                    P = Pool_(nc, ph3)
                    gw, gwB = P.sb([128, 4, 512], BF16, "gluw")
                    pool.dma(gw[:], s5_glu_w[l].rearrange("(k p) n -> p k n", p=128), writes=[gwB])
                    gbt, gbtB = P.sb([128, 4], F32, "glub")
                    for q in range(4):
                        sp.dma(gbt[:, q:q + 1], AP(s5_glu_b.tensor, l * 512 + q * 128, [[1, 128], [1, 1]]), writes=[gbtB])
                    sg = [P.sb([128, 512], F32, f"sg{i}") for i in range(2)]
                    ob = [P.sb([128, 512], BF16, f"ob{i}") for i in range(2)]
                    yps = [P.ps([128, 512], F32, f"gyps{i}") for i in range(2)]
                    nn = 0
                    for qo in range(4):
                        for (c0, c1) in [(0, NCTX)] + [(NCTX + i * 512, NCTX + (i + 1) * 512) for i in range(8)]:
                            W = c1 - c0
                            yp, ypB = yps[nn % 2]
                            a0_, a0B = sg[nn % 2]
                            o_, oB = ob[nn % 2]
                            nn += 1
                            for q in range(4):
                                pe.op(lambda e, q=q: e.matmul(yp[:, 0:W], lhsT=gw[:, q, qo * 128:(qo + 1) * 128], rhs=gT[q][0][:, c0:c1], start=(q == 0), stop=(q == 3)),
                                      reads=[gwB, gT[q][1]], writes=[ypB])
                            act.op(lambda e: e.activation(out=a0_[:, 0:W], in_=yp[:, 0:W], func=AF.Sigmoid, bias=gbt[:, qo:qo + 1]), reads=[ypB, gbtB], writes=[a0B])
                            dve.op(lambda e: e.tensor_tensor(out=o_[:, 0:W], in0=a0_[:, 0:W], in1=gT[qo][0][:, c0:c1], op=ALU.mult), reads=[a0B, gT[qo][1]], writes=[oB])
                            sp.dma(s5T_d[qo * 128:(qo + 1) * 128, c0:c1], o_[:, 0:W], reads=[oB], writes=[ysB[2]])
            c.barrier()

        def merge_phase(l, include_ctx):
            with ExitStack() as ph:
                P = Pool_(nc, ph)
                wbr, wbrB = P.sb([128, 16, D], BF16, "wbr")
                pool.dma(wbr[:], w_branch[l].rearrange("b (k p) n -> p (b k) n", p=128), writes=[wbrB])
                wo, woB = P.sb([128, 8, D], BF16, "wo")
                pool.dma(wo[:], w_out[l].rearrange("(k p) n -> p k n", p=128), writes=[woB])
                bct = alloc_bc(P)
                ysf = [P.sb([128, 3, 512], F32, f"ysf{i}") for i in range(2)]
                ysb, ysbB = P.sb([128, 3, 512], BF16, "ysb")
                ysT, ysTB = P.sb([128, 16, 128], BF16, "ysT")
                gsg = [P.sb([128, 4096], BF16, f"gsg{i}") for i in range(2)]
                mg, mgB = P.sb([128, D], F32, "mg")
                mt, mtB = P.sb([128, 512], F32, "mt")
                mb, mbB = P.sb([128, D], BF16, "mb")
                mT, mTB = P.sb([128, 8, 128], BF16, "mT")
                hts = [P.sb([128, D], F32, f"mh{i}") for i in range(2)]
                tps = [P.ps([128, 8, 128], BF16, f"mtp{i}") for i in range(2)]
                mps = [P.ps([128, 512], F32, f"mps{i}") for i in range(4)]
                cur_v = None
                nn = 0
                for v, tiles in groups_for(include_ctx):
                    if v != cur_v:
                        bc = fill_bc(bct, l, 1, v, 3, 4, 5, 1.0)
                        cur_v = v
                    gt = bc[2]
                    for t in tiles:
                        rows = slice(t * 128, (t + 1) * 128)
                        yf_, yfB_ = ysf[nn % 2]
                        gs_, gsB_ = gsg[nn % 2]
                        ht, htB = hts[nn % 2]
                        nn += 1
                        for bi_, br in enumerate((0, 1, 3)):
                            sp.dma(yf_[:, bi_, :], ys_d[br, rows, :], reads=[ysB[br]], writes=[yfB_])
                        sp.dma(gs_[:], gsig_d[rows, :], reads=[gsigB], writes=[gsB_])
                        pool.op(lambda e: e.tensor_copy(out=ysb[:], in_=yf_[:]), reads=[yfB_], writes=[ysbB])
                        for bi_, br in enumerate((0, 1, 3)):
                            tp, tpB = tps[bi_ % 2]
                            for kc in range(4):
                                pe.op(lambda e, kc=kc: e.transpose(out=tp[:, kc, :], in_=ysb[:, bi_, kc * 128:(kc + 1) * 128], identity=ident[:]),
                                      reads=[ysbB, identB], writes=[tpB])
                            act.op(lambda e: e.copy(out=ysT[:, br * 4:br * 4 + 4, :], in_=tp[:, 0:4, :]), reads=[tpB], writes=[ysTB])
                        for kc in range(4):
                            sp.dma(ysT[:, 8 + kc, :], s5T_d[kc * 128:(kc + 1) * 128, rows], reads=[ysB[2]], writes=[ysTB])
                        for br in range(4):
                            for n in range(2):
                                mp, mpB = mps[(br * 2 + n) % 4]
                                for kc in range(4):
                                    pe.op(lambda e, kc=kc: e.matmul(mp[:], lhsT=ysT[:, br * 4 + kc, :], rhs=wbr[:, br * 4 + kc, n * 512:(n + 1) * 512],
                                                                    start=(kc == 0), stop=(kc == 3)), reads=[ysTB, wbrB], writes=[mpB])
                                gsl = gs_[:, br * 1024 + n * 512:br * 1024 + (n + 1) * 512]
                                if br == 0:
                                    dve.op(lambda e: e.tensor_tensor(out=mg[:, n * 512:(n + 1) * 512], in0=mp[:], in1=gsl, op=ALU.mult), reads=[mpB, gsB_], writes=[mgB])
                                else:
                                    dve.op(lambda e: e.tensor_tensor(out=mt[:], in0=mp[:], in1=gsl, op=ALU.mult), reads=[mpB, gsB_], writes=[mtB])
                                    pool.op(lambda e: e.tensor_tensor(out=mg[:, n * 512:(n + 1) * 512], in0=mg[:, n * 512:(n + 1) * 512], in1=mt[:], op=ALU.add),
                                            reads=[mgB, mtB], writes=[mgB])
                        act.op(lambda e: e.copy(out=mb[:], in_=mg[:]), reads=[mgB], writes=[mbB])
                        tp, tpB = tps[0]
                        for k in range(8):
                            pe.op(lambda e, k=k: e.transpose(out=tp[:, k, :], in_=mb[:, k * 128:(k + 1) * 128], identity=ident[:]), reads=[mbB, identB], writes=[tpB])
                        act.op(lambda e: e.copy(out=mT[:], in_=tp[:]), reads=[tpB], writes=[mTB])
                        pcs, deps = h_pieces(l, t)
                        for (p0, n_, ap) in pcs:
                            sp.dma(ht[p0:p0 + n_, :], ap, reads=deps, writes=[htB])
                        for n in range(2):
                            mp, mpB = mps[n]
                            for k in range(8):
                                pe.op(lambda e, k=k: e.matmul(mp[:], lhsT=mT[:, k, :], rhs=wo[:, k, n * 512:(n + 1) * 512], start=(k == 0), stop=(k == 7)),
                                      reads=[mTB, woB], writes=[mpB])
                            dve.op(lambda e: e.tensor_tensor(out=mt[:], in0=mp[:], in1=gt[0][:, n * 512:(n + 1) * 512], op=ALU.mult), reads=[mpB, gt[1]], writes=[mtB])
                            pool.op(lambda e: e.tensor_tensor(out=ht[:, n * 512:(n + 1) * 512], in0=ht[:, n * 512:(n + 1) * 512], in1=mt[:], op=ALU.add),
                                    reads=[htB, mtB], writes=[htB])
                        for (p0, n_, ap) in pcs:
                            sp.dma(ap, ht[p0:p0 + n_, :], reads=[htB], writes=deps)
            c.barrier()

        def mixer_layer(l, need_ctx):
            proj_phase_a(l)
            proj_phase_b(l)
            conv_phase(l)
            for kind in cfg.get("kinds", ("ssm", "ml", "gdn")):
                gla_branch(l, kind)
            if cfg.get("s5", True):
                s5_phase(l)
            if cfg.get("merge", True):
                merge_phase(l, need_ctx)

        def final_phase():
            with ExitStack() as ph:
                P = Pool_(nc, ph)
                fw, fwB = P.sb([128, D], F32, "fw")
                sp.dma(fw[:], final_norm[0:1, :].partition_broadcast(128), writes=[fwB])
                hts = [P.sb([128, D], F32, f"fh{i}") for i in range(3)]
                junks = [P.sb([128, D], F32, f"fj{i}") for i in range(2)]
                sss = [P.sb([128, 1], F32, f"fs{i}") for i in range(2)]
                for i in range(SEQ // 128):
                    ht, htB = hts[i % 3]
                    junk, junkB = junks[i % 2]
                    ss, ssB = sss[i % 2]
                    t = 2 + i
                    sp.dma(ht[:], h_d[t * 128:(t + 1) * 128, :], reads=[hB[t]], writes=[htB])
                    act.op(lambda e: e.activation(out=junk[:], in_=ht[:], func=AF.Square, accum_out=ss[:]), reads=[htB], writes=[junkB, ssB])
                    rstd_inplace(ss, ssB, 1.0 / D)
                    dve.op(lambda e: e.scalar_tensor_tensor(out=junk[:], in0=ht[:], scalar=ss[:, 0:1], in1=fw[:], op0=ALU.mult, op1=ALU.mult),
                           reads=[htB, ssB, fwB], writes=[junkB])
                    sp.dma(out_d[i * 128:(i + 1) * 128, :], junk[:], reads=[junkB])
            c.barrier()

        for l in range(nlayers):
            need_ctx = l < DEPTH - 1
            if not cfg.get('skip_ffn'):
                ffn_phase(l, 0, True)
            if do_mix:
                mixer_layer(l, need_ctx)
            if not cfg.get('skip_ffn') and not cfg.get('skip_ffn2'):
                ffn_phase(l, 1, need_ctx)
        final_phase()
        if "ys" in dbg:
            for bi_, br in enumerate((0, 1, 3)):
                if ("ssm", "ml", "x", "gdn")[br] in cfg.get("kinds", ("ssm", "ml", "gdn")):
                    sp.dma(dbg["ys"][bi_ * NTOK:(bi_ + 1) * NTOK, :], ys_d[br, :, :], reads=[ysB[br]])
            c.barrier()
        if "s5dbg" in dbg:
            pool.dma(dbg["s5dbg"][:, :], s5T_d[:, :], reads=[ysB[2]])
            c.barrier()
        if "ptokdbg" in dbg:
            sp.dma(dbg["ptokdbg"][:, :], ptok_d[:, :], reads=[ptokB])
            c.barrier()
        if "hdbg" in dbg:
            sp.dma(dbg["hdbg"][:, :], h_d[:, :], reads=hB)
            c.barrier()
        if "mod" in dbg:
            sp.dma(dbg["mod"][:, :], mod_d[0, :, :], reads=[modB])
            c.barrier()
        print("ninstr", c.ninstr)
    return nc


_PARAM_NAMES = ["ada_w", "ada_b", "norm_w", "ffn_up", "ffn_down", "w_in", "ssm_conv_w", "ssm_conv_b", "ssm_d", "ssm_norm",
                "ml_conv_w", "ml_conv_b", "ml_norm", "s5_a_re", "s5_a_im", "s5_log_dt", "s5_b_re", "s5_b_im", "s5_c_re", "s5_c_im",
                "s5_d", "s5_glu_w", "s5_glu_b", "gdn_conv_w", "w_branch", "w_out"]
_RESHAPE = {"ssm_dt_bias": (DEPTH, 16), "ssm_a_log": (DEPTH, 16), "ml_gate_b": (DEPTH, 16), "gdn_dt_bias": (DEPTH, 8), "gdn_a_log": (DEPTH, 8),
            "gdn_norm": (1, DEPTH * 128), "gate_b": (DEPTH, 4096)}


def make_in_maps(inputs, ncores=4):
    f = lambda a: np.ascontiguousarray(np.asarray(a, dtype=np.float32))
    shared = {k: f(inputs[k]) for k in _PARAM_NAMES}
    shared["final_norm"] = f(inputs["final_norm"]).reshape(1, D)
    for k, shp in _RESHAPE.items():
        shared[k] = f(inputs[k]).reshape(shp)
    maps = []
    for b in range(ncores):
        m = dict(shared)
        m["x"] = f(inputs["x"][b])
        m["ctx"] = f(inputs["ctx"][b])
        m["cc"] = np.ascontiguousarray(np.stack([np.asarray(inputs["c"][b], np.float32), np.asarray(inputs["c_ctx"], np.float32)]))
        maps.append(m)
    return maps


def kernel(**inputs):
    nc = build()
    maps = make_in_maps(inputs, 4)
    res = run_bass_kernel_spmd(nc, maps, core_ids=list(range(4)))
    return np.stack([res.results[b]["out"] for b in range(4)]).astype(np.float32)
```

```python
import numpy as np
from contextlib import ExitStack
import concourse.bass as bass
import concourse.mybir as mybir
from concourse.bass_utils import run_bass_kernel_spmd
from concourse.ap import AP

F32 = mybir.dt.float32
BF16 = mybir.dt.bfloat16
ALU = mybir.AluOpType
AF = mybir.ActivationFunctionType
AX = mybir.AxisListType

D = 1024
SEQ = 4096
NCTX = 256
NTOK = SEQ + NCTX
DEPTH = 4
DFF = 2816
DIN = 9776
EPS = 1e-6
NB = 4


class Buf:
    __slots__ = ("name", "w", "r", "excl")

    def __init__(self, name="", excl=False):
        self.name = name
        self.w = None
        self.r = {}
        self.excl = excl


class Eng:
    def __init__(self, ctx, name, e, sem, is_pe=False):
        self.ctx, self.name, self.e, self.sem = ctx, name, e, sem
        self.count = 0
        self.known = {}
        self.is_pe = is_pe
        self.dsems = []
        self.dn = 0
        self.pending = None
        self.pkey = None

    def close(self):
        if self.pending is not None:
            self.pending.then_inc(self.sem, 1)
            self.count += 1
            self.pending = None
            self.pkey = None

    def _wait(self, ev):
        if ev is None:
            return
        sem, val = ev
        if self.is_pe and sem is self.sem:
            return
        if self.known.get(id(sem), 0) >= val:
            return
        pe = self.ctx.pe
        if sem is pe.sem and pe.pending is not None and val == pe.count + 1:
            pe.close()
        self.e.wait_ge(sem, val)
        self.known[id(sem)] = val
        self.ctx.ninstr += 1

    def _deps(self, reads, writes):
        for b in reads:
            self._wait(b.w)
            if b.excl:
                for ev in list(b.r.values()):
                    if ev[0] is not self.sem:
                        self._wait(ev)
        for b in writes:
            self._wait(b.w)
            for ev in list(b.r.values()):
                self._wait(ev)

    def _mark(self, ev, reads, writes):
        for b in reads:
            b.r[id(ev[0])] = ev
        for b in writes:
            b.w = ev
            b.r = {}

    def op(self, fn, reads=(), writes=()):
        if self.is_pe:
            key = tuple(id(b) for b in writes)
            if self.pending is not None and key != self.pkey:
                self.close()
            self._deps(reads, writes)
            if self.pending is not None and key != self.pkey:
                self.close()
            ins = fn(self.e)
            self.pending, self.pkey = ins, key
            ev = (self.sem, self.count + 1)
            self._mark(ev, reads, writes)
            self.ctx.ninstr += 1
            return ev
        self._deps(reads, writes)
        ins = fn(self.e)
        self.count += 1
        ins.then_inc(self.sem, 1)
        ev = (self.sem, self.count)
        self._mark(ev, reads, writes)
        self.ctx.ninstr += 1
        return ev

    def dma(self, out, in_, reads=(), writes=(), **kw):
        K = len(self.dsems)
        n = self.dn
        sem = self.dsems[n % K]
        if n >= K:
            self._wait((sem, 16 * (n // K)))
        self._deps(reads, writes)
        ins = self.e.dma_start(out=out, in_=in_, **kw)
        ins.then_inc(sem, 16)
        ev = (sem, 16 * (n // K + 1))
        self.dn += 1
        self._mark(ev, reads, writes)
        self.ctx.ninstr += 1
        return ev

    def dma_final_events(self):
        K = len(self.dsems)
        out = []
        for i in range(min(K, self.dn)):
            cnt = (self.dn - 1 - i) // K + 1
            out.append((self.dsems[i], 16 * cnt))
        return out


class Ctx:
    def __init__(self, nc, stack, ndsem=8):
        self.nc = nc
        self.ninstr = 0
        S = lambda n: stack.enter_context(nc.semaphore(n))
        self.pe = Eng(self, "pe", nc.tensor, S("s_pe"), is_pe=True)
        self.act = Eng(self, "act", nc.scalar, S("s_act"))
        self.dve = Eng(self, "dve", nc.vector, S("s_dve"))
        self.pool = Eng(self, "pool", nc.gpsimd, S("s_pool"))
        self.sp = Eng(self, "sp", nc.sync, S("s_sp"))
        self.engs = [self.pe, self.act, self.dve, self.pool, self.sp]
        self.dmaq = [self.sp, self.pool, self.act]
        for e in self.dmaq:
            e.dsems = [S(f"d_{e.name}{i}") for i in range(ndsem)]

    def barrier(self):
        self.pe.close()
        evs = []
        for q in self.dmaq:
            evs += q.dma_final_events()
        for o in self.engs:
            if o.count:
                evs.append((o.sem, o.count))
        for e in self.engs:
            for ev in evs:
                if ev[0] is e.sem:
                    continue
                e._wait(ev)


class Pool_:
    _uid = [0]

    def __init__(self, nc, stack):
        self.nc, self.stack = nc, stack
        Pool_._uid[0] += 1
        self.uid = Pool_._uid[0]
        self.n = 0

    def sb(self, shape, dt=F32, name=None):
        self.n += 1
        t = self.stack.enter_context(self.nc.sbuf_tensor(f"{name or 't'}_{self.n}_{self.uid}", list(shape), dt))
        return t, Buf(name or "t")

    def ps(self, shape, dt=F32, name=None):
        self.n += 1
        t = self.stack.enter_context(self.nc.psum_tensor(f"{name or 'p'}_{self.n}_{self.uid}", list(shape), dt))
        nbytes = int(np.prod(shape[1:])) * (4 if dt == F32 else 2)
        assert nbytes == 2048, (name, shape)
        return t, Buf(name or "p", excl=True)


def bcast_row(ap_row, n=128):
    return ap_row.partition_broadcast(n)


def build(cfg=None):
    cfg = cfg or {}
    nlayers = cfg.get("nlayers", DEPTH)
    do_mix = cfg.get("mix", True)
    nc = bass.Bass("TRN2", target_bir_lowering=False)
    dt_in = lambda name, shape: nc.dram_tensor(name, list(shape), F32, kind="ExternalInput").ap()
    x_d = dt_in("x", [SEQ, D])
    ctx_d = dt_in("ctx", [NCTX, D])
    cc_d = dt_in("cc", [2, D])
    ada_w = dt_in("ada_w", [DEPTH, D, 9 * D])
    ada_b = dt_in("ada_b", [DEPTH, 9 * D])
    norm_w = dt_in("norm_w", [DEPTH, 3, D])
    ffn_up = dt_in("ffn_up", [DEPTH, 2, D, 2 * DFF])
    ffn_down = dt_in("ffn_down", [DEPTH, 2, DFF, D])
    final_norm = dt_in("final_norm", [1, D])
    w_in = dt_in("w_in", [DEPTH, D, DIN])
    ssm_conv_w = dt_in("ssm_conv_w", [DEPTH, 5, 1024])
    ssm_conv_b = dt_in("ssm_conv_b", [DEPTH, 1024])
    ssm_dt_bias = dt_in("ssm_dt_bias", [DEPTH, 16])
    ssm_a_log = dt_in("ssm_a_log", [DEPTH, 16])
    ssm_d = dt_in("ssm_d", [DEPTH, 8])
    ssm_norm = dt_in("ssm_norm", [DEPTH, 512])
    ml_conv_w = dt_in("ml_conv_w", [DEPTH, 5, 512])
    ml_conv_b = dt_in("ml_conv_b", [DEPTH, 512])
    ml_gate_b = dt_in("ml_gate_b", [DEPTH, 16])
    ml_norm = dt_in("ml_norm", [DEPTH, 512])
    s5_a_re = dt_in("s5_a_re", [DEPTH, 2, 32, 64])
    s5_a_im = dt_in("s5_a_im", [DEPTH, 2, 32, 64])
    s5_log_dt = dt_in("s5_log_dt", [DEPTH, 2, 32])
    s5_b_re = dt_in("s5_b_re", [DEPTH, 32, 64, 16])
    s5_b_im = dt_in("s5_b_im", [DEPTH, 32, 64, 16])
    s5_c_re = dt_in("s5_c_re", [DEPTH, 32, 16, 64])
    s5_c_im = dt_in("s5_c_im", [DEPTH, 32, 16, 64])
    s5_d = dt_in("s5_d", [DEPTH, 512])
    s5_glu_w = dt_in("s5_glu_w", [DEPTH, 512, 512])
    s5_glu_b = dt_in("s5_glu_b", [DEPTH, 512])
    gdn_conv_w = dt_in("gdn_conv_w", [DEPTH, 5, 1536])
    gdn_dt_bias = dt_in("gdn_dt_bias", [DEPTH, 8])
    gdn_a_log = dt_in("gdn_a_log", [DEPTH, 8])
    gdn_norm = dt_in("gdn_norm", [1, DEPTH * 128])
    gate_b = dt_in("gate_b", [DEPTH, 4096])
    w_branch = dt_in("w_branch", [DEPTH, 4, 512, D])
    w_out = dt_in("w_out", [DEPTH, D, D])
    projT_d = nc.dram_tensor("projT_scr", [3072, NTOK], BF16, kind="Internal").ap()
    projU_d = nc.dram_tensor("projU_scr", [512, NTOK], F32, kind="Internal").ap()
    ptok_d = nc.dram_tensor("ptok_scr", [NTOK, 2096], F32, kind="Internal").ap()
    gsig_d = nc.dram_tensor("gsig_scr", [NTOK, 4096], BF16, kind="Internal").ap()
    convT_d = nc.dram_tensor("convT_scr", [3072, NTOK], BF16, kind="Internal").ap()
    ctok_d = nc.dram_tensor("ctok_scr", [NTOK, 2048], F32, kind="Internal").ap()
    yf_d = nc.dram_tensor("yf_scr", [NTOK, 512], F32, kind="Internal").ap()
    ys_d = nc.dram_tensor("ys_scr", [4, NTOK, 512], F32, kind="Internal").ap()
    s5T_d = nc.dram_tensor("s5T_scr", [512, NTOK], BF16, kind="Internal").ap()
    s5acc_d = nc.dram_tensor("s5acc_scr", [512, NTOK], F32, kind="Internal").ap()
    out_d = nc.dram_tensor("out", [SEQ, D], F32, kind="ExternalOutput").ap()
    h_d = nc.dram_tensor("h_scr", [NTOK, D], F32, kind="Internal").ap()
    mod_d = nc.dram_tensor("mod_scr", [DEPTH, 2, 9 * D], F32, kind="Internal").ap()
    dbg = {}
    for name, shape in cfg.get("dbg", {}).items():
        dbg[name] = nc.dram_tensor(name, list(shape), F32, kind="ExternalOutput").ap()

    hB = [Buf(f"h{i}") for i in range(NTOK // 128)]
    modB = Buf("mod")

    with ExitStack() as top:
        c = Ctx(nc, top)
        pe, act, dve, pool, sp = c.pe, c.act, c.dve, c.pool, c.sp

        cst = Pool_(nc, top)
        ident, identB = cst.sb([128, 128], BF16, "ident")
        identf, identfB = cst.sb([128, 128], F32, "identf")
        pool.op(lambda e: e.memset(identf[:], 0.0), writes=[identfB])
        pool.op(lambda e: e.affine_select(out=identf[:], in_=identf[:], pattern=[[-1, 128]], compare_op=ALU.not_equal,
                                          fill=1.0, base=0, channel_multiplier=1), reads=[identfB], writes=[identfB])
        dve.op(lambda e: e.tensor_copy(out=ident[:], in_=identf[:]), reads=[identfB], writes=[identB])

        with ExitStack() as ph:
            P = Pool_(nc, ph)
            sp.dma(h_d[0:NCTX, :], ctx_d[:, :], writes=hB[0:2])
            for i in range(8):
                sp.dma(h_d[NCTX + i * 512:NCTX + (i + 1) * 512, :], x_d[i * 512:(i + 1) * 512, :],
                       writes=hB[2 + 4 * i:2 + 4 * (i + 1)])
            ccT, ccTB = P.sb([128, 8, 2], F32, "ccT")
            for k in range(8):
                for r in range(2):
                    src = AP(cc_d.tensor, r * D + k * 128, [[1, 128], [1, 1]])
                    sp.dma(ccT[:, k, r:r + 1], src, writes=[ccTB])
            sT, sTB = P.sb([128, 8, 2], F32, "sT")
            act.op(lambda e: e.activation(out=sT[:], in_=ccT[:], func=AF.Silu), reads=[ccTB], writes=[sTB])
            wbuf = [P.sb([128, 8, 512], F32, f"adaw{i}") for i in range(2)]
            mps = [P.ps([2, 512], F32, f"mps{i}") for i in range(2)]
            n_it = 0
            row, rowB = P.sb([2, 9 * D], F32, "modrow")
            bia, biaB = P.sb([2, 9 * D], F32, "modb")
            for l in range(nlayers):
                sp.dma(bia[:], ada_b[l:l + 1, :].partition_broadcast(2), writes=[biaB])
                for nb in range(18):
                    wt, wB = wbuf[n_it % 2]
                    pt, pB = mps[n_it % 2]
                    n_it += 1
                    q = sp if nb % 2 == 0 else act
                    q.dma(wt[:], ada_w[l, :, nb * 512:(nb + 1) * 512].rearrange("(k p) n -> p k n", p=128), writes=[wB])
                    for k in range(8):
                        pe.op(lambda e, k=k: e.matmul(pt[:], lhsT=sT[:, k, :], rhs=wt[:, k, :], start=(k == 0), stop=(k == 7)),
                              reads=[sTB, wB], writes=[pB])
                    dve.op(lambda e: e.tensor_tensor(out=row[:, nb * 512:(nb + 1) * 512], in0=pt[:], in1=bia[:, nb * 512:(nb + 1) * 512],
                                                     op=ALU.add), reads=[pB, biaB], writes=[rowB])
                sp.dma(mod_d[l, :, :], row[:], reads=[rowB], writes=[modB])
        c.barrier()

        def rstd_inplace(ss, ssB, scale, sl=None):
            a = ss[:] if sl is None else sl
            dve.op(lambda e: e.tensor_scalar(out=a, in0=a, scalar1=scale, scalar2=EPS, op0=ALU.mult, op1=ALU.add), reads=[ssB], writes=[ssB])
            act.op(lambda e: e.activation(out=a, in_=a, func=AF.Sqrt), reads=[ssB], writes=[ssB])
            dve.op(lambda e: e.reciprocal(out=a, in_=a), reads=[ssB], writes=[ssB])

        def alloc_bc(P):
            return [P.sb([128, D], F32, n) for n in ("g1", "sh", "gt")]

        def fill_bc(bc, l, which, v, shift_i, scale_i, gate_i, gate_mul):
            (g1, g1B), (sh, shB), (gt, gtB) = bc
            tmp, tmpB = gt, gtB
            sp.dma(tmp[:], mod_d[l, v:v + 1, scale_i * D:(scale_i + 1) * D].partition_broadcast(128), reads=[modB], writes=[tmpB])
            sp.dma(g1[:], norm_w[l, which:which + 1, :].partition_broadcast(128), writes=[g1B])
            sp.dma(sh[:], mod_d[l, v:v + 1, shift_i * D:(shift_i + 1) * D].partition_broadcast(128), reads=[modB], writes=[shB])
            dve.op(lambda e: e.scalar_tensor_tensor(out=g1[:], in0=tmp[:], scalar=1.0, in1=g1[:], op0=ALU.add, op1=ALU.mult),
                   reads=[tmpB, g1B], writes=[g1B])
            sp.dma(gt[:], mod_d[l, v:v + 1, gate_i * D:(gate_i + 1) * D].partition_broadcast(128), reads=[modB], writes=[gtB])
            if gate_mul != 1.0:
                act.op(lambda e: e.mul(out=gt[:], in_=gt[:], mul=gate_mul), reads=[gtB], writes=[gtB])
            return (g1, g1B), (sh, shB), (gt, gtB)

        def norm_mod(P, ht, hBuf, g1, sh, out_bf, outB, ss, ssB, junk, junkB):
            act.op(lambda e: e.activation(out=junk[:], in_=ht[:], func=AF.Square, accum_out=ss[:]), reads=[hBuf], writes=[junkB, ssB])
            rstd_inplace(ss, ssB, 1.0 / D)
            dve.op(lambda e: e.scalar_tensor_tensor(out=junk[:], in0=ht[:], scalar=ss[:, 0:1], in1=g1[0][:], op0=ALU.mult, op1=ALU.mult),
                   reads=[hBuf, ssB, g1[1]], writes=[junkB])
            dve.op(lambda e: e.tensor_tensor(out=out_bf[:], in0=junk[:], in1=sh[0][:], op=ALU.add), reads=[junkB, sh[1]], writes=[outB])

        def groups_for(include_ctx=True):
            gs = []
            if include_ctx:
                gs.append((1, [0, 1]))
            for i in range(8):
                gs.append((0, [2 + 4 * i + j for j in range(4)]))
            return gs

        def ffn_phase(l, which, include_ctx=True):
            shift_i, scale_i, gate_i = (0, 1, 2) if which == 0 else (6, 7, 8)
            nw_i = 0 if which == 0 else 2
            with ExitStack() as ph:
                P = Pool_(nc, ph)
                wup, wupB = P.sb([128, 8, 2 * DFF], BF16, "wup")
                wdn, wdnB = P.sb([128, 22, D], BF16, "wdn")
                wupBs = [Buf() for _ in range(8)]
                for k in range(8):
                    pool.dma(wup[:, k, :], ffn_up[l, which, k * 128:(k + 1) * 128, :], writes=[wupBs[k]])
                wdnBs = [Buf() for _ in range(2)]
                for j in range(2):
                    pool.dma(wdn[:, 11 * j:11 * (j + 1), :],
                             ffn_down[l, which, 11 * j * 128:11 * (j + 1) * 128, :].rearrange("(k p) n -> p k n", p=128), writes=[wdnBs[j]])
                xT, xTB = P.sb([128, 8, 512], BF16, "xT")
                actT, actTB = P.sb([128, 22, 512], BF16, "actT")
                hts = [P.sb([128, D], F32, f"ht{i}") for i in range(1)]
                xms = [P.sb([128, D], BF16, f"xm{i}") for i in range(1)]
                sgs = [P.sb([128, 512], BF16, f"sg{i}") for i in range(2)]
                junk, junkB = P.sb([128, D], F32, "junk")
                ss, ssB = P.sb([128, 1], F32, "ss")
                tps = [P.ps([128, 8, 128], BF16, f"tp{i}") for i in range(2)]
                gps = [P.ps([128, 512], F32, f"gp{i}") for i in range(4)]
                dps = [P.ps([128, 512], F32, f"dp{i}") for i in range(2)]
                cur_v = None
                it = 0
                bct = alloc_bc(P)
                for v, tiles in groups_for(include_ctx):
                    if v != cur_v:
                        bc = fill_bc(bct, l, nw_i, v, shift_i, scale_i, gate_i, 0.5)
                        cur_v = v
                    g1, sh, gt = bc
                    nt = len(tiles)
                    W = nt * 128
                    for ti, t in enumerate(tiles):
                        ht, htB = hts[0]
                        xm, xmB = xms[0]
                        tp, tpB = tps[it % 2]
                        it += 1
                        sp.dma(ht[:], h_d[t * 128:(t + 1) * 128, :], reads=[hB[t]], writes=[htB])
                        norm_mod(P, ht, htB, g1, sh, xm, xmB, ss, ssB, junk, junkB)
                        for k in range(8):
                            pe.op(lambda e, k=k: e.transpose(out=tp[:, k, :], in_=xm[:, k * 128:(k + 1) * 128], identity=ident[:]),
                                  reads=[xmB, identB], writes=[tpB])
                        act.op(lambda e: e.copy(out=xT[:, :, ti * 128:(ti + 1) * 128], in_=tp[:]), reads=[tpB], writes=[xTB])
                    for f in range(22):
                        gp, gpB = gps[2 * (f % 2)]
                        up, upB = gps[2 * (f % 2) + 1]
                        sg, sgB = sgs[f % 2]
                        for k in range(8):
                            pe.op(lambda e, k=k: e.matmul(gp[:, 0:W], lhsT=wup[:, k, f * 128:(f + 1) * 128], rhs=xT[:, k, 0:W],
                                                          start=(k == 0), stop=(k == 7)), reads=[wupBs[k], xTB], writes=[gpB])
                        for k in range(8):
                            pe.op(lambda e, k=k: e.matmul(up[:, 0:W], lhsT=wup[:, k, DFF + f * 128:DFF + (f + 1) * 128], rhs=xT[:, k, 0:W],
                                                          start=(k == 0), stop=(k == 7)), reads=[wupBs[k], xTB], writes=[upB])
                        act.op(lambda e: e.activation(out=sg[:, 0:W], in_=gp[:, 0:W], func=AF.Silu), reads=[gpB], writes=[sgB])
                        dve.op(lambda e: e.tensor_tensor(out=actT[:, f, 0:W], in0=up[:, 0:W], in1=sg[:, 0:W], op=ALU.mult),
                               reads=[upB, sgB], writes=[actTB])
                    for ti, t in enumerate(tiles):
                        ht, htB = hts[0]
                        it += 1
                        sp.dma(ht[:], h_d[t * 128:(t + 1) * 128, :], reads=[hB[t]], writes=[htB])
                        for n in range(2):
                            dp, dpB = dps[n]
                            for f in range(22):
                                pe.op(lambda e, f=f: e.matmul(dp[:], lhsT=actT[:, f, ti * 128:(ti + 1) * 128], rhs=wdn[:, f, n * 512:(n + 1) * 512],
                                                              start=(f == 0), stop=(f == 21)), reads=[actTB, wdnBs[f // 11]], writes=[dpB])
                            dve.op(lambda e: e.tensor_tensor(out=junk[:, n * 512:(n + 1) * 512], in0=dp[:], in1=gt[0][:, n * 512:(n + 1) * 512],
                                                             op=ALU.mult), reads=[dpB, gt[1]], writes=[junkB])
                        pool.op(lambda e: e.tensor_tensor(out=ht[:], in0=ht[:], in1=junk[:], op=ALU.add), reads=[htB, junkB], writes=[htB])
                        sp.dma(h_d[t * 128:(t + 1) * 128, :], ht[:], reads=[htB], writes=[hB[t]])
            c.barrier()

        NEG = -30000.0
        ones_f, ones_fB = cst.sb([128, 128], F32, "ones_f")
        pool.op(lambda e: e.memset(ones_f[:], 1.0), writes=[ones_fB])
        def tri_const(name, pattern, cm, cmp, base_val, fill):
            t, tB = cst.sb([128, 128], F32, name)
            pool.op(lambda e: e.memset(t[:], base_val), writes=[tB])
            pool.op(lambda e: e.affine_select(out=t[:], in_=t[:], pattern=pattern, compare_op=cmp, fill=fill, base=0,
                                              channel_multiplier=cm), reads=[tB], writes=[tB])
            return t, tB
        TRI = [tri_const("trif", [[1, 128]], -1, ALU.is_ge, 1.0, 0.0), tri_const("trib", [[-1, 128]], 1, ALU.is_ge, 1.0, 0.0)]
        TRIC = [tri_const("tricf", [[-1, 128]], 1, ALU.is_gt, 1.0, 0.0), tri_const("tricb", [[1, 128]], -1, ALU.is_gt, 1.0, 0.0)]
        def rep_mask(name, pattern, cm, cmp):
            t, tB = cst.sb([128, 4, 128], F32, name)
            pool.op(lambda e: e.memset(t[:], 0.0), writes=[tB])
            pool.op(lambda e: e.affine_select(out=t[:], in_=t[:], pattern=[[0, 4]] + pattern, compare_op=cmp, fill=NEG, base=0,
                                              channel_multiplier=cm), reads=[tB], writes=[tB])
            return t, tB
        MASKT = [rep_mask("maskTf", [[1, 128]], -1, ALU.is_ge), rep_mask("maskTb", [[-1, 128]], 1, ALU.is_ge)]
        MASKL = [rep_mask("maskLf", [[-1, 128]], 1, ALU.is_gt), rep_mask("maskLb", [[1, 128]], -1, ALU.is_gt)]
        ind, indB = cst.sb([8, 8, 128], F32, "ind")
        pool.op(lambda e: e.memset(ind[:], 1.0), writes=[indB])
        pool.op(lambda e: e.affine_select(out=ind[:], in_=ind[:], pattern=[[-1, 8], [0, 128]], compare_op=ALU.is_equal, fill=0.0, base=0,
                                          channel_multiplier=1), reads=[indB], writes=[indB])
        ones8, ones8B = cst.sb([8, 128], F32, "ones8")
        nones8, nones8B = cst.sb([8, 128], F32, "nones8")
        pool.op(lambda e: e.memset(ones8[:], 1.0), writes=[ones8B])
        pool.op(lambda e: e.memset(nones8[:], -1.0), writes=[nones8B])
        ident4, ident4B = cst.sb([128, 4, 128], BF16, "ident4")
        for hh in range(4):
            dve.op(lambda e, hh=hh: e.tensor_copy(out=ident4[:, hh, :], in_=identf[:]), reads=[identfB], writes=[ident4B])

        onesb, onesbB = cst.sb([128, 64], BF16, "onesb")
        pool.op(lambda e: e.memset(onesb[:], 1.0), writes=[onesbB])
        msk, mskB = cst.sb([128, 2, 7, 128], BF16, "msk")
        pool.op(lambda e: e.memset(msk[:], 0.0), writes=[mskB])
        for lev in range(7):
            b_ = 1 << lev
            for m in range(128 // (2 * b_)):
                r0 = m * 2 * b_
                q_ = sp if (m % 2 == 0) else act
                q_.dma(msk[r0 + b_:r0 + 2 * b_, 0, lev, r0:r0 + b_], onesb[0:b_, 0:b_], reads=[onesbB], writes=[mskB])
                q_.dma(msk[r0:r0 + b_, 1, lev, r0 + b_:r0 + 2 * b_], onesb[0:b_, 0:b_], reads=[onesbB], writes=[mskB])
        projB, ptokB, gsigB, convTB, ctokB, yfB = Buf("projT"), Buf("ptok"), Buf("gsig"), Buf("convT"), Buf("ctok"), Buf("yf")
        ysB = [Buf(f"ys{i}") for i in range(4)]
        s5accB = Buf("s5acc")
        LAT = hB[2:]

        def h_pieces(l, t):
            if l % 2 == 0 or t < 2:
                return [(0, 128, h_d[t * 128:(t + 1) * 128, :])], [hB[t]]
            i = t - 2
            out = []
            for wi in range(2):
                out.append((wi * 64, 64, AP(h_d.tensor, (NCTX + 2 * i + wi) * D, [[64 * D, 64], [1, D]])))
            return out, LAT

        def prelude(P, l, tiles, W, res, bc):
            g1, sh, _ = bc
            for ti, t in enumerate(tiles):
                ht, htB = res["hts"][res["it"] % 2]
                xm, xmB = res["xms"][res["it"] % 2]
                tp, tpB = res["tps"][res["it"] % 2]
                res["it"] += 1
                pcs, deps = h_pieces(l, t)
                for (p0, n, ap) in pcs:
                    sp.dma(ht[p0:p0 + n, :], ap, reads=deps, writes=[htB])
                (jk_, jkB_), (ss_, ssB_) = res["junks"][ti % 2], res["sss"][ti % 2]
                norm_mod(P, ht, htB, g1, sh, xm, xmB, ss_, ssB_, jk_, jkB_)
                for k in range(8):
                    pe.op(lambda e, k=k: e.transpose(out=tp[:, k, :], in_=xm[:, k * 128:(k + 1) * 128], identity=ident[:]),
                          reads=[xmB, identB], writes=[tpB])
                act.op(lambda e: e.copy(out=res["xT"][:, :, ti * 128:(ti + 1) * 128], in_=tp[:]), reads=[tpB], writes=[res["xTB"]])

        def prelude_res(P):
            res = {"it": 0}
            res["xT"], res["xTB"] = P.sb([128, 8, 512], BF16, "xT")
            res["hts"] = [P.sb([128, D], F32, f"ht{i}") for i in range(2)]
            res["xms"] = [P.sb([128, D], BF16, f"xm{i}") for i in range(2)]
            res["junks"] = [P.sb([128, D], F32, f"junk{i}") for i in range(2)]
            res["sss"] = [P.sb([128, 1], F32, f"ss{i}") for i in range(2)]
            res["tps"] = [P.ps([128, 8, 128], BF16, f"tp{i}") for i in range(2)]
            return res

        SEGF = [(512, 1536), (1552, 2064), (3616, 5152), (3104, 3616)]
        SEGT = [(0, 512), (1536, 1552), (2064, 2576), (2576, 3088), (3088, 3104), (5152, 5664), (5664, 5672), (5672, 5680)]
        NF, NT = 3584, 2096

        def proj_phase_a(l):
            with ExitStack() as ph:
                P = Pool_(nc, ph)
                wF, wFB = P.sb([128, 8, NF], BF16, "wF")
                wT, wTB = P.sb([128, 8, NT], BF16, "wT")
                off = 0
                for (a, b) in SEGF:
                    pool.dma(wF[:, :, off:off + b - a], w_in[l, :, a:b].rearrange("(k p) n -> p k n", p=128), writes=[wFB])
                    off += b - a
                off = 0
                for (a, b) in SEGT:
                    pool.dma(wT[:, :, off:off + b - a], w_in[l, :, a:b].rearrange("(k p) n -> p k n", p=128), writes=[wTB])
                    off += b - a
                res = prelude_res(P)
                bct = alloc_bc(P)
                fst = [P.sb([128, 512], F32, f"fst{i}") for i in range(2)]
                fstb = [P.sb([128, 512], BF16, f"fstb{i}") for i in range(4)]
                tst = [P.sb([128, NT], F32, f"tst{i}") for i in range(2)]
                fps = [P.ps([128, 512], F32, f"fps{i}") for i in range(3)]
                tpp = [P.ps([128, 512], F32, f"tpp{i}") for i in range(3)]
                cur_v = None
                n1 = n2 = n3 = 0
                for v, tiles in groups_for(True):
                    if v != cur_v:
                        bc = fill_bc(bct, l, 1, v, 3, 4, 5, 1.0)
                        cur_v = v
                    W = len(tiles) * 128
                    tok0 = tiles[0] * 128
                    prelude(P, l, tiles, W, res, bc)
                    xT, xTB = res["xT"], res["xTB"]
                    for cf in range(NF // 128):
                        fp, fpB = fps[n1 % 3]
                        st, stB = fstb[n1 % 4] if cf < 24 else fst[n1 % 2]
                        for k in range(8):
                            pe.op(lambda e, k=k: e.matmul(fp[:, 0:W], lhsT=wF[:, k, cf * 128:(cf + 1) * 128], rhs=xT[:, k, 0:W],
                                                          start=(k == 0), stop=(k == 7)), reads=[wFB, xTB], writes=[fpB])
                        eng = act if n1 % 2 == 0 else dve
                        if eng is act:
                            act.op(lambda e: e.copy(out=st[:, 0:W], in_=fp[:, 0:W]), reads=[fpB], writes=[stB])
                        else:
                            dve.op(lambda e: e.tensor_copy(out=st[:, 0:W], in_=fp[:, 0:W]), reads=[fpB], writes=[stB])
                        n1 += 1
                        if cf < 24:
                            sp.dma(projT_d[cf * 128:(cf + 1) * 128, tok0:tok0 + W], st[:, 0:W], reads=[stB], writes=[projB])
                        else:
                            sp.dma(projU_d[(cf - 24) * 128:(cf - 23) * 128, tok0:tok0 + W], st[:, 0:W], reads=[stB], writes=[projB])
                    for ti, t in enumerate(tiles):
                        ts_, tsB = tst[n2 % 2]
                        n2 += 1
                        for c0 in range(0, NT, 512):
                            c1 = min(NT, c0 + 512)
                            tp_, tpB_ = tpp[n3 % 3]
                            for k in range(8):
                                pe.op(lambda e, k=k: e.matmul(tp_[:, 0:c1 - c0], lhsT=xT[:, k, ti * 128:(ti + 1) * 128], rhs=wT[:, k, c0:c1],
                                                              start=(k == 0), stop=(k == 7)), reads=[wTB, xTB], writes=[tpB_])
                            if n3 % 2 == 0:
                                act.op(lambda e: e.copy(out=ts_[:, c0:c1], in_=tp_[:, 0:c1 - c0]), reads=[tpB_], writes=[tsB])
                            else:
                                dve.op(lambda e: e.tensor_copy(out=ts_[:, c0:c1], in_=tp_[:, 0:c1 - c0]), reads=[tpB_], writes=[tsB])
                            n3 += 1
                        sp.dma(ptok_d[t * 128:(t + 1) * 128, :], ts_[:], reads=[tsB], writes=[ptokB])
            c.barrier()

        def proj_phase_b(l):
            with ExitStack() as ph:
                P = Pool_(nc, ph)
                wG, wGB = P.sb([128, 8, 4096], BF16, "wG")
                for k in range(8):
                    pool.dma(wG[:, k, :], w_in[l, k * 128:(k + 1) * 128, 5680:9776], writes=[wGB])
                gb, gbB = P.sb([128, 4096], F32, "gb")
                sp.dma(gb[:], gate_b[l:l + 1, :].partition_broadcast(128), writes=[gbB])
                res = prelude_res(P)
                bct = alloc_bc(P)
                gst = [P.sb([128, 4096], BF16, f"gst{i}") for i in range(2)]
                gtm = [P.sb([128, 512], F32, f"gtm{i}") for i in range(2)]
                gps_ = [P.ps([128, 512], F32, f"gps{i}") for i in range(4)]
                cur_v = None
                n2 = n3 = 0
                for v, tiles in groups_for(True):
                    if v != cur_v:
                        bc = fill_bc(bct, l, 1, v, 3, 4, 5, 1.0)
                        cur_v = v
                    W = len(tiles) * 128
                    prelude(P, l, tiles, W, res, bc)
                    xT, xTB = res["xT"], res["xTB"]
                    for ti, t in enumerate(tiles):
                        gs_, gsB = gst[n2 % 2]
                        n2 += 1
                        for cb in range(8):
                            gp, gpB = gps_[n3 % 4]
                            tm, tmB = gtm[n3 % 2]
                            n3 += 1
                            for k in range(8):
                                pe.op(lambda e, k=k: e.matmul(gp[:], lhsT=xT[:, k, ti * 128:(ti + 1) * 128], rhs=wG[:, k, cb * 512:(cb + 1) * 512],
                                                              start=(k == 0), stop=(k == 7)), reads=[wGB, xTB], writes=[gpB])
                            dve.op(lambda e: e.tensor_tensor(out=tm[:], in0=gp[:], in1=gb[:, cb * 512:(cb + 1) * 512], op=ALU.add),
                                   reads=[gpB, gbB], writes=[tmB])
                            act.op(lambda e: e.activation(out=gs_[:, cb * 512:(cb + 1) * 512], in_=tm[:], func=AF.Sigmoid), reads=[tmB], writes=[gsB])
                        sp.dma(gsig_d[t * 128:(t + 1) * 128, :], gs_[:], reads=[gsB], writes=[gsigB])
            c.barrier()

        CTOK_COL = {0: 0, 1: 128, 2: 256, 3: 384, 4: 512, 5: 640, 10: 768, 11: 896, 16: 1024, 17: 1152, 18: 1280, 19: 1408,
                    20: 1536, 21: 1664, 22: 1792, 23: 1920}
        SEGS = [(0, NCTX), (NCTX, NTOK)]

        def conv_phase(l):
            with ExitStack() as ph:
                P = Pool_(nc, ph)
                xins = [P.sb([128, NTOK], BF16, f"xin{i}") for i in range(2)]
                accs = [P.sb([128, NTOK], F32, f"acc{i}") for i in range(2)]
                sbfs = [P.sb([128, NTOK], BF16, f"sbf{i}") for i in range(2)]
                sq, sqB = P.sb([128, NTOK], F32, "sq")
                rn, rnB = P.sb([128, 512], F32, "rn")
                cws = [P.sb([128, 6], F32, f"cw{i}") for i in range(2)]
                dgs = [P.sb([128, 5, 128], BF16, f"dg{i}") for i in range(2)]
                tst_ = [P.sb([128, 4, 128], F32, f"cts{i}") for i in range(2)]
                nps = [P.ps([128, 512], F32, f"nps{i}") for i in range(2)]
                tps_ = [P.ps([128, 4, 128], F32, f"ctp{i}") for i in range(2)]
                cps = [P.ps([128, 512], F32, f"cps{i}") for i in range(2)]
                nn = 0
                ncv = 0
                for ct in range(24):
                    xin, xinB = xins[ct % 2]
                    acc, accB = accs[ct % 2]
                    sbf, sbfB = sbfs[ct % 2]
                    cw, cwB = cws[ct % 2]
                    dg, dgB = dgs[ct % 2]
                    ve = dve
                    sp.dma(xin[:], projT_d[ct * 128:(ct + 1) * 128, :], reads=[projB], writes=[xinB])
                    if ct < 8:
                        wsrc, C, c0, bsrc = ssm_conv_w, 1024, ct * 128, ssm_conv_b
                    elif ct < 12:
                        wsrc, C, c0, bsrc = ml_conv_w, 512, (ct - 8) * 128, ml_conv_b
                    else:
                        wsrc, C, c0, bsrc = gdn_conv_w, 1536, (ct - 12) * 128, None
                    act.dma(cw[:, 0:5], AP(wsrc.tensor, l * 5 * C + c0, [[1, 128], [C, 5]]), writes=[cwB], allow_slow_non_contiguous=True)
                    if bsrc is not None:
                        act.dma(cw[:, 5:6], AP(bsrc.tensor, l * C + c0, [[1, 128], [1, 1]]), writes=[cwB])
                    else:
                        pool.op(lambda e: e.memset(cw[:, 5:6], 0.0), writes=[cwB])
                    for k in range(5):
                        pool.op(lambda e, k=k: e.tensor_scalar(out=dg[:, k, :], in0=ident[:], scalar1=cw[:, k:k + 1], scalar2=None, op0=ALU.mult),
                                reads=[identB, cwB], writes=[dgB])
                    for (s0, s1) in SEGS:
                        for b0 in range(s0, s1, 512):
                            b1 = min(s1, b0 + 512)
                            Wb = b1 - b0
                            cp, cpB = cps[ncv % 2]
                            ncv += 1
                            pe.op(lambda e: e.matmul(cp[:, 0:Wb], lhsT=dg[:, 2, :], rhs=xin[:, b0:b1], start=True, stop=False), reads=[dgB, xinB], writes=[cpB])
                            for ki, k in enumerate((0, 1, 3, 4)):
                                d = k - 2
                                a_ = max(b0 + d, s0)
                                b_ = min(b1 + d, s1)
                                pe.op(lambda e, k=k, a_=a_, b_=b_, d=d: e.matmul(cp[:, a_ - d - b0:b_ - d - b0], lhsT=dg[:, k, :], rhs=xin[:, a_:b_],
                                                                              start=False, stop=(ki == 3)), reads=[dgB, xinB], writes=[cpB])
                            act.op(lambda e: e.activation(out=acc[:, b0:b1], in_=cp[:, 0:Wb], func=AF.Silu, bias=cw[:, 5:6]), reads=[cpB, cwB], writes=[accB])
                    if ct in (8, 9):
                        act.op(lambda e: e.mul(out=acc[:], in_=acc[:], mul=0.125), reads=[accB], writes=[accB])
                    if 12 <= ct < 20:
                        qs = 128.0 ** -0.5 if ct < 16 else 1.0
                        act.op(lambda e: e.activation(out=sq[:], in_=acc[:], func=AF.Square), reads=[accB], writes=[sqB])
                        for g0 in range(0, NTOK, 512):
                            g1_ = min(NTOK, g0 + 512)
                            Wd = g1_ - g0
                            npt, npB = nps[nn % 2]
                            nn += 1
                            pe.op(lambda e: e.matmul(npt[:, 0:Wd], lhsT=ones_f[:], rhs=sq[:, g0:g1_], start=True, stop=True),
                                  reads=[ones_fB, sqB], writes=[npB])
                            dve.op(lambda e: e.tensor_scalar(out=rn[:, 0:Wd], in0=npt[:, 0:Wd], scalar1=EPS, scalar2=None, op0=ALU.add),
                                   reads=[npB], writes=[rnB])
                            act.op(lambda e: e.activation(out=rn[:, 0:Wd], in_=rn[:, 0:Wd], func=AF.Sqrt), reads=[rnB], writes=[rnB])
                            dve.op(lambda e: e.reciprocal(out=rn[:, 0:Wd], in_=rn[:, 0:Wd]), reads=[rnB], writes=[rnB])
                            dve.op(lambda e: e.scalar_tensor_tensor(out=acc[:, g0:g1_], in0=acc[:, g0:g1_], scalar=qs, in1=rn[:, 0:Wd],
                                                                    op0=ALU.mult, op1=ALU.mult), reads=[accB, rnB], writes=[accB])
                    ve.op(lambda e: e.tensor_copy(out=sbf[:], in_=acc[:]), reads=[accB], writes=[sbfB])
                    sp.dma(convT_d[ct * 128:(ct + 1) * 128, :], sbf[:], reads=[sbfB], writes=[convTB])
                    if ct in CTOK_COL:
                        col = CTOK_COL[ct]
                        for t0 in range(0, NTOK // 128, 4):
                            nt_ = min(4, NTOK // 128 - t0)
                            tp_, tpB_ = tps_[nn % 2]
                            ts_, tsB_ = tst_[nn % 2]
                            nn += 1
                            for j in range(nt_):
                                pe.op(lambda e, j=j: e.transpose(out=tp_[:, j, :], in_=acc[:, (t0 + j) * 128:(t0 + j + 1) * 128], identity=identf[:]),
                                      reads=[accB, identfB], writes=[tpB_])
                            act.op(lambda e: e.copy(out=ts_[:, 0:nt_, :], in_=tp_[:, 0:nt_, :]), reads=[tpB_], writes=[tsB_])
                            dst = AP(ctok_d.tensor, t0 * 128 * 2048 + col, [[2048, 128], [128 * 2048, nt_], [1, 128]])
                            sp.dma(dst, ts_[:, 0:nt_, :], reads=[tsB_], writes=[ctokB])
            c.barrier()
        def gla_branch(l, kind):
            if kind == "ssm":
                H, NK, PV, br = 8, 128, 64, 0
            elif kind == "ml":
                H, NK, PV, br = 4, 64, 129, 1
            else:
                H, NK, PV, br = 4, 128, 128, 3
            HP = H * PV
            with ExitStack() as ph:
                P = Pool_(nc, ph)
                par, parB = P.sb([128, 64], F32, "par")
                if kind == "ssm":
                    sp.dma(par[:, 0:16], ssm_dt_bias[l:l + 1, :].partition_broadcast(128), writes=[parB])
                    sp.dma(par[:, 16:32], ssm_a_log[l:l + 1, :].partition_broadcast(128), writes=[parB])
                    sp.dma(par[:, 32:40], ssm_d[l:l + 1, :].partition_broadcast(128), writes=[parB])
                    act.op(lambda e: e.activation(out=par[:, 16:32], in_=par[:, 16:32], func=AF.Exp), reads=[parB], writes=[parB])
                    act.op(lambda e: e.mul(out=par[:, 16:32], in_=par[:, 16:32], mul=-1.0), reads=[parB], writes=[parB])
                elif kind == "ml":
                    sp.dma(par[:, 0:16], ml_gate_b[l:l + 1, :].partition_broadcast(128), writes=[parB])
                else:
                    sp.dma(par[:, 0:8], gdn_dt_bias[l:l + 1, :].partition_broadcast(128), writes=[parB])
                    sp.dma(par[:, 16:24], gdn_a_log[l:l + 1, :].partition_broadcast(128), writes=[parB])
                    act.op(lambda e: e.activation(out=par[:, 16:24], in_=par[:, 16:24], func=AF.Exp), reads=[parB], writes=[parB])
                    act.op(lambda e: e.mul(out=par[:, 16:24], in_=par[:, 16:24], mul=-1.0), reads=[parB], writes=[parB])
                nrm, nrmB = P.sb([128, 512], F32, "nrm")
                if kind == "ssm":
                    sp.dma(nrm[:], ssm_norm[l:l + 1, :].partition_broadcast(128), writes=[nrmB])
                elif kind == "ml":
                    sp.dma(nrm[:], ml_norm[l:l + 1, :].partition_broadcast(128), writes=[nrmB])
                else:
                    for hh in range(4):
                        sp.dma(nrm[:, hh * 128:(hh + 1) * 128], gdn_norm[0:1, l * 128:(l + 1) * 128].partition_broadcast(128), writes=[nrmB])
                T = lambda shape, dt=F32, n=None: P.sb(shape, dt, n)
                qT, qTB = T([NK, H if kind != "ssm" else 2, 128], BF16, "qT")
                kT, kTB = T([NK, H if kind != "ssm" else 2, 128], BF16, "kT")
                ktok, ktokB = T([128, (2 if kind == "ssm" else H) * NK], BF16, "ktok")
                vtok, vtokB = T([128, H, PV if kind != "ml" else 128], F32, "vtok")
                graw, grawB = T([128, 16], F32, "graw")
                gwk, gwkB = T([128, 4, H], F32, "gwk")
                e3, e3B = T([128, 3, H], F32, "e3")
                gcs, gcsB = T([128, H], F32, "gcs")
                gcT, gcTB = T([H, 128], F32, "gcT")
                ngcT, ngcTB = T([H, 128], F32, "ngcT")
                gbd, gbdB = T([H, H, 128], F32, "gbd")
                decT, decTB = T([128, H, 128], F32, "decT")
                scT, scTB = T([128, H, 128], BF16, "scT")
                vs, vsB = T([128, H, PV], BF16, "vs")
                vw, vwB = T([128, H, PV], BF16, "vw")
                S, SB = T([NK, H, PV], F32, "S")
                Sb, SbB = T([NK, H, PV], BF16, "Sb")
                ytmp, ytmpB = T([128, H, PV], F32, "ytmp")
                y, yB_ = T([128, H, PV], F32, "y")
                yfl, yflB = T([128, 512], F32, "yfl")
                yo, yoB = T([128, 512], F32, "yo")
                zt, ztB = T([128, 512], F32, "zt")
                st4, st4B = T([128, 8], F32, "st4")
                junk, junkB = T([128, 512], F32, "gjunk")
                if kind == "gdn":
                    decL, decLB = T([128, H, 128], F32, "decL")
                    L0, L0B = T([128, H, 128], BF16, "L0")
                    U0, U0B = T([128, H, 128], BF16, "U0")
                    Tb, TbB = T([128, H, 128], BF16, "Tb")
                    Ttb, TtbB = T([128, H, 128], BF16, "Ttb")
                    Mb, MbB = T([128, H, 128], BF16, "Mb")
                    Mtb, MtbB = T([128, H, 128], BF16, "Mtb")
                    Ab, AbB = T([128, H, 128], BF16, "Ab")
                    Atb, AtbB = T([128, H, 128], BF16, "Atb")
                    rhsb, rhsbB = T([128, H, 128], BF16, "rhsb")
                sm, smB = P.ps([128, 512], F32, "sm")
                gd = [P.ps([128, 4, 128], F32, f"gd{i}") for i in range(2)]
                cb, cbB = P.ps([128, 4, 128], F32, "cb")
                yi = [P.ps([128, 512], F32, f"yi{i}") for i in range(2)]
                yx = [P.ps([128, 512], F32, f"yx{i}") for i in range(2)]
                nG = (H * 128) // 512
                hpb = 512 // PV if kind != "ml" else 2
                ybank = lambda arr, hh: arr[hh // hpb]
                ycol = lambda hh: (hh % hpb) * PV

                def chunk_rows(t):
                    return slice(t * 128, (t + 1) * 128)

                def mk_lset(i):
                    return (T([NK, H if kind != "ssm" else 2, 128], BF16, f"qT{i}"), T([NK, H if kind != "ssm" else 2, 128], BF16, f"kT{i}"),
                            T([128, (2 if kind == "ssm" else H) * NK], BF16, f"ktok{i}"), T([128, H, PV if kind != "ml" else 128], F32, f"vtok{i}"),
                            T([128, 16], F32, f"graw{i}"))
                lsets = [((qT, qTB), (kT, kTB), (ktok, ktokB), (vtok, vtokB), (graw, grawB)), mk_lset(1), mk_lset(2)]
                itc = [0]
                e3s = [(e3, e3B), T([128, 3, H], F32, "e3b")]
                scTs = [(scT, scTB), T([128, H, 128], BF16, "scTb")]
                gwks = [(gwk, gwkB), T([128, 4, H], F32, "gwkb")]
                vss = [(vs, vsB), T([128, H, PV], BF16, "vsb")]
                vws = [(vw, vwB), T([128, H, PV], BF16, "vwb")]
                if kind == "gdn":
                    Ttbs = [(Ttb, TtbB), T([128, H, 128], BF16, "Ttbb")]
                def issue_loads(t, d, tl):
                    (qT, qTB), (kT, kTB), (ktok, ktokB), (vtok, vtokB), (graw, grawB) = tl
                    rows = chunk_rows(t)
                    if kind == "ssm":
                        sp.dma(kT[:], convT_d[512:768, rows].rearrange("(g n) t -> n g t", g=2), reads=[convTB], writes=[kTB])
                        sp.dma(qT[:], convT_d[768:1024, rows].rearrange("(g n) t -> n g t", g=2), reads=[convTB], writes=[qTB])
                        pool.dma(ktok[:], ctok_d[rows, 512:768], reads=[ctokB], writes=[ktokB])
                        sp.dma(vtok[:].rearrange("p h v -> p (h v)"), ctok_d[rows, 0:512], reads=[ctokB], writes=[vtokB])
                        sp.dma(graw[:, 0:8], ptok_d[rows, 512 + 8 * d:520 + 8 * d], reads=[ptokB], writes=[grawB])
                    elif kind == "ml":
                        sp.dma(qT[:], convT_d[1024:1280, rows].rearrange("(h n) t -> n h t", h=4), reads=[convTB], writes=[qTB])
                        sp.dma(kT[:], convT_d[1280:1536, rows].rearrange("(h n) t -> n h t", h=4), reads=[convTB], writes=[kTB])
                        pool.dma(ktok[:], ctok_d[rows, 768:1024], reads=[ctokB], writes=[ktokB])
                        sp.dma(vtok[:].rearrange("p h v -> p (h v)"), ptok_d[rows, 528:1040], reads=[ptokB], writes=[vtokB])
                        sp.dma(graw[:, 0:8], ptok_d[rows, 1552 + 8 * d:1560 + 8 * d], reads=[ptokB], writes=[grawB])
                    else:
                        sp.dma(qT[:], convT_d[1536:2048, rows].rearrange("(h n) t -> n h t", h=4), reads=[convTB], writes=[qTB])
                        sp.dma(kT[:], convT_d[2048:2560, rows].rearrange("(h n) t -> n h t", h=4), reads=[convTB], writes=[kTB])
                        pool.dma(ktok[:], ctok_d[rows, 1024:1536], reads=[ctokB], writes=[ktokB])
                        sp.dma(vtok[:].rearrange("p h v -> p (h v)"), ctok_d[rows, 1536:2048], reads=[ctokB], writes=[vtokB])
                        sp.dma(graw[:, 0:4], ptok_d[rows, 2080 + 4 * d:2084 + 4 * d], reads=[ptokB], writes=[grawB])
                        sp.dma(graw[:, 4:8], ptok_d[rows, 2088 + 4 * d:2092 + 4 * d], reads=[ptokB], writes=[grawB])

                def chunk_gen(d, t, order):
                    p_ = itc[0] % 2
                    e3, e3B = e3s[p_]
                    scT, scTB = scTs[p_]
                    gwk, gwkB = gwks[p_]
                    vs, vsB = vss[p_]
                    vw, vwB = vws[p_]
                    if kind == "gdn":
                        Ttb, TtbB = Ttbs[p_]
                    rows = chunk_rows(t)
                    idx_ = order.index(t)
                    if idx_ == 0:
                        issue_loads(t, d, lsets[itc[0] % 3])
                    (qT, qTB), (kT, kTB), (ktok, ktokB), (vtok, vtokB), (graw, grawB) = lsets[itc[0] % 3]
                    if idx_ + 1 < len(order):
                        issue_loads(order[idx_ + 1], d, lsets[(itc[0] + 1) % 3])
                    itc[0] += 1
                    if kind == "ssm":
                        dve.op(lambda e: e.tensor_tensor(out=gwk[:, 2, :], in0=graw[:, 0:8], in1=par[:, 8 * d:8 * d + 8], op=ALU.add),
                               reads=[grawB, parB], writes=[gwkB])
                        act.op(lambda e: e.activation(out=gwk[:, 2, :], in_=gwk[:, 2, :], func=AF.Exp), reads=[gwkB], writes=[gwkB])
                        act.op(lambda e: e.activation(out=gwk[:, 0, :], in_=gwk[:, 2, :], func=AF.Ln, bias=1.0), reads=[gwkB], writes=[gwkB])
                        dve.op(lambda e: e.tensor_tensor(out=gwk[:, 1, :], in0=gwk[:, 0, :], in1=par[:, 16 + 8 * d:24 + 8 * d], op=ALU.mult),
                               reads=[gwkB, parB], writes=[gwkB])
                    elif kind == "ml":
                        dve.op(lambda e: e.tensor_tensor(out=gwk[:, 2:4, :], in0=graw[:, 0:8].rearrange("p (a h) -> p a h", a=2),
                                                         in1=par[:, 8 * d:8 * d + 8].rearrange("p (a h) -> p a h", a=2), op=ALU.add),
                               reads=[grawB, parB], writes=[gwkB])
                        act.op(lambda e: e.activation(out=gwk[:, 0, :], in_=gwk[:, 2, :], func=AF.Exp), reads=[gwkB], writes=[gwkB])
                        act.op(lambda e: e.activation(out=gwk[:, 3, :], in_=gwk[:, 3, :], func=AF.Exp, scale=-1.0), reads=[gwkB], writes=[gwkB])
                        act.op(lambda e: e.activation(out=gwk[:, 3, :], in_=gwk[:, 3, :], func=AF.Ln, bias=1.0), reads=[gwkB], writes=[gwkB])
                        act.op(lambda e: e.mul(out=gwk[:, 1, :], in_=gwk[:, 3, :], mul=-1.0), reads=[gwkB], writes=[gwkB])
                    else:
                        dve.op(lambda e: e.tensor_tensor(out=gwk[:, 2, :], in0=graw[:, 0:4], in1=par[:, 4 * d:4 * d + 4], op=ALU.add),
                               reads=[grawB, parB], writes=[gwkB])
                        act.op(lambda e: e.activation(out=gwk[:, 2, :], in_=gwk[:, 2, :], func=AF.Exp), reads=[gwkB], writes=[gwkB])
                        act.op(lambda e: e.activation(out=gwk[:, 2, :], in_=gwk[:, 2, :], func=AF.Ln, bias=1.0), reads=[gwkB], writes=[gwkB])
                        dve.op(lambda e: e.tensor_tensor(out=gwk[:, 1, :], in0=gwk[:, 2, :], in1=par[:, 16 + 4 * d:20 + 4 * d], op=ALU.mult),
                               reads=[gwkB, parB], writes=[gwkB])
                        act.op(lambda e: e.activation(out=gwk[:, 3, :], in_=graw[:, 4:8], func=AF.Sigmoid), reads=[grawB], writes=[gwkB])
                    pe.op(lambda e: e.matmul(sm[:, 0:H], lhsT=TRI[d][0][:], rhs=gwk[:, 1, :], start=True, stop=True),
                          reads=[TRI[d][1], gwkB], writes=[smB])
                    pe.op(lambda e: e.matmul(sm[:, H:2 * H], lhsT=TRIC[d][0][:], rhs=gwk[:, 1, :], start=True, stop=True),
                          reads=[TRIC[d][1], gwkB], writes=[smB])
                    pe.op(lambda e: e.matmul(sm[:, 2 * H:3 * H], lhsT=ones_f[:], rhs=gwk[:, 1, :], start=True, stop=True),
                          reads=[ones_fB, gwkB], writes=[smB])
                    act.op(lambda e: e.activation(out=e3[:].rearrange("p a h -> p (a h)"), in_=sm[:, 0:3 * H], func=AF.Exp), reads=[smB], writes=[e3B])
                    dve.op(lambda e: e.tensor_copy(out=gcs[:], in_=sm[:, 0:H]), reads=[smB], writes=[gcsB])
                    pe.op(lambda e: e.matmul(sm[0:H, 128:256], lhsT=gwk[:, 1, :], rhs=TRI[d][0][:], start=True, stop=True), reads=[gwkB, TRI[d][1]], writes=[smB])
                    dve.op(lambda e: e.tensor_copy(out=gcT[:], in_=sm[0:H, 128:256]), reads=[smB], writes=[gcTB])
                    act.op(lambda e: e.mul(out=ngcT[:], in_=sm[0:H, 128:256], mul=-1.0), reads=[smB], writes=[ngcTB])
                    dve.op(lambda e: e.tensor_tensor(out=gbd[:], in0=gcT[:].unsqueeze(1).to_broadcast([H, H, 128]), in1=ind[0:H, 0:H, :], op=ALU.mult),
                           reads=[gcTB, indB], writes=[gbdB])
                    for g in range(nG):
                        hs = slice(4 * g, 4 * g + 4)
                        gp, gpB = gd[g]
                        pe.op(lambda e: e.matmul(gp[:], lhsT=ones8[0:H, :], rhs=gbd[:, hs, :], start=True, stop=False), reads=[ones8B, gbdB], writes=[gpB])
                        pe.op(lambda e: e.matmul(gp[:], lhsT=ngcT[:], rhs=ind[0:H, hs, :], start=False, stop=False), reads=[ngcTB, indB], writes=[gpB])
                        pe.op(lambda e: e.matmul(gp[:], lhsT=identf[:], rhs=MASKT[d][0][:, 0:4, :], start=False, stop=True),
                              reads=[identfB, MASKT[d][1]], writes=[gpB])
                        act.op(lambda e: e.activation(out=decT[:, hs, :], in_=gp[:], func=AF.Exp), reads=[gpB], writes=[decTB])
                    if kind == "ssm":
                        for g in range(2):
                            pe.op(lambda e, g=g: e.matmul(cb[:, g, :], lhsT=kT[:, g, :], rhs=qT[:, g, :], start=True, stop=True), reads=[kTB, qTB], writes=[cbB])
                        for g in range(2):
                            dve.op(lambda e, g=g: e.tensor_tensor(out=scT[:, 4 * g:4 * g + 4, :], in0=decT[:, 4 * g:4 * g + 4, :],
                                                                  in1=cb[:, g:g + 1, :].to_broadcast([128, 4, 128]), op=ALU.mult),
                                   reads=[decTB, cbB], writes=[scTB])
                    else:
                        for hh in range(H):
                            pe.op(lambda e, hh=hh: e.matmul(cb[:, hh, :], lhsT=kT[:, hh, :], rhs=qT[:, hh, :], start=True, stop=True), reads=[kTB, qTB], writes=[cbB])
                        dve.op(lambda e: e.tensor_tensor(out=scT[:], in0=decT[:], in1=cb[:], op=ALU.mult), reads=[decTB, cbB], writes=[scTB])
                    if kind == "ssm":
                        dve.op(lambda e: e.tensor_tensor(out=vs[:], in0=vtok[:], in1=gwk[:, 0, :].unsqueeze(2).to_broadcast([128, H, PV]), op=ALU.mult),
                               reads=[vtokB, gwkB], writes=[vsB])
                    elif kind == "ml":
                        dve.op(lambda e: e.tensor_tensor(out=vs[:, :, 0:128], in0=vtok[:], in1=gwk[:, 0, :].unsqueeze(2).to_broadcast([128, H, 128]), op=ALU.mult),
                               reads=[vtokB, gwkB], writes=[vsB])
                        dve.op(lambda e: e.tensor_copy(out=vs[:, :, 128:129], in_=gwk[:, 0, :].unsqueeze(2)), reads=[gwkB], writes=[vsB])
                    else:
                        for hh in range(H):
                            pe.op(lambda e, hh=hh: e.matmul(yi[1][0][:, hh * 128:(hh + 1) * 128], lhsT=kT[:, hh, :], rhs=kT[:, hh, :], start=True, stop=True),
                                  reads=[kTB], writes=[yi[1][1]])
                        gp, gpB = gd[1]
                        pe.op(lambda e: e.matmul(gp[:], lhsT=gcT[:], rhs=ind[0:H, 0:4, :], start=True, stop=False), reads=[gcTB, indB], writes=[gpB])
                        pe.op(lambda e: e.matmul(gp[:], lhsT=nones8[0:H, :], rhs=gbd[:], start=False, stop=False), reads=[nones8B, gbdB], writes=[gpB])
                        pe.op(lambda e: e.matmul(gp[:], lhsT=identf[:], rhs=MASKL[d][0][:, 0:4, :], start=False, stop=True),
                              reads=[identfB, MASKL[d][1]], writes=[gpB])
                        act.op(lambda e: e.activation(out=decL[:], in_=gp[:], func=AF.Exp), reads=[gpB], writes=[decLB])
                        dve.op(lambda e: e.tensor_tensor(out=decL[:], in0=decL[:], in1=yi[1][0][:].rearrange("p (h j) -> p h j", h=4), op=ALU.mult),
                               reads=[decLB, yi[1][1]], writes=[decLB])
                        dve.op(lambda e: e.tensor_tensor(out=L0[:], in0=decL[:], in1=gwk[:, 3, :].unsqueeze(2).to_broadcast([128, H, 128]), op=ALU.mult),
                               reads=[decLB, gwkB], writes=[L0B])
                        tpb, tpbB = yx[1]
                        tpv = tpb[:].bitcast(BF16).rearrange("p (h j) -> p h j", h=8)[:, 0:4, :]
                        for hh in range(H):
                            pe.op(lambda e, hh=hh: e.transpose(out=tpv[:, hh, :], in_=L0[:, hh, :], identity=ident[:]), reads=[L0B, identB], writes=[tpbB])
                        act.op(lambda e: e.copy(out=U0[:], in_=tpv), reads=[tpbB], writes=[U0B])
                        mlo, mup = (0, 1) if d == 0 else (1, 0)
                        mk = lambda a_, lev: msk[:, a_, lev, :].unsqueeze(1).to_broadcast([128, H, 128])
                        dve.op(lambda e: e.tensor_tensor(out=Mb[:], in0=L0[:], in1=mk(mlo, 0), op=ALU.mult), reads=[L0B, mskB], writes=[MbB])
                        dve.op(lambda e: e.tensor_tensor(out=Mtb[:], in0=U0[:], in1=mk(mup, 0), op=ALU.mult), reads=[U0B, mskB], writes=[MtbB])
                        dve.op(lambda e: e.tensor_tensor(out=Tb[:], in0=ident4[:], in1=Mb[:], op=ALU.subtract), reads=[ident4B, MbB], writes=[TbB])
                        dve.op(lambda e: e.tensor_tensor(out=Ttb[:], in0=ident4[:], in1=Mtb[:], op=ALU.subtract), reads=[ident4B, MtbB], writes=[TtbB])
                        pa, paB = yi[1]
                        pb_, pbB = yx[1]
                        pav = pa[:].rearrange("p (h j) -> p h j", h=4)
                        pbv = pb_[:].rearrange("p (h j) -> p h j", h=4)
                        for lev in range(1, 7):
                            last = lev == 6
                            dve.op(lambda e, lev=lev: e.tensor_tensor(out=Mb[:], in0=L0[:], in1=mk(mlo, lev), op=ALU.mult), reads=[L0B, mskB], writes=[MbB])
                            if not last:
                                dve.op(lambda e, lev=lev: e.tensor_tensor(out=Mtb[:], in0=U0[:], in1=mk(mup, lev), op=ALU.mult), reads=[U0B, mskB], writes=[MtbB])
                                for hh in range(H):
                                    pe.op(lambda e, hh=hh: e.matmul(pav[:, hh, :], lhsT=Mtb[:, hh, :], rhs=Tb[:, hh, :], start=True, stop=True), reads=[MtbB, TbB], writes=[paB])
                            for hh in range(H):
                                pe.op(lambda e, hh=hh: e.matmul(pbv[:, hh, :], lhsT=Mb[:, hh, :], rhs=Ttb[:, hh, :], start=True, stop=True), reads=[MbB, TtbB], writes=[pbB])
                            if not last:
                                act.op(lambda e: e.copy(out=Ab[:], in_=pav), reads=[paB], writes=[AbB])
                            dve.op(lambda e: e.tensor_copy(out=Atb[:], in_=pbv), reads=[pbB], writes=[AtbB])
                            if not last:
                                for hh in range(H):
                                    pe.op(lambda e, hh=hh: e.matmul(pav[:, hh, :], lhsT=Ttb[:, hh, :], rhs=Ab[:, hh, :], start=True, stop=True), reads=[TtbB, AbB], writes=[paB])
                            for hh in range(H):
                                pe.op(lambda e, hh=hh: e.matmul(pbv[:, hh, :], lhsT=Tb[:, hh, :], rhs=Atb[:, hh, :], start=True, stop=True), reads=[TbB, AtbB], writes=[pbB])
                            if not last:
                                dve.op(lambda e: e.tensor_tensor(out=Tb[:], in0=Tb[:], in1=pav, op=ALU.subtract), reads=[TbB, paB], writes=[TbB])
                            dve.op(lambda e: e.tensor_tensor(out=Ttb[:], in0=Ttb[:], in1=pbv, op=ALU.subtract), reads=[TtbB, pbB], writes=[TtbB])
                        Tt, TtB = Ttb, TtbB
                        yield
                        ks_, ksB = yi[1]
                        for hh in range(H):
                            pe.op(lambda e, hh=hh: e.matmul(ks_[:, hh * 128:(hh + 1) * 128], lhsT=kT[:, hh, :], rhs=Sb[:, hh, :], start=True, stop=True),
                                  reads=[kTB, SbB], writes=[ksB])
                        dve.op(lambda e: e.tensor_tensor(out=ytmp[:], in0=ks_[:].rearrange("p (h v) -> p h v", h=4),
                                                         in1=e3[:, 0, :].unsqueeze(2).to_broadcast([128, H, 128]), op=ALU.mult), reads=[ksB, e3B], writes=[ytmpB])
                        dve.op(lambda e: e.tensor_tensor(out=ytmp[:], in0=vtok[:], in1=ytmp[:], op=ALU.subtract), reads=[vtokB, ytmpB], writes=[ytmpB])
                        dve.op(lambda e: e.tensor_tensor(out=rhsb[:], in0=ytmp[:], in1=gwk[:, 3, :].unsqueeze(2).to_broadcast([128, H, 128]), op=ALU.mult),
                               reads=[ytmpB, gwkB], writes=[rhsbB])
                        for hh in range(H):
                            pe.op(lambda e, hh=hh: e.matmul(ks_[:, hh * 128:(hh + 1) * 128], lhsT=Tt[:, hh, :], rhs=rhsb[:, hh, :], start=True, stop=True),
                                  reads=[TtB, rhsbB], writes=[ksB])
                        act.op(lambda e: e.copy(out=vs[:], in_=ks_[:].rearrange("p (h v) -> p h v", h=4)), reads=[ksB], writes=[vsB])
                    dve.op(lambda e: e.tensor_tensor(out=vw[:], in0=vs[:], in1=e3[:, 1, :].unsqueeze(2).to_broadcast([128, H, PV]), op=ALU.mult),
                           reads=[vsB, e3B], writes=[vwB])
                    if kind != "gdn":
                        yield
                    for hh in range(H):
                        yb, ybB = ybank(yi, hh)
                        xb, xbB = ybank(yx, hh)
                        kq = (hh // 4) if kind == "ssm" else hh
                        pe.op(lambda e, hh=hh, yb=yb: e.matmul(yb[:, ycol(hh):ycol(hh) + PV], lhsT=scT[:, hh, :], rhs=vs[:, hh, :], start=True, stop=True),
                              reads=[scTB, vsB], writes=[ybB])
                        pe.op(lambda e, hh=hh, xb=xb, kq=kq: e.matmul(xb[:, ycol(hh):ycol(hh) + PV], lhsT=qT[:, kq, :], rhs=Sb[:, hh, :], start=True, stop=True),
                              reads=[qTB, SbB], writes=[xbB])
                    nb_ = (H + hpb - 1) // hpb
                    for b_ in range(nb_):
                        hs = slice(b_ * hpb, min(H, (b_ + 1) * hpb))
                        nh = hs.stop - hs.start
                        xv = yx[b_][0][:, 0:nh * PV].rearrange("p (h v) -> p h v", h=nh)
                        yv = yi[b_][0][:, 0:nh * PV].rearrange("p (h v) -> p h v", h=nh)
                        dve.op(lambda e: e.tensor_tensor(out=ytmp[:, hs, :], in0=xv, in1=e3[:, 0, hs].unsqueeze(2).to_broadcast([128, nh, PV]), op=ALU.mult),
                               reads=[yx[b_][1], e3B], writes=[ytmpB])
                        dve.op(lambda e: e.tensor_tensor(out=y[:, hs, :], in0=yv, in1=ytmp[:, hs, :], op=ALU.add), reads=[yi[b_][1], ytmpB], writes=[yB_])
                    nsb = (HP + 511) // 512
                    for hh in range(H):
                        sb_, sbB_ = gd[hh // hpb]
                        kq = (hh // 4) if kind == "ssm" else hh
                        pe.op(lambda e, hh=hh, sb_=sb_, kq=kq: e.matmul(sb_[0:NK].rearrange("p a b -> p (a b)")[:, ycol(hh):ycol(hh) + PV],
                                                                         lhsT=ktok[:, kq * NK:(kq + 1) * NK], rhs=vw[:, hh, :], start=True, stop=True),
                              reads=[ktokB, vwB], writes=[sbB_])
                    dve.op(lambda e: e.tensor_tensor(out=S[:], in0=S[:], in1=e3[0:NK, 2, :].unsqueeze(2).to_broadcast([NK, H, PV]), op=ALU.mult),
                           reads=[SB, e3B], writes=[SB])
                    for b_ in range(nb_):
                        hs = slice(b_ * hpb, min(H, (b_ + 1) * hpb))
                        nh = hs.stop - hs.start
                        sv = gd[b_][0][0:NK].rearrange("p a b -> p (a b)")[:, 0:nh * PV].rearrange("p (h v) -> p h v", h=nh)
                        dve.op(lambda e: e.tensor_tensor(out=S[:, hs, :], in0=S[:, hs, :], in1=sv, op=ALU.add), reads=[SB, gd[b_][1]], writes=[SB])
                    act.op(lambda e: e.copy(out=Sb[:], in_=S[:]), reads=[SB], writes=[SbB])
                    if kind == "ml":
                        act.op(lambda e: e.activation(out=st4[:, 0:4], in_=y[:, :, 128], func=AF.Abs), reads=[yB_], writes=[st4B])
                        dve.op(lambda e: e.tensor_scalar(out=st4[:, 0:4], in0=st4[:, 0:4], scalar1=1.0, scalar2=None, op0=ALU.max), reads=[st4B], writes=[st4B])
                        dve.op(lambda e: e.reciprocal(out=st4[:, 0:4], in_=st4[:, 0:4]), reads=[st4B], writes=[st4B])
                        dve.op(lambda e: e.tensor_tensor(out=yo[:].rearrange("p (h v) -> p h v", h=4), in0=y[:, :, 0:128],
                                                         in1=st4[:, 0:4].unsqueeze(2).to_broadcast([128, 4, 128]), op=ALU.mult), reads=[yB_, st4B], writes=[yoB])
                        ycur, ycurB = yo, yoB
                        yflat = yo[:]
                    else:
                        ycur, ycurB = y, yB_
                        yflat = y[:].rearrange("p h v -> p (h v)")
                    if d == 0:
                        sp.dma(yf_d[rows, :], yflat, reads=[ycurB], writes=[yfB])
                        return
                    sp.dma(yfl[:], yf_d[rows, :], reads=[yfB], writes=[yflB])
                    dve.op(lambda e: e.tensor_tensor(out=yfl[:], in0=yfl[:], in1=yflat, op=ALU.add), reads=[yflB, ycurB], writes=[yflB])
                    if kind == "ssm":
                        sp.dma(zt[:], ptok_d[rows, 0:512], reads=[ptokB], writes=[ztB])
                        dve.op(lambda e: e.tensor_tensor(out=ytmp[:], in0=vtok[:], in1=par[:, 32:40].unsqueeze(2).to_broadcast([128, 8, 64]), op=ALU.mult),
                               reads=[vtokB, parB], writes=[ytmpB])
                        dve.op(lambda e: e.tensor_tensor(out=yfl[:], in0=yfl[:], in1=ytmp[:].rearrange("p h v -> p (h v)"), op=ALU.add), reads=[yflB, ytmpB], writes=[yflB])
                        act.op(lambda e: e.activation(out=zt[:], in_=zt[:], func=AF.Silu), reads=[ztB], writes=[ztB])
                        dve.op(lambda e: e.tensor_tensor(out=yfl[:], in0=yfl[:], in1=zt[:], op=ALU.mult), reads=[yflB, ztB], writes=[yflB])
                        act.op(lambda e: e.activation(out=junk[:], in_=yfl[:], func=AF.Square, accum_out=st4[:, 4:5]), reads=[yflB], writes=[junkB, st4B])
                        rstd_inplace(st4, st4B, 1.0 / 512, st4[:, 4:5])
                        dve.op(lambda e: e.scalar_tensor_tensor(out=yo[:], in0=yfl[:], scalar=st4[:, 4:5], in1=nrm[:], op0=ALU.mult, op1=ALU.mult),
                               reads=[yflB, st4B, nrmB], writes=[yoB])
                    else:
                        zc = (1040, 1552) if kind == "ml" else (1568, 2080)
                        sp.dma(zt[:], ptok_d[rows, zc[0]:zc[1]], reads=[ptokB], writes=[ztB])
                        act.op(lambda e: e.activation(out=zt[:], in_=zt[:], func=AF.Sigmoid if kind == "ml" else AF.Silu), reads=[ztB], writes=[ztB])
                        dve.op(lambda e: e.tensor_tensor(out=junk[:], in0=yfl[:], in1=yfl[:], op=ALU.mult), reads=[yflB], writes=[junkB])
                        dve.op(lambda e: e.tensor_reduce(out=st4[:, 4:8], in_=junk[:].rearrange("p (h v) -> p h v", h=4), axis=AX.X, op=ALU.add),
                               reads=[junkB], writes=[st4B])
                        rstd_inplace(st4, st4B, 1.0 / 128, st4[:, 4:8])
                        dve.op(lambda e: e.tensor_tensor(out=junk[:].rearrange("p (h v) -> p h v", h=4), in0=yfl[:].rearrange("p (h v) -> p h v", h=4),
                                                         in1=st4[:, 4:8].unsqueeze(2).to_broadcast([128, 4, 128]), op=ALU.mult), reads=[yflB, st4B], writes=[junkB])
                        dve.op(lambda e: e.tensor_tensor(out=junk[:], in0=junk[:], in1=nrm[:], op=ALU.mult), reads=[junkB, nrmB], writes=[junkB])
                        dve.op(lambda e: e.tensor_tensor(out=yo[:], in0=junk[:], in1=zt[:], op=ALU.mult), reads=[junkB, ztB, yflB], writes=[yoB])
                    sp.dma(ys_d[br, rows, :], yo[:], reads=[yoB], writes=[ysB[br]])

                for d in range(2):
                    order = [0, 1] + list(range(2, 34)) if d == 0 else [1, 0] + list(range(33, 1, -1))
                    dve.op(lambda e: e.memset(S[:], 0.0), writes=[SB])
                    dve.op(lambda e: e.memset(Sb[:], 0.0), writes=[SbB])
                    prev = None
                    for t in order:
                        g_ = chunk_gen(d, t, order)
                        next(g_)
                        if prev is not None:
                            for _ in prev:
                                pass
                        prev = g_
                    for _ in prev:
                        pass
            c.barrier()
        def s5_phase(l):
            CH = 256
            chunks = [(i * CH, (i + 1) * CH) for i in range(NTOK // CH)]
            with ExitStack() as ph:
                P0 = Pool_(nc, ph)
                gT = [P0.sb([128, NTOK], BF16, f"gT{q}") for q in range(4)]
                with ExitStack() as ph2:
                    P = Pool_(nc, ph2)
                    T = lambda shape, dt=F32, n=None: P.sb(shape, dt, n)
                    uf, ufB = T([128, NTOK], F32, "uf")
                    uTq, uTqB = T([128, NTOK], BF16, "uTq")
                    ya, yaB = T([128, NTOK], F32, "yacc")
                    dsk, dskB = T([128, 4], F32, "dsk")
                    prs = [T([128, 24, 16], F32, "s5par")] * 2
                    bre, breB = T([128, 16, 16], F32, "bre")
                    bim, bimB = T([128, 16, 16], F32, "bim")
                    cre, creB = T([128, 16, 16], F32, "cre")
                    cim, cimB = T([128, 16, 16], F32, "cim")
                    bbr, bbrB = T([128, 16, 16], F32, "bbr")
                    bbi, bbiB = T([128, 16, 16], F32, "bbi")
                    tmpb, tmpbB = T([128, 16, 16], F32, "tmpb")
                    pad, padB = T([128, 16, 128], F32, "pad")
                    WB = [[T([128, 16, 128], BF16, f"WB{r}") for r in range(2)]] * 2
                    (WBr, WBrB), (WBi, WBiB) = WB[0]
                    WCr, WCrB = T([128, 16, 128], BF16, "WCr")
                    WCi, WCiB = T([128, 16, 128], BF16, "WCi")
                    E = [[T([128, 16, CH], F32, f"E{r}") for r in range(2)]] * 2
                    tab1, tab1B = T([128, 16, CH // 2], F32, "tab1")
                    tab2, tab2B = T([128, 16, CH // 2], F32, "tab2")
                    w1s = [[T([128, CH], F32, f"w1_{j}_{i}") for i in range(4)] for j in range(2)]
                    w1 = w1s[0]
                    grs = [T([128, CH], F32, f"gr{j}") for j in range(2)]
                    gis = [T([128, CH], F32, f"gi{j}") for j in range(2)]
                    hcIB = Buf("hcI")
                    hcCB = [Buf(f"hcC{j}") for j in range(4)]
                    hr = [T([128, CH], BF16, f"hr{i}") for i in range(2)]
                    hi = [T([128, CH], BF16, f"hi{i}") for i in range(2)]
                    hc, hcB = T([128, 5, 16], F32, "hc")
                    bps = [P.ps([128, 512], F32, f"bps{i}") for i in range(4)]
                    yps = [P.ps([128, 512], F32, f"yps{i}") for i in range(2)]
                    tpp_ = [P.ps([128, 4, 128], F32, f"s5tp{i}") for i in range(2)]
                    sp.dma(bre[:], AP(s5_b_re.tensor, l * 32768, [[16, 128], [2048, 16], [1, 16]]), writes=[breB])
                    sp.dma(bim[:], AP(s5_b_im.tensor, l * 32768, [[16, 128], [2048, 16], [1, 16]]), writes=[bimB])
                    for gg in range(2):
                        for s_ in range(16):
                            off = l * 32768 + (2 * s_ + gg) * 1024
                            sp.dma(cre[gg * 64:(gg + 1) * 64, s_, :], AP(s5_c_re.tensor, off, [[1, 64], [64, 16]]), writes=[creB], allow_slow_non_contiguous=True)
                            act.dma(cim[gg * 64:(gg + 1) * 64, s_, :], AP(s5_c_im.tensor, off, [[1, 64], [64, 16]]), writes=[cimB], allow_slow_non_contiguous=True)
                    for q in range(4):
                        sp.dma(dsk[:, q:q + 1], AP(s5_d.tensor, l * 512 + q * 128, [[1, 128], [1, 1]]), writes=[dskB])

                    def fill_pad(src, srcB, neg=False):
                        dve.op(lambda e: e.memset(pad[:], 0.0), writes=[padB])
                        for gg in range(2):
                            for r in range(4):
                                dst = pad[gg * 64:(gg + 1) * 64, r::4, 32 * r + 16 * gg:32 * r + 16 * gg + 16]
                                s_ = src[gg * 64:(gg + 1) * 64, r::4, :]
                                if neg:
                                    dve.op(lambda e, dst=dst, s_=s_: e.tensor_scalar(out=dst, in0=s_, scalar1=-1.0, scalar2=None, op0=ALU.mult), reads=[srcB], writes=[padB])
                                else:
                                    dve.op(lambda e, dst=dst, s_=s_: e.tensor_copy(out=dst, in_=s_), reads=[srcB], writes=[padB])

                    fill_pad(cre, creB)
                    dve.op(lambda e: e.tensor_copy(out=WCr[:], in_=pad[:]), reads=[padB], writes=[WCrB])
                    fill_pad(cim, cimB, neg=True)
                    dve.op(lambda e: e.tensor_copy(out=WCi[:], in_=pad[:]), reads=[padB], writes=[WCiB])

                    for d in range(2):
                        pr, prB = prs[d]
                        pv = lambda i, pr=pr: pr[:, i, :]
                        TS = lambda out, in0, s1, s2, op0, op1=None, prB=prB: dve.op(
                            (lambda e: e.tensor_scalar(out=out, in0=in0, scalar1=s1, scalar2=s2, op0=op0, op1=op1)) if op1 is not None else
                            (lambda e: e.tensor_scalar(out=out, in0=in0, scalar1=s1, scalar2=None, op0=op0)), reads=[prB], writes=[prB])
                        TT = lambda out, a_, b_, op, prB=prB: dve.op(lambda e: e.tensor_tensor(out=out, in0=a_, in1=b_, op=op), reads=[prB], writes=[prB])
                        sp.dma(pv(0), AP(s5_a_re.tensor, (l * 2 + d) * 2048, [[1, 128], [128, 16]]), writes=[prB], allow_slow_non_contiguous=True)
                        sp.dma(pv(1), AP(s5_a_im.tensor, (l * 2 + d) * 2048, [[1, 128], [128, 16]]), writes=[prB], allow_slow_non_contiguous=True)
                        for gg in range(2):
                            sp.dma(pr[gg * 64:(gg + 1) * 64, 2, :], AP(s5_log_dt.tensor, (l * 2 + d) * 32 + gg, [[0, 64], [2, 16]]), writes=[prB],
                                   allow_slow_non_contiguous=True)
                        TS(pv(0), pv(0), -1e-4, None, ALU.min)
                        act.op(lambda e: e.activation(out=pv(2), in_=pv(2), func=AF.Exp), reads=[prB], writes=[prB])
                        TT(pv(3), pv(0), pv(2), ALU.mult)
                        TT(pv(4), pv(1), pv(2), ALU.mult)
                        act.op(lambda e: e.activation(out=pv(5), in_=pv(3), func=AF.Exp), reads=[prB], writes=[prB])
                        TS(pv(8), pv(4), 1.0 / 64, None, ALU.mult)
                        TT(pv(9), pv(8), pv(8), ALU.mult)
                        TS(pv(10), pv(9), -1.0 / 42, 1.0, ALU.mult, ALU.add)
                        TT(pv(10), pv(10), pv(9), ALU.mult)
                        TS(pv(10), pv(10), -1.0 / 20, 1.0, ALU.mult, ALU.add)
                        TT(pv(10), pv(10), pv(9), ALU.mult)
                        TS(pv(10), pv(10), -1.0 / 6, 1.0, ALU.mult, ALU.add)
                        TT(pv(7), pv(10), pv(8), ALU.mult)
                        TS(pv(10), pv(9), -1.0 / 56, 1.0, ALU.mult, ALU.add)
                        TT(pv(10), pv(10), pv(9), ALU.mult)
                        TS(pv(10), pv(10), -1.0 / 30, 1.0, ALU.mult, ALU.add)
                        TT(pv(10), pv(10), pv(9), ALU.mult)
                        TS(pv(10), pv(10), -1.0 / 12, 1.0, ALU.mult, ALU.add)
                        TT(pv(10), pv(10), pv(9), ALU.mult)
                        TS(pv(6), pv(10), -0.5, 1.0, ALU.mult, ALU.add)

                        def csq(ci, si, co, so):
                            TT(pv(11), pv(ci), pv(ci), ALU.mult)
                            TT(pv(12), pv(si), pv(si), ALU.mult)
                            TT(pv(13), pv(ci), pv(si), ALU.mult)
                            TT(pv(co), pv(11), pv(12), ALU.subtract)
                            TS(pv(so), pv(13), 2.0, None, ALU.mult)
                        for _ in range(6):
                            csq(6, 7, 6, 7)
                        dve.op(lambda e: e.tensor_copy(out=pr[:, 18:20, :], in_=pr[:, 6:8, :]), reads=[prB], writes=[prB])
                        TT(pv(14), pv(5), pv(6), ALU.mult)
                        TT(pv(15), pv(5), pv(7), ALU.mult)
                        TS(pv(8), pv(14), -1.0, None, ALU.add)
                        TT(pv(9), pv(0), pv(0), ALU.mult)
                        TT(pv(10), pv(1), pv(1), ALU.mult)
                        TT(pv(9), pv(9), pv(10), ALU.add)
                        dve.op(lambda e: e.reciprocal(out=pv(9), in_=pv(9)), reads=[prB], writes=[prB])
                        TT(pv(10), pv(8), pv(0), ALU.mult)
                        TT(pv(11), pv(15), pv(1), ALU.mult)
                        TT(pv(10), pv(10), pv(11), ALU.add)
                        TT(pv(16), pv(10), pv(9), ALU.mult)
                        TT(pv(10), pv(15), pv(0), ALU.mult)
                        TT(pv(11), pv(8), pv(1), ALU.mult)
                        TT(pv(10), pv(10), pv(11), ALU.subtract)
                        TT(pv(17), pv(10), pv(9), ALU.mult)
                        fb = lambda i: pr[:, i, :].unsqueeze(2).to_broadcast([128, 16, 16])
                        dve.op(lambda e: e.tensor_tensor(out=bbr[:], in0=bre[:], in1=fb(16), op=ALU.mult), reads=[breB, prB], writes=[bbrB])
                        dve.op(lambda e: e.tensor_tensor(out=tmpb[:], in0=bim[:], in1=fb(17), op=ALU.mult), reads=[bimB, prB], writes=[tmpbB])
                        dve.op(lambda e: e.tensor_tensor(out=bbr[:], in0=bbr[:], in1=tmpb[:], op=ALU.subtract), reads=[bbrB, tmpbB], writes=[bbrB])
                        dve.op(lambda e: e.tensor_tensor(out=bbi[:], in0=bim[:], in1=fb(16), op=ALU.mult), reads=[bimB, prB], writes=[bbiB])
                        dve.op(lambda e: e.tensor_tensor(out=tmpb[:], in0=bre[:], in1=fb(17), op=ALU.mult), reads=[breB, prB], writes=[tmpbB])
                        dve.op(lambda e: e.tensor_tensor(out=bbi[:], in0=bbi[:], in1=tmpb[:], op=ALU.add), reads=[bbiB, tmpbB], writes=[bbiB])
                        for (src, srcB, (dstW, dstWB)) in ((bbr, bbrB, WB[d][0]), (bbi, bbiB, WB[d][1])):
                            fill_pad(src, srcB)
                            for s4 in range(4):
                                tp_, tpB_ = tpp_[s4 % 2]
                                for j in range(4):
                                    pe.op(lambda e, j=j: e.transpose(out=tp_[:, j, :], in_=pad[:, s4 * 4 + j, :], identity=identf[:]), reads=[padB, identfB], writes=[tpB_])
                                act.op(lambda e: e.copy(out=dstW[:, s4 * 4:s4 * 4 + 4, :], in_=tp_[:]), reads=[tpB_], writes=[dstWB])
                        (Er, ErB), (Ei, EiB) = E[d]
                        dve.op(lambda e: e.memset(Er[:, :, 0:1], 1.0), writes=[ErB])
                        dve.op(lambda e: e.memset(Ei[:, :, 0:1], 0.0), writes=[EiB])
                        k = 0
                        while (1 << k) < CH:
                            n = 1 << k
                            cb_ = pr[:, 6, :].unsqueeze(2).to_broadcast([128, 16, n])
                            sb_ = pr[:, 7, :].unsqueeze(2).to_broadcast([128, 16, n])
                            t1 = tab1[:, :, 0:n]
                            t2 = tab2[:, :, 0:n]
                            dve.op(lambda e: e.tensor_tensor(out=t1, in0=Er[:, :, 0:n], in1=cb_, op=ALU.mult), reads=[ErB, prB], writes=[tab1B])
                            dve.op(lambda e: e.tensor_tensor(out=t2, in0=Ei[:, :, 0:n], in1=sb_, op=ALU.mult), reads=[EiB, prB], writes=[tab2B])
                            dve.op(lambda e: e.tensor_tensor(out=Er[:, :, n:2 * n], in0=t1, in1=t2, op=ALU.subtract), reads=[tab1B, tab2B], writes=[ErB])
                            dve.op(lambda e: e.tensor_tensor(out=t1, in0=Er[:, :, 0:n], in1=sb_, op=ALU.mult), reads=[ErB, prB], writes=[tab1B])
                            dve.op(lambda e: e.tensor_tensor(out=t2, in0=Ei[:, :, 0:n], in1=cb_, op=ALU.mult), reads=[EiB, prB], writes=[tab2B])
                            dve.op(lambda e: e.tensor_tensor(out=Ei[:, :, n:2 * n], in0=t1, in1=t2, op=ALU.add), reads=[tab1B, tab2B], writes=[EiB])
                            k += 1
                            if (1 << k) < CH:
                                csq(6, 7, 6, 7)

                        for q in range(4):
                            sp.dma(uf[:], projU_d[q * 128:(q + 1) * 128, :], reads=[projB], writes=[ufB])
                            if d == 0:
                                dve.op(lambda e: e.tensor_copy(out=uTq[:], in_=uf[:]), reads=[ufB], writes=[uTqB])
                                dve.op(lambda e: e.tensor_scalar(out=ya[:], in0=uf[:], scalar1=dsk[:, q:q + 1], scalar2=None, op0=ALU.mult), reads=[ufB, dskB], writes=[yaB])
                            else:
                                for (s0, s1) in SEGS:
                                    rev = AP(uf[:].tensor, uf[:, s1 - 1:s1].offset, [list(uf[:].ap[0]), [-1, s1 - s0]])
                                    dve.op(lambda e, rev=rev: e.tensor_copy(out=uTq[:, s0:s1], in_=rev), reads=[ufB], writes=[uTqB])
                                sp.dma(ya[:], s5acc_d[q * 128:(q + 1) * 128, :], reads=[s5accB], writes=[yaB])
                            nacc = 0
                            if True:
                                usrc, usrcB = uTq, uTqB
                                dve.op(lambda e: e.memset(hc[:], 0.0), writes=[hcB, hcIB] + hcCB)
                                qs = slice(q * 4, q * 4 + 4)
                                for (c0, c1) in chunks:
                                    W = c1 - c0
                                    yp, ypB = yps[(c0 // CH) % 2]
                                    def s5_iter(s4, si):
                                        s = q * 4 + s4
                                        br_, brB_ = bps[2 * si]
                                        bi_, biB_ = bps[2 * si + 1]
                                        hr_, hrB_ = hr[si]
                                        hi_, hiB_ = hi[si]
                                        gr, grB = grs[si]
                                        gi, giB = gis[si]
                                        w1 = w1s[si]
                                        pe.op(lambda e: e.matmul(br_[:, 0:W], lhsT=WBr[:, s, :], rhs=usrc[:, c0:c1], start=True, stop=True), reads=[WBrB, usrcB], writes=[brB_])
                                        yield
                                        pe.op(lambda e: e.matmul(bi_[:, 0:W], lhsT=WBi[:, s, :], rhs=usrc[:, c0:c1], start=True, stop=True), reads=[WBiB, usrcB], writes=[biB_])
                                        yield
                                        er, ei = Er[:, s, 0:W], Ei[:, s, 0:W]
                                        a0, a1, a2, a3 = [w1[i][0][:, 0:W] for i in range(4)]
                                        aB = [w1[i][1] for i in range(4)]
                                        dve.op(lambda e: e.tensor_tensor(out=a0, in0=br_[:, 0:W], in1=er, op=ALU.mult), reads=[brB_, ErB], writes=[aB[0]])
                                        yield
                                        dve.op(lambda e: e.tensor_tensor(out=a1, in0=bi_[:, 0:W], in1=ei, op=ALU.mult), reads=[biB_, EiB], writes=[aB[1]])
                                        yield
                                        pool.op(lambda e: e.tensor_tensor(out=a0, in0=a0, in1=a1, op=ALU.add), reads=[aB[0], aB[1]], writes=[aB[0]])
                                        yield
                                        dve.op(lambda e: e.tensor_tensor(out=a2, in0=bi_[:, 0:W], in1=er, op=ALU.mult), reads=[biB_, ErB], writes=[aB[2]])
                                        yield
                                        dve.op(lambda e: e.tensor_tensor(out=a3, in0=br_[:, 0:W], in1=ei, op=ALU.mult), reads=[brB_, EiB], writes=[aB[3]])
                                        yield
                                        pool.op(lambda e: e.tensor_tensor(out=a2, in0=a2, in1=a3, op=ALU.subtract), reads=[aB[2], aB[3]], writes=[aB[2]])
                                        yield
                                        mg = pr[:, 5, s:s + 1].to_broadcast([128, W])
                                        dve.op(lambda e: e.tensor_tensor_scan(out=gr[:, 0:W], data0=mg, data1=a0, initial=hc[:, 2, s:s + 1], op0=ALU.mult, op1=ALU.add),
                                               reads=[prB, aB[0], hcIB], writes=[grB])
                                        dve.op(lambda e: e.tensor_tensor_scan(out=gi[:, 0:W], data0=mg, data1=a2, initial=hc[:, 3, s:s + 1], op0=ALU.mult, op1=ALU.add),
                                               reads=[prB, aB[2], hcIB], writes=[giB])
                                        dve.op(lambda e: e.tensor_tensor(out=a0, in0=gr[:, 0:W], in1=er, op=ALU.mult), reads=[grB, ErB], writes=[aB[0]])
                                        yield
                                        dve.op(lambda e: e.tensor_tensor(out=a1, in0=gi[:, 0:W], in1=ei, op=ALU.mult), reads=[giB, EiB], writes=[aB[1]])
                                        yield
                                        dve.op(lambda e: e.tensor_tensor(out=a0, in0=a0, in1=a1, op=ALU.subtract), reads=[aB[0], aB[1]], writes=[aB[0]])
                                        yield
                                        dve.op(lambda e: e.tensor_tensor(out=a2, in0=gr[:, 0:W], in1=ei, op=ALU.mult), reads=[grB, EiB], writes=[aB[2]])
                                        yield
                                        dve.op(lambda e: e.tensor_tensor(out=a3, in0=gi[:, 0:W], in1=er, op=ALU.mult), reads=[giB, ErB], writes=[aB[3]])
                                        yield
                                        pool.op(lambda e: e.tensor_tensor(out=a2, in0=a2, in1=a3, op=ALU.add), reads=[aB[2], aB[3]], writes=[aB[2]])
                                        yield
                                        act.op(lambda e: e.copy(out=hr_[:, 0:W], in_=a0), reads=[aB[0]], writes=[hrB_])
                                        yield
                                        act.op(lambda e: e.copy(out=hi_[:, 0:W], in_=a2), reads=[aB[2]], writes=[hiB_])
                                        yield
                                        act.op(lambda e: e.copy(out=hc[:, 0, s:s + 1], in_=w1[0][0][:, W - 1:W]), reads=[aB[0]], writes=[hcCB[s4]])
                                        yield
                                        act.op(lambda e: e.copy(out=hc[:, 1, s:s + 1], in_=w1[2][0][:, W - 1:W]), reads=[aB[2]], writes=[hcCB[s4]])
                                        yield
                                        pe.op(lambda e: e.matmul(yp[:, 0:W], lhsT=WCr[:, s, :], rhs=hr_[:, 0:W], start=(s4 == 0), stop=False), reads=[WCrB, hrB_], writes=[ypB])
                                        yield
                                        pe.op(lambda e: e.matmul(yp[:, 0:W], lhsT=WCi[:, s, :], rhs=hi_[:, 0:W], start=False, stop=(s4 == 3)), reads=[WCiB, hiB_], writes=[ypB])
                                        yield

                                    for pair in range(2):
                                        alive = [s5_iter(pair * 2, 0), s5_iter(pair * 2 + 1, 1)]
                                        while alive:
                                            for g_ in list(alive):
                                                try:
                                                    next(g_)
                                                except StopIteration:
                                                    alive.remove(g_)
                                    if d == 0:
                                        dve.op(lambda e: e.tensor_tensor(out=ya[:, c0:c1], in0=ya[:, c0:c1], in1=yp[:, 0:W], op=ALU.add), reads=[yaB, ypB], writes=[yaB])
                                    else:
                                        seg0, seg1 = (0, NCTX) if c0 < NCTX else (NCTX, NTOK)
                                        o1 = seg1 - 1 - (c0 - seg0)
                                        o0 = o1 - W + 1
                                        a1_ = w1[1][0]
                                        act.op(lambda e: e.copy(out=a1_[:, 0:W], in_=yp[:, 0:W]), reads=[ypB], writes=[w1[1][1]])
                                        rev = AP(a1_[:].tensor, a1_[:, W - 1:W].offset, [list(a1_[:].ap[0]), [-1, W]])
                                        dve.op(lambda e, rev=rev: e.tensor_tensor(out=ya[:, o0:o1 + 1], in0=ya[:, o0:o1 + 1], in1=rev, op=ALU.add),
                                               reads=[yaB, w1[1][1]], writes=[yaB])
                                    h0, h1, i0, i1, tt = hc[:, 0, qs], hc[:, 1, qs], hc[:, 2, qs], hc[:, 3, qs], hc[:, 4, qs]
                                    uc, us = pr[:, 18, qs], pr[:, 19, qs]
                                    dve.op(lambda e: e.tensor_tensor(out=i0, in0=h0, in1=uc, op=ALU.mult), reads=[hcB, prB] + hcCB, writes=[hcB, hcIB])
                                    dve.op(lambda e: e.tensor_tensor(out=tt, in0=h1, in1=us, op=ALU.mult), reads=[hcB, prB] + hcCB, writes=[hcB, hcIB])
                                    dve.op(lambda e: e.tensor_tensor(out=i0, in0=i0, in1=tt, op=ALU.subtract), reads=[hcB] + hcCB, writes=[hcB, hcIB])
                                    dve.op(lambda e: e.tensor_tensor(out=i1, in0=h0, in1=us, op=ALU.mult), reads=[hcB, prB] + hcCB, writes=[hcB, hcIB])
                                    dve.op(lambda e: e.tensor_tensor(out=tt, in0=h1, in1=uc, op=ALU.mult), reads=[hcB, prB] + hcCB, writes=[hcB, hcIB])
                                    dve.op(lambda e: e.tensor_tensor(out=i1, in0=i1, in1=tt, op=ALU.add), reads=[hcB] + hcCB, writes=[hcB, hcIB])
                            if d == 0:
                                sp.dma(s5acc_d[q * 128:(q + 1) * 128, :], ya[:], reads=[yaB], writes=[s5accB])
                                continue
                            dve.op(lambda e: e.tensor_tensor(out=uf[:], in0=ya[:], in1=ya[:], op=ALU.mult), reads=[yaB], writes=[ufB])
                            dve.op(lambda e: e.tensor_scalar(out=uf[:], in0=uf[:], scalar1=0.044715, scalar2=1.0, op0=ALU.mult, op1=ALU.add), reads=[ufB], writes=[ufB])
                            dve.op(lambda e: e.tensor_tensor(out=uf[:], in0=uf[:], in1=ya[:], op=ALU.mult), reads=[ufB, yaB], writes=[ufB])
                            act.op(lambda e: e.activation(out=uf[:], in_=uf[:], func=AF.Sigmoid, scale=1.5957691216057308), reads=[ufB], writes=[ufB])
                            dve.op(lambda e: e.tensor_tensor(out=gT[q][0][:], in0=ya[:], in1=uf[:], op=ALU.mult), reads=[ufB, yaB], writes=[gT[q][1]])

                c.barrier()
                with ExitStack() as ph3:
                    P = Pool_(nc, ph3)
                    gw, gwB = P.sb([128, 4, 512], BF16, "gluw")
                    pool.dma(gw[:], s5_glu_w[l].rearrange("(k p) n -> p k n", p=128), writes=[gwB])
                    gbt, gbtB = P.sb([128, 4], F32, "glub")
                    for q in range(4):
                        sp.dma(gbt[:, q:q + 1], AP(s5_glu_b.tensor, l * 512 + q * 128, [[1, 128], [1, 1]]), writes=[gbtB])
                    sg = [P.sb([128, 512], F32, f"sg{i}") for i in range(2)]
                    ob = [P.sb([128, 512], BF16, f"ob{i}") for i in range(2)]
                    yps = [P.ps([128, 512], F32, f"gyps{i}") for i in range(2)]
                    nn = 0
                    for qo in range(4):
                        for (c0, c1) in [(0, NCTX)] + [(NCTX + i * 512, NCTX + (i + 1) * 512) for i in range(8)]:
                            W = c1 - c0
                            yp, ypB = yps[nn % 2]
                            a0_, a0B = sg[nn % 2]
                            o_, oB = ob[nn % 2]
                            nn += 1
                            for q in range(4):
                                pe.op(lambda e, q=q: e.matmul(yp[:, 0:W], lhsT=gw[:, q, qo * 128:(qo + 1) * 128], rhs=gT[q][0][:, c0:c1], start=(q == 0), stop=(q == 3)),
                                      reads=[gwB, gT[q][1]], writes=[ypB])
                            act.op(lambda e: e.activation(out=a0_[:, 0:W], in_=yp[:, 0:W], func=AF.Sigmoid, bias=gbt[:, qo:qo + 1]), reads=[ypB, gbtB], writes=[a0B])
                            dve.op(lambda e: e.tensor_tensor(out=o_[:, 0:W], in0=a0_[:, 0:W], in1=gT[qo][0][:, c0:c1], op=ALU.mult), reads=[a0B, gT[qo][1]], writes=[oB])
                            sp.dma(s5T_d[qo * 128:(qo + 1) * 128, c0:c1], o_[:, 0:W], reads=[oB], writes=[ysB[2]])
            c.barrier()

        def merge_phase(l, include_ctx):
            with ExitStack() as ph:
                P = Pool_(nc, ph)
                wbr, wbrB = P.sb([128, 16, D], BF16, "wbr")
                pool.dma(wbr[:], w_branch[l].rearrange("b (k p) n -> p (b k) n", p=128), writes=[wbrB])
                wo, woB = P.sb([128, 8, D], BF16, "wo")
                pool.dma(wo[:], w_out[l].rearrange("(k p) n -> p k n", p=128), writes=[woB])
                bct = alloc_bc(P)
                ysf = [P.sb([128, 3, 512], F32, f"ysf{i}") for i in range(2)]
                ysb, ysbB = P.sb([128, 3, 512], BF16, "ysb")
                ysT, ysTB = P.sb([128, 16, 128], BF16, "ysT")
                gsg = [P.sb([128, 4096], BF16, f"gsg{i}") for i in range(2)]
                mg, mgB = P.sb([128, D], F32, "mg")
                mt, mtB = P.sb([128, 512], F32, "mt")
                mb, mbB = P.sb([128, D], BF16, "mb")
                mT, mTB = P.sb([128, 8, 128], BF16, "mT")
                hts = [P.sb([128, D], F32, f"mh{i}") for i in range(2)]
                tps = [P.ps([128, 8, 128], BF16, f"mtp{i}") for i in range(2)]
                mps = [P.ps([128, 512], F32, f"mps{i}") for i in range(4)]
                cur_v = None
                nn = 0
                for v, tiles in groups_for(include_ctx):
                    if v != cur_v:
                        bc = fill_bc(bct, l, 1, v, 3, 4, 5, 1.0)
                        cur_v = v
                    gt = bc[2]
                    for t in tiles:
                        rows = slice(t * 128, (t + 1) * 128)
                        yf_, yfB_ = ysf[nn % 2]
                        gs_, gsB_ = gsg[nn % 2]
                        ht, htB = hts[nn % 2]
                        nn += 1
                        for bi_, br in enumerate((0, 1, 3)):
                            sp.dma(yf_[:, bi_, :], ys_d[br, rows, :], reads=[ysB[br]], writes=[yfB_])
                        sp.dma(gs_[:], gsig_d[rows, :], reads=[gsigB], writes=[gsB_])
                        pool.op(lambda e: e.tensor_copy(out=ysb[:], in_=yf_[:]), reads=[yfB_], writes=[ysbB])
                        for bi_, br in enumerate((0, 1, 3)):
                            tp, tpB = tps[bi_ % 2]
                            for kc in range(4):
                                pe.op(lambda e, kc=kc: e.transpose(out=tp[:, kc, :], in_=ysb[:, bi_, kc * 128:(kc + 1) * 128], identity=ident[:]),
                                      reads=[ysbB, identB], writes=[tpB])
                            act.op(lambda e: e.copy(out=ysT[:, br * 4:br * 4 + 4, :], in_=tp[:, 0:4, :]), reads=[tpB], writes=[ysTB])
                        for kc in range(4):
                            sp.dma(ysT[:, 8 + kc, :], s5T_d[kc * 128:(kc + 1) * 128, rows], reads=[ysB[2]], writes=[ysTB])
                        for br in range(4):
                            for n in range(2):
                                mp, mpB = mps[(br * 2 + n) % 4]
                                for kc in range(4):
                                    pe.op(lambda e, kc=kc: e.matmul(mp[:], lhsT=ysT[:, br * 4 + kc, :], rhs=wbr[:, br * 4 + kc, n * 512:(n + 1) * 512],
                                                                    start=(kc == 0), stop=(kc == 3)), reads=[ysTB, wbrB], writes=[mpB])
                                gsl = gs_[:, br * 1024 + n * 512:br * 1024 + (n + 1) * 512]
                                if br == 0:
                                    dve.op(lambda e: e.tensor_tensor(out=mg[:, n * 512:(n + 1) * 512], in0=mp[:], in1=gsl, op=ALU.mult), reads=[mpB, gsB_], writes=[mgB])
                                else:
                                    dve.op(lambda e: e.tensor_tensor(out=mt[:], in0=mp[:], in1=gsl, op=ALU.mult), reads=[mpB, gsB_], writes=[mtB])
                                    pool.op(lambda e: e.tensor_tensor(out=mg[:, n * 512:(n + 1) * 512], in0=mg[:, n * 512:(n + 1) * 512], in1=mt[:], op=ALU.add),
                                            reads=[mgB, mtB], writes=[mgB])
                        act.op(lambda e: e.copy(out=mb[:], in_=mg[:]), reads=[mgB], writes=[mbB])
                        tp, tpB = tps[0]
                        for k in range(8):
                            pe.op(lambda e, k=k: e.transpose(out=tp[:, k, :], in_=mb[:, k * 128:(k + 1) * 128], identity=ident[:]), reads=[mbB, identB], writes=[tpB])
                        act.op(lambda e: e.copy(out=mT[:], in_=tp[:]), reads=[tpB], writes=[mTB])
                        pcs, deps = h_pieces(l, t)
                        for (p0, n_, ap) in pcs:
                            sp.dma(ht[p0:p0 + n_, :], ap, reads=deps, writes=[htB])
                        for n in range(2):
                            mp, mpB = mps[n]
                            for k in range(8):
                                pe.op(lambda e, k=k: e.matmul(mp[:], lhsT=mT[:, k, :], rhs=wo[:, k, n * 512:(n + 1) * 512], start=(k == 0), stop=(k == 7)),
                                      reads=[mTB, woB], writes=[mpB])
                            dve.op(lambda e: e.tensor_tensor(out=mt[:], in0=mp[:], in1=gt[0][:, n * 512:(n + 1) * 512], op=ALU.mult), reads=[mpB, gt[1]], writes=[mtB])
                            pool.op(lambda e: e.tensor_tensor(out=ht[:, n * 512:(n + 1) * 512], in0=ht[:, n * 512:(n + 1) * 512], in1=mt[:], op=ALU.add),
                                    reads=[htB, mtB], writes=[htB])
                        for (p0, n_, ap) in pcs:
                            sp.dma(ap, ht[p0:p0 + n_, :], reads=[htB], writes=deps)
            c.barrier()

        def mixer_layer(l, need_ctx):
            proj_phase_a(l)
            proj_phase_b(l)
            conv_phase(l)
            for kind in cfg.get("kinds", ("ssm", "ml", "gdn")):
                gla_branch(l, kind)
            if cfg.get("s5", True):
                s5_phase(l)
            if cfg.get("merge", True):
                merge_phase(l, need_ctx)

        def final_phase():
            with ExitStack() as ph:
                P = Pool_(nc, ph)
                fw, fwB = P.sb([128, D], F32, "fw")
                sp.dma(fw[:], final_norm[0:1, :].partition_broadcast(128), writes=[fwB])
                hts = [P.sb([128, D], F32, f"fh{i}") for i in range(3)]
                junks = [P.sb([128, D], F32, f"fj{i}") for i in range(2)]
                sss = [P.sb([128, 1], F32, f"fs{i}") for i in range(2)]
                for i in range(SEQ // 128):
                    ht, htB = hts[i % 3]
                    junk, junkB = junks[i % 2]
                    ss, ssB = sss[i % 2]
                    t = 2 + i
                    sp.dma(ht[:], h_d[t * 128:(t + 1) * 128, :], reads=[hB[t]], writes=[htB])
                    act.op(lambda e: e.activation(out=junk[:], in_=ht[:], func=AF.Square, accum_out=ss[:]), reads=[htB], writes=[junkB, ssB])
                    rstd_inplace(ss, ssB, 1.0 / D)
                    dve.op(lambda e: e.scalar_tensor_tensor(out=junk[:], in0=ht[:], scalar=ss[:, 0:1], in1=fw[:], op0=ALU.mult, op1=ALU.mult),
                           reads=[htB, ssB, fwB], writes=[junkB])
                    sp.dma(out_d[i * 128:(i + 1) * 128, :], junk[:], reads=[junkB])
            c.barrier()

        for l in range(nlayers):
            need_ctx = l < DEPTH - 1
            if not cfg.get('skip_ffn'):
                ffn_phase(l, 0, True)
            if do_mix:
                mixer_layer(l, need_ctx)
            if not cfg.get('skip_ffn') and not cfg.get('skip_ffn2'):
                ffn_phase(l, 1, need_ctx)
        final_phase()
        if "ys" in dbg:
            for bi_, br in enumerate((0, 1, 3)):
                if ("ssm", "ml", "x", "gdn")[br] in cfg.get("kinds", ("ssm", "ml", "gdn")):
                    sp.dma(dbg["ys"][bi_ * NTOK:(bi_ + 1) * NTOK, :], ys_d[br, :, :], reads=[ysB[br]])
            c.barrier()
        if "s5dbg" in dbg and cfg.get("s5", True):
            pool.dma(dbg["s5dbg"][:, :], s5T_d[:, :], reads=[ysB[2]])
            c.barrier()
        if "ptokdbg" in dbg:
            sp.dma(dbg["ptokdbg"][:, :], ptok_d[:, :], reads=[ptokB])
            c.barrier()
        if "hdbg" in dbg and cfg.get("merge", True):
            sp.dma(dbg["hdbg"][:, :], h_d[:, :], reads=hB)
            c.barrier()
        if "mod" in dbg:
            sp.dma(dbg["mod"][:, :], mod_d[0, :, :], reads=[modB])
            c.barrier()
        print("ninstr", c.ninstr)
    return nc


_PARAM_NAMES = ["ada_w", "ada_b", "norm_w", "ffn_up", "ffn_down", "w_in", "ssm_conv_w", "ssm_conv_b", "ssm_d", "ssm_norm",
                "ml_conv_w", "ml_conv_b", "ml_norm", "s5_a_re", "s5_a_im", "s5_log_dt", "s5_b_re", "s5_b_im", "s5_c_re", "s5_c_im",
                "s5_d", "s5_glu_w", "s5_glu_b", "gdn_conv_w", "w_branch", "w_out"]
_RESHAPE = {"ssm_dt_bias": (DEPTH, 16), "ssm_a_log": (DEPTH, 16), "ml_gate_b": (DEPTH, 16), "gdn_dt_bias": (DEPTH, 8), "gdn_a_log": (DEPTH, 8),
            "gdn_norm": (1, DEPTH * 128), "gate_b": (DEPTH, 4096)}


def make_in_maps(inputs, ncores=4):
    f = lambda a: np.ascontiguousarray(np.asarray(a, dtype=np.float32))
    shared = {k: f(inputs[k]) for k in _PARAM_NAMES}
    shared["final_norm"] = f(inputs["final_norm"]).reshape(1, D)
    for k, shp in _RESHAPE.items():
        shared[k] = f(inputs[k]).reshape(shp)
    maps = []
    for b in range(ncores):
        m = dict(shared)
        m["x"] = f(inputs["x"][b])
        m["ctx"] = f(inputs["ctx"][b])
        m["cc"] = np.ascontiguousarray(np.stack([np.asarray(inputs["c"][b], np.float32), np.asarray(inputs["c_ctx"], np.float32)]))
        maps.append(m)
    return maps


def kernel(**inputs):
    nc = build()
    maps = make_in_maps(inputs, 4)
    res = run_bass_kernel_spmd(nc, maps, core_ids=list(range(4)))
    return np.stack([res.results[b]["out"] for b in range(4)]).astype(np.float32)
```
